# Optimizing a Trainium2 kernel written in Bass

```python
import math
import jax, jax.numpy as jnp
from jax import lax
import numpy as np

D_MODEL = 1024
BATCH = 4
SEQ = 4096
DEPTH = 2
DEC_BATCH = 128
DEC_SEQ = 4
PAST_LEN = 2048
PAGE_SIZE = 128

HEAD_DIM = 64
N_HEADS = D_MODEL // HEAD_DIM
N_MIXERS = 2
N_LAYERS_A = (DEPTH + 1) // 2
N_LAYERS_B = DEPTH // 2
DECAY_LORA = 64
AAA_LORA = 64
GATE_LORA = 128
GN_EPS = 64e-5
WINDOWS = (128, 512, 2048)
DILATIONS = (1, 4, 16)
N_GROUPS = 3
ROPE_THETA = 10000.0
D_FF = 2816
CONV_W = 3
NORM_EPS = 1e-6

kernel_name = "rwkv7_dilated_swa_convglu_step"

F32 = jnp.float32


def rmsnorm(x, g):
    x32 = x.astype(F32)
    y = x32 * lax.rsqrt(jnp.mean(x32 * x32, axis=-1, keepdims=True) + NORM_EPS)
    return (y * g).astype(x.dtype)


def rope(x, pos):
    half = HEAD_DIM // 2
    inv = jnp.exp(-math.log(ROPE_THETA) * jnp.arange(half, dtype=F32) * 2.0 / HEAD_DIM)
    ang = pos.astype(F32)[:, None] * inv[None, :]
    cos = jnp.cos(ang)[None, :, None, :]
    sin = jnp.sin(ang)[None, :, None, :]
    x1 = x[..., :half].astype(F32)
    x2 = x[..., half:].astype(F32)
    return jnp.concatenate([x1 * cos - x2 * sin, x2 * cos + x1 * sin], axis=-1)


def wkv7_scan(r, w, k, v, a, b, S0):
    def step(S, inp):
        r_t, w_t, k_t, v_t, a_t, b_t = inp
        sa = jnp.einsum('bhvk,bhk->bhv', S, a_t)
        S = S * w_t[:, :, None, :] + sa[..., None] * b_t[:, :, None, :] + v_t[..., None] * k_t[:, :, None, :]
        y = jnp.einsum('bhvk,bhk->bhv', S, r_t)
        return S, y
    seq = tuple(jnp.moveaxis(t, 1, 0) for t in (r, w, k, v, a, b))
    S_T, ys = lax.scan(step, S0, seq)
    return jnp.moveaxis(ys, 0, 1), S_T


def rwkv7_time_mix(xn, x_last, S0, mu, w_rkv, w0, w1, w2, a0, a1, a2, g1, g2, k_k, k_a, r_k, ln_w, ln_b, w_o):
    B, T, D = xn.shape
    x_prev = jnp.concatenate([x_last[:, None, :].astype(xn.dtype), xn[:, :-1]], axis=1)
    xx = x_prev - xn
    xs = xn[:, :, None, :] + xx[:, :, None, :] * mu
    rkv = jnp.einsum('btnc,ncd->btnd', xs[:, :, :3], w_rkv)
    r, k, v = rkv[:, :, 0], rkv[:, :, 1], rkv[:, :, 2]
    xw, xa, xg = xs[:, :, 3], xs[:, :, 4], xs[:, :, 5]
    w_log = -jax.nn.softplus(-(w0 + jnp.tanh(xw @ w1) @ w2)) - 0.5
    decay = jnp.exp(-jnp.exp(w_log.astype(F32)))
    a = jax.nn.sigmoid(a0 + (xa @ a1) @ a2)
    g = jax.nn.sigmoid(xg @ g1) @ g2

    def heads(t):
        return t.reshape(B, T, N_HEADS, HEAD_DIM).astype(F32)

    kk = heads(k * k_k)
    kk = kk / jnp.maximum(jnp.sqrt(jnp.sum(kk * kk, axis=-1, keepdims=True)), 1e-12)
    k = k * (1.0 + (a - 1.0) * k_a)
    rh, kh, vh, ah = heads(r), heads(k), heads(v), heads(a)
    y, S_T = wkv7_scan(rh, heads(decay), kh, vh, -kk, kk * ah, S0.astype(F32))
    mean = jnp.mean(y, axis=-1, keepdims=True)
    var = jnp.mean(jnp.square(y - mean), axis=-1, keepdims=True)
    y = ((y - mean) * lax.rsqrt(var + GN_EPS)).reshape(B, T, D) * ln_w + ln_b
    bonus = jnp.sum(rh * kh * r_k, axis=-1, keepdims=True) * vh
    y = y + bonus.reshape(B, T, D)
    out = (y * g) @ w_o
    return out.astype(xn.dtype), xn[:, -1], S_T


def band_dilated_attention(q, k, v, dil, n_keys):
    B, T, H, Dh = q.shape
    L = T // dil
    blk = n_keys
    nb = -(-L // blk)
    Lp = nb * blk

    def cls(t):
        t = t.reshape(B, L, dil, H, Dh).transpose(0, 2, 1, 3, 4)
        t = jnp.pad(t, ((0, 0), (0, 0), (0, Lp - L), (0, 0), (0, 0)))
        return t.reshape(B, dil, nb, blk, H, Dh)

    qb, kb, vb = cls(q), cls(k), cls(v.astype(F32))

    def prev_block(t):
        return jnp.concatenate([jnp.zeros_like(t[:, :, :1]), t[:, :, :-1]], axis=2)

    kc = jnp.concatenate([prev_block(kb), kb], axis=3)
    vc = jnp.concatenate([prev_block(vb), vb], axis=3)
    s = jnp.einsum('bcnqhd,bcnkhd->bcnhqk', qb, kc).astype(F32) * (Dh ** -0.5)
    i = jnp.arange(blk)[:, None]
    j = jnp.arange(2 * blk)[None, :]
    dist = blk + i - j
    bidx = jnp.arange(nb)[:, None, None]
    valid = (dist >= 0) & (dist <= n_keys) & ((bidx > 0) | (j >= blk))
    s = jnp.where(valid[None, None, :, None, :, :], s, -jnp.inf)
    m = jnp.max(s, axis=-1, keepdims=True)
    p = jnp.exp(s - m)
    den = jnp.sum(p, axis=-1)
    o = jnp.einsum('bcnhqk,bcnkhd->bcnqhd', p, vc) / jnp.swapaxes(den, -1, -2)[..., None]
    lse = m[..., 0] + jnp.log(den)
    o = o.reshape(B, dil, Lp, H, Dh)[:, :, :L].transpose(0, 2, 1, 3, 4).reshape(B, T, H, Dh)
    lse = jnp.swapaxes(lse, -1, -2).reshape(B, dil, Lp, H)[:, :, :L].transpose(0, 2, 1, 3).reshape(B, T, H)
    return o, lse


def gathered_dilated_attention(q, k_all, v_all, dil, n_keys, n_past):
    S = q.shape[1]
    idx = n_past + jnp.arange(S)[:, None] - dil * jnp.arange(n_keys + 1)[None, :]
    valid = idx >= 0
    idx_c = jnp.maximum(idx, 0)
    kg = k_all[:, idx_c]
    vg = v_all[:, idx_c].astype(F32)
    s = jnp.einsum('bshd,bsmhd->bshm', q, kg).astype(F32) * (HEAD_DIM ** -0.5)
    s = jnp.where(valid[None, :, None, :], s, -jnp.inf)
    m = jnp.max(s, axis=-1, keepdims=True)
    p = jnp.exp(s - m)
    den = jnp.sum(p, axis=-1)
    o = jnp.einsum('bshm,bsmhd->bshd', p, vg) / den[..., None]
    return o, m[..., 0] + jnp.log(den)


def combine_by_denominators(outs, lses):
    wts = jax.nn.softmax(jnp.stack(lses, axis=0), axis=0)
    return jnp.sum(wts[..., None] * jnp.stack(outs, axis=0), axis=0)


def split_qkv(xn, w_in):
    B, T, _ = xn.shape
    return (xn @ w_in).reshape(B, T, N_GROUPS, 3, N_HEADS, HEAD_DIM)


def attn_prompt(xn, w_in, w_o):
    B, T, _ = xn.shape
    qkv = split_qkv(xn, w_in)
    pos = jnp.arange(T)
    outs, lses, ks, vs = [], [], [], []
    for g in range(N_GROUPS):
        q = rope(qkv[:, :, g, 0], pos)
        k = rope(qkv[:, :, g, 1], pos)
        v = qkv[:, :, g, 2]
        o, l = band_dilated_attention(q, k, v, DILATIONS[g], WINDOWS[g] // DILATIONS[g])
        outs.append(o)
        lses.append(l)
        keep = min(WINDOWS[g], T)
        ks.append(k[:, T - keep:])
        vs.append(v[:, T - keep:])
    o = combine_by_denominators(outs, lses)
    return (o.reshape(B, T, -1) @ w_o).astype(xn.dtype), ks, vs


def attn_sample(xn, k_caches, v_caches, w_in, w_o):
    B, S, _ = xn.shape
    qkv = split_qkv(xn, w_in)
    pos = PAST_LEN + jnp.arange(S)
    outs, lses, ks, vs = [], [], [], []
    for g in range(N_GROUPS):
        q = rope(qkv[:, :, g, 0], pos)
        k = rope(qkv[:, :, g, 1], pos)
        v = qkv[:, :, g, 2]
        n_past = k_caches[g].shape[1]
        k_all = jnp.concatenate([k_caches[g], k], axis=1)
        v_all = jnp.concatenate([v_caches[g], v], axis=1)
        o, l = gathered_dilated_attention(q, k_all, v_all, DILATIONS[g], WINDOWS[g] // DILATIONS[g], n_past)
        outs.append(o)
        lses.append(l)
        ks.append(k)
        vs.append(v)
    o = combine_by_denominators(outs, lses)
    return (o.reshape(B, S, -1) @ w_o).astype(xn.dtype), ks, vs


def conv_glu_ffn(xn, buf, w_up, conv_w, conv_b, w_down):
    T = xn.shape[1]
    u = xn @ w_up
    gate, val = u[..., :D_FF], u[..., D_FF:]
    ext = jnp.concatenate([buf.astype(gate.dtype), gate], axis=1)
    conv = conv_b + sum(ext[:, i:i + T] * conv_w[i] for i in range(CONV_W))
    out = (jax.nn.silu(conv) * val) @ w_down
    return out.astype(xn.dtype), ext[:, -(CONV_W - 1):]


def setup_inputs(seed: int = 0) -> dict:
    key = jax.random.key(seed)
    ks = iter(jax.random.split(key, 48))
    D, H, Dh = D_MODEL, N_HEADS, HEAD_DIM

    def nrm(shape, scale=1.0):
        return jax.random.normal(next(ks), shape, F32) * scale

    def gain(shape):
        return 1.0 + nrm(shape, 0.05)

    inp = {}
    inp["x_prompt"] = nrm((BATCH, SEQ, D))
    inp["x_sample"] = nrm((DEC_BATCH, DEC_SEQ, D))
    inp["state_rwkv_shift"] = nrm((N_LAYERS_A, DEC_BATCH, D))
    inp["state_rwkv_wkv"] = nrm((N_LAYERS_A, DEC_BATCH, H, Dh, Dh), 0.5)
    for w in WINDOWS:
        rows = min(w, PAST_LEN)
        inp["cache_k_w%d" % w] = nrm((N_LAYERS_B, DEC_BATCH, rows, H, Dh))
        inp["cache_v_w%d" % w] = nrm((N_LAYERS_B, DEC_BATCH, rows, H, Dh))
    inp["state_ffn_conv"] = nrm((DEPTH, DEC_BATCH, CONV_W - 1, D_FF))
    inp["norm_mix"] = gain((DEPTH, D))
    inp["norm_ffn"] = gain((DEPTH, D))
    inp["norm_final"] = gain((D,))
    inp["rwkv_mu"] = jax.random.uniform(next(ks), (N_LAYERS_A, 6, D), F32)
    inp["rwkv_w_rkv"] = nrm((N_LAYERS_A, 3, D, D), D ** -0.5)
    inp["rwkv_w0"] = jax.random.uniform(next(ks), (N_LAYERS_A, D), F32, -4.0, 0.0)
    inp["rwkv_w1"] = nrm((N_LAYERS_A, D, DECAY_LORA), D ** -0.5)
    inp["rwkv_w2"] = nrm((N_LAYERS_A, DECAY_LORA, D), 0.5 * DECAY_LORA ** -0.5)
    inp["rwkv_a0"] = nrm((N_LAYERS_A, D), 0.5)
    inp["rwkv_a1"] = nrm((N_LAYERS_A, D, AAA_LORA), D ** -0.5)
    inp["rwkv_a2"] = nrm((N_LAYERS_A, AAA_LORA, D), 0.5 * AAA_LORA ** -0.5)
    inp["rwkv_g1"] = nrm((N_LAYERS_A, D, GATE_LORA), D ** -0.5)
    inp["rwkv_g2"] = nrm((N_LAYERS_A, GATE_LORA, D), GATE_LORA ** -0.5)
    inp["rwkv_k_k"] = 0.85 + nrm((N_LAYERS_A, D), 0.05)
    inp["rwkv_k_a"] = 1.0 + nrm((N_LAYERS_A, D), 0.05)
    inp["rwkv_r_k"] = nrm((N_LAYERS_A, H, Dh), 0.5)
    inp["rwkv_ln_w"] = gain((N_LAYERS_A, D))
    inp["rwkv_ln_b"] = nrm((N_LAYERS_A, D), 0.02)
    inp["rwkv_w_o"] = nrm((N_LAYERS_A, D, D), D ** -0.5)
    inp["attn_w_in"] = nrm((N_LAYERS_B, D, N_GROUPS * 3 * H * Dh), D ** -0.5)
    inp["attn_w_o"] = nrm((N_LAYERS_B, H * Dh, D), (H * Dh) ** -0.5)
    inp["ffn_w_up"] = nrm((DEPTH, D, 2 * D_FF), D ** -0.5)
    inp["ffn_conv_w"] = nrm((DEPTH, CONV_W, D_FF), CONV_W ** -0.5)
    inp["ffn_conv_b"] = nrm((DEPTH, D_FF), 0.02)
    inp["ffn_w_down"] = nrm((DEPTH, D_FF, D), D_FF ** -0.5)
    return inp


def reference(x_prompt, x_sample, state_rwkv_shift, state_rwkv_wkv,
              cache_k_w128, cache_v_w128, cache_k_w512, cache_v_w512, cache_k_w2048, cache_v_w2048,
              state_ffn_conv, norm_mix, norm_ffn, norm_final,
              rwkv_mu, rwkv_w_rkv, rwkv_w0, rwkv_w1, rwkv_w2, rwkv_a0, rwkv_a1, rwkv_a2,
              rwkv_g1, rwkv_g2, rwkv_k_k, rwkv_k_a, rwkv_r_k, rwkv_ln_w, rwkv_ln_b, rwkv_w_o,
              attn_w_in, attn_w_o, ffn_w_up, ffn_conv_w, ffn_conv_b, ffn_w_down):
    Bp = x_prompt.shape[0]
    xp, xs = x_prompt, x_sample
    p_shift, p_wkv, s_shift, s_wkv = [], [], [], []
    p_k = [[] for _ in range(N_GROUPS)]
    p_v = [[] for _ in range(N_GROUPS)]
    s_k = [[] for _ in range(N_GROUPS)]
    s_v = [[] for _ in range(N_GROUPS)]
    p_conv, s_conv = [], []
    for i in range(DEPTH):
        hp = rmsnorm(xp, norm_mix[i])
        hs = rmsnorm(xs, norm_mix[i])
        li = i // N_MIXERS
        if i % N_MIXERS == 0:
            prm = (rwkv_mu[li], rwkv_w_rkv[li], rwkv_w0[li], rwkv_w1[li], rwkv_w2[li],
                   rwkv_a0[li], rwkv_a1[li], rwkv_a2[li], rwkv_g1[li], rwkv_g2[li],
                   rwkv_k_k[li], rwkv_k_a[li], rwkv_r_k[li], rwkv_ln_w[li], rwkv_ln_b[li], rwkv_w_o[li])
            zero_shift = jnp.zeros((Bp, D_MODEL), F32)
            zero_wkv = jnp.zeros((Bp, N_HEADS, HEAD_DIM, HEAD_DIM), F32)
            mp, shp, stp = rwkv7_time_mix(hp, zero_shift, zero_wkv, *prm)
            ms, shs, sts = rwkv7_time_mix(hs, state_rwkv_shift[li], state_rwkv_wkv[li], *prm)
            p_shift.append(shp)
            p_wkv.append(stp)
            s_shift.append(shs)
            s_wkv.append(sts)
        else:
            mp, kp, vp = attn_prompt(hp, attn_w_in[li], attn_w_o[li])
            ms, kn, vn = attn_sample(hs, (cache_k_w128[li], cache_k_w512[li], cache_k_w2048[li]),
                                     (cache_v_w128[li], cache_v_w512[li], cache_v_w2048[li]),
                                     attn_w_in[li], attn_w_o[li])
            for g in range(N_GROUPS):
                p_k[g].append(kp[g])
                p_v[g].append(vp[g])
                s_k[g].append(kn[g])
                s_v[g].append(vn[g])
        xp = xp + mp
        xs = xs + ms
        hp = rmsnorm(xp, norm_ffn[i])
        hs = rmsnorm(xs, norm_ffn[i])
        fp, bp = conv_glu_ffn(hp, jnp.zeros((Bp, CONV_W - 1, D_FF), F32),
                              ffn_w_up[i], ffn_conv_w[i], ffn_conv_b[i], ffn_w_down[i])
        fs, bs = conv_glu_ffn(hs, state_ffn_conv[i], ffn_w_up[i], ffn_conv_w[i], ffn_conv_b[i], ffn_w_down[i])
        p_conv.append(bp)
        s_conv.append(bs)
        xp = xp + fp
        xs = xs + fs
    y_prompt = rmsnorm(xp, norm_final)
    y_sample = rmsnorm(xs, norm_final)
    return (y_prompt, y_sample,
            jnp.stack(p_shift), jnp.stack(p_wkv),
            jnp.stack(p_k[0]), jnp.stack(p_v[0]), jnp.stack(p_k[1]), jnp.stack(p_v[1]),
            jnp.stack(p_k[2]), jnp.stack(p_v[2]), jnp.stack(p_conv),
            jnp.stack(s_shift), jnp.stack(s_wkv),
            jnp.stack(s_k[0]), jnp.stack(s_v[0]), jnp.stack(s_k[1]), jnp.stack(s_v[1]),
            jnp.stack(s_k[2]), jnp.stack(s_v[2]), jnp.stack(s_conv))
```

```python
import math
import os
import numpy as np
import ml_dtypes
from contextlib import ExitStack
import concourse.bass as bass
import concourse.mybir as mybir
from concourse.bass_utils import run_bass_kernel_spmd

F32 = mybir.dt.float32
BF16 = mybir.dt.bfloat16
ALU = mybir.AluOpType
AF = mybir.ActivationFunctionType
AX = mybir.AxisListType

D = 1024
H = 16
DH = 64
FF = 2816
NFC = FF // 128
NGRP = 3
WINDOWS = (128, 512, 2048)
DILS = (1, 4, 16)
PAST = 2048
NCORES = 8
HS = 68
EXPM05 = math.exp(-0.5)
DEBUG_SITES = False
SITES = {}


class Tok:
    __slots__ = ("w", "r")

    def __init__(self):
        self.w = None
        self.r = []


class Op:
    __slots__ = ("eng", "fn", "deps", "need_inc", "cnt", "is_dma", "dsem", "dval", "prev_dval", "site",
                 "sdeps", "cost", "idx", "succ", "indeg", "ready", "fin")


class V:
    def __init__(self, ap, toks=None):
        self.ap = ap
        self.toks = toks if toks is not None else [Tok()]

    def __getitem__(self, k):
        return V(self.ap[k], self.toks)

    def re(self, pat, **kw):
        return V(self.ap.rearrange(pat, **kw), self.toks)

    def bc(self, axis, shape):
        return V(self.ap.unsqueeze(axis).to_broadcast(list(shape)), self.toks)

    def bitcast(self, dt):
        return V(self.ap.bitcast(dt), self.toks)


def _ap(x):
    return x.ap if isinstance(x, V) else x


def _fsz(x):
    sh = _ap(x).shape
    n = 1
    for d_ in sh[1:]:
        n *= int(d_)
    return n


def _toks(*xs):
    r = []
    for x in xs:
        if isinstance(x, V):
            r.extend(x.toks)
    return r


class Prog:
    ENGS = ("pe", "act", "dve", "pool", "sp")
    NDSEM = 14

    def __init__(self, nc, stack):
        self.nc = nc
        self.ops = {e: [] for e in self.ENGS}
        self.esem = {e: stack.enter_context(nc.semaphore("es_" + e)) for e in ("pe", "act", "dve", "pool")}
        self.dsems = {e: [stack.enter_context(nc.semaphore("ds_%s%d" % (e, i))) for i in range(self.NDSEM)]
                      for e in ("sp", "act", "pool")}
        self.ndma = {e: 0 for e in ("sp", "act", "pool")}
        self.regions = []
        self.cur = []
        self.nins = 0

    def add(self, eng, fn, rd=(), wr=(), dma=False, extra=(), cost=100.0):
        o = Op()
        o.eng = eng
        o.fn = fn
        o.site = None
        o.need_inc = False
        o.cnt = 0
        o.is_dma = dma
        o.cost = cost
        o.idx = self.nins
        deps = list(extra)
        for t in rd:
            if t.w is not None:
                deps.append(t.w)
        for t in wr:
            if t.w is not None:
                deps.append(t.w)
            deps.extend(t.r)
        dd = []
        sd = []
        seen = set()
        for d in deps:
            if id(d) in seen or d is o:
                continue
            seen.add(id(d))
            sd.append(d)
            if (not d.is_dma) and d.eng == "pe" and eng == "pe" and not dma:
                continue
            dd.append(d)
        o.deps = dd
        o.sdeps = sd
        for t in rd:
            t.r.append(o)
        for t in wr:
            t.w = o
            t.r = []
        self.cur.append(o)
        self.nins += 1
        return o

    def barrier(self):
        if self.cur:
            self.regions.append(self.cur)
            self.cur = []

    def mm(self, out, lhsT, rhs, start=True, stop=True, extra_rd=()):
        o, l, r = _ap(out), _ap(lhsT), _ap(rhs)
        c_ = max(_fsz(rhs), 64) / 1.6 + 20.0
        if l.dtype == F32:
            c_ *= 4.0
        return self.add("pe", lambda e: e.matmul(o, lhsT=l, rhs=r, start=start, stop=stop),
                        rd=_toks(lhsT, rhs) + list(extra_rd), wr=_toks(out), cost=c_)

    def tr(self, out, in_, ident):
        o, i, d = _ap(out), _ap(in_), _ap(ident)
        c_ = 110.0 * (4.0 if i.dtype == F32 else 1.0)
        return self.add("pe", lambda e: e.transpose(out=o, in_=i, identity=d), rd=_toks(in_, ident), wr=_toks(out), cost=c_)

    def tt(self, eng, out, in0, in1, op):
        o, a, b = _ap(out), _ap(in0), _ap(in1)
        return self.add(eng, lambda e: e.tensor_tensor(out=o, in0=a, in1=b, op=op), rd=_toks(in0, in1), wr=_toks(out),
                        cost=self.ecost(eng, out))

    def ts(self, eng, out, in0, s1, op0, s2=None, op1=None):
        o, a = _ap(out), _ap(in0)
        s1a, s2a = _ap(s1), _ap(s2)
        if op1 is None:
            fn = lambda e: e.tensor_scalar(out=o, in0=a, scalar1=s1a, scalar2=None, op0=op0)
        else:
            fn = lambda e: e.tensor_scalar(out=o, in0=a, scalar1=s1a, scalar2=s2a, op0=op0, op1=op1)
        return self.add(eng, fn, rd=_toks(in0, s1, s2), wr=_toks(out), cost=self.ecost(eng, out))

    def stt(self, eng, out, in0, scalar, in1, op0, op1):
        o, a, s, b = _ap(out), _ap(in0), _ap(scalar), _ap(in1)
        return self.add(eng, lambda e: e.scalar_tensor_tensor(out=o, in0=a, scalar=s, in1=b, op0=op0, op1=op1),
                        rd=_toks(in0, scalar, in1), wr=_toks(out), cost=self.ecost(eng, out))

    def copy(self, eng, out, in_):
        o, i = _ap(out), _ap(in_)
        if eng == "act":
            fn = lambda e: e.copy(out=o, in_=i)
        else:
            fn = lambda e: e.tensor_copy(out=o, in_=i)
        return self.add(eng, fn, rd=_toks(in_), wr=_toks(out), cost=self.ecost(eng, out))

    def ecost(self, eng, out):
        n = _fsz(out)
        if eng == "act":
            return 230.0 + 0.85 * n
        if eng == "pool":
            return 120.0 + 2.0 * n
        return 70.0 + 1.05 * n

    def act(self, out, in_, func, scale=1.0, bias=None, accum=None):
        o, i, b, ac = _ap(out), _ap(in_), _ap(bias), _ap(accum)
        kw = {}
        if bias is not None:
            kw["bias"] = b
        if accum is not None:
            kw["accum_out"] = ac
        sc = _ap(scale)
        return self.add("act", lambda e: e.activation(out=o, in_=i, func=func, scale=sc, **kw),
                        rd=_toks(in_, bias, scale), wr=_toks(out, accum), cost=self.ecost("act", out))

    def red(self, eng, out, in_, op=None, axis=None):
        o, i = _ap(out), _ap(in_)
        op = op or ALU.add
        axis = axis or AX.X
        return self.add(eng, lambda e: e.tensor_reduce(out=o, in_=i, axis=axis, op=op), rd=_toks(in_), wr=_toks(out),
                        cost=self.ecost(eng, in_))

    def memset(self, eng, out, val):
        o = _ap(out)
        return self.add(eng, lambda e: e.memset(o, val), wr=_toks(out), cost=self.ecost(eng, out))

    def dma(self, q, out, in_, **kw):
        o, i = _ap(out), _ap(in_)
        nb_ = _fsz(out) * int(o.shape[0]) * (4 if o.dtype == F32 else 2)
        return self.add(q, lambda e: e.dma_start(out=o, in_=i, **kw), rd=_toks(in_), wr=_toks(out), dma=True,
                        cost=2200.0 + nb_ / 150.0)

    LAT = 300.0

    def schedule_region(self, ops):
        engs = self.ENGS
        pend = {e: [] for e in engs}
        inreg = set(id(o) for o in ops)
        for o in ops:
            o.succ = []
            o.indeg = 0
            o.ready = 0.0
        for o in ops:
            for d in o.sdeps:
                if id(d) in inreg:
                    d.succ.append(o)
                    o.indeg += 1
            pend[o.eng].append(o)
        free = {e: 0.0 for e in engs}
        out = {e: [] for e in engs}
        n = len(ops)
        W = int(os.environ.get("BASS_W", "40"))
        LAT = self.LAT
        while n > 0:
            bst = None
            bo = None
            be = None
            bi = 0
            for e in engs:
                lst = pend[e]
                if not lst:
                    continue
                fe = free[e]
                lim = W if len(lst) > W else len(lst)
                for i in range(lim):
                    o = lst[i]
                    if o.indeg == 0:
                        st = o.ready if o.ready > fe else fe
                        if bo is None or st < bst or (st == bst and o.idx < bo.idx):
                            bst, bo, be, bi = st, o, e, i
                        if st <= fe:
                            break
            o = bo
            del pend[be][bi]
            out[be].append(o)
            if o.is_dma:
                free[be] = bst + 120.0
                fin = bst + o.cost
            elif be == "pe":
                free[be] = bst + o.cost
                fin = bst + o.cost + 120.0
            else:
                free[be] = bst + o.cost
                fin = free[be]
            for s_ in o.succ:
                s_.indeg -= 1
                r = fin if (s_.eng == be and not o.is_dma) else fin + LAT
                if r > s_.ready:
                    s_.ready = r
            n -= 1
        if os.environ.get("BASS_SCHED_STATS"):
            busy = {e: sum(o.cost if not o.is_dma else 120.0 for o in out[e]) for e in engs}
            print("SCHED region: n=%d makespan=%.0f us busy(us)=%s" % (len(ops), max(free.values()) / 1e3, {e: round(v / 1e3) for e, v in busy.items()}))
        return out

    def finalize(self):
        if self.cur:
            self.regions.append(self.cur)
            self.cur = []
        sched = os.environ.get("BASS_NOSCHED", "") == ""
        for ops in self.regions:
            if sched:
                out = self.schedule_region(ops)
            else:
                out = {e: [o for o in ops if o.eng == e] for e in self.ENGS}
            last = {}
            dmas = []
            for e in self.ENGS:
                for o in out[e]:
                    if o.is_dma:
                        dmas.append(o)
                    else:
                        last[e] = o
                self.ops[e].extend(out[e])
            for e in self.ENGS:
                bo = Op()
                bo.eng = e
                bo.fn = None
                bo.site = None
                bo.need_inc = False
                bo.cnt = 0
                bo.is_dma = False
                bo.deps = [o for k, o in last.items() if k != e] + dmas
                self.ops[e].append(bo)
        for e in self.ndma:
            i = 0
            for o in self.ops[e]:
                if o.is_dma:
                    o.dsem = self.dsems[e][i % self.NDSEM]
                    o.dval = 16 * (i // self.NDSEM + 1)
                    o.prev_dval = o.dval - 16
                    i += 1
            self.ndma[e] = i
        for e in self.ENGS:
            for o in self.ops[e]:
                for d in o.deps:
                    if not d.is_dma:
                        d.need_inc = True

    def emit(self):
        nc = self.nc
        self.finalize()
        for e in ("pe", "act", "dve", "pool"):
            c = 0
            for o in self.ops[e]:
                if o.is_dma or o.fn is None:
                    o.cnt = c
                    continue
                if o.need_inc:
                    c += 1
                o.cnt = c
        prog = self

        def run(e, eng):
            known = {}
            for o in prog.ops[e]:
                waits = []
                for d in o.deps:
                    if d.is_dma:
                        waits.append((d.dsem, d.dval))
                    else:
                        waits.append((prog.esem[d.eng], d.cnt))
                if o.is_dma and o.prev_dval > 0:
                    waits.append((o.dsem, o.prev_dval))
                for (s, v) in waits:
                    k = id(s)
                    if known.get(k, 0) < v:
                        eng.wait_ge(s, v)
                        known[k] = v
                if o.fn is None:
                    continue
                ins = o.fn(eng)
                if DEBUG_SITES:
                    SITES[str(ins.ins.name)] = o.site
                if o.is_dma:
                    ins.then_inc(o.dsem, 16)
                elif o.need_inc:
                    ins.then_inc(prog.esem[e], 1)
            if e in prog.ndma:
                n = prog.ndma[e]
                for i in range(min(n, prog.NDSEM)):
                    uses = (n - 1 - i) // prog.NDSEM + 1
                    s = prog.dsems[e][i]
                    if known.get(id(s), 0) < 16 * uses:
                        eng.wait_ge(s, 16 * uses)

        with nc.Block() as block:
            @block.sync
            def _(eng):
                run("sp", eng)

            @block.tensor
            def _(eng):
                run("pe", eng)

            @block.scalar
            def _(eng):
                run("act", eng)

            @block.vector
            def _(eng):
                run("dve", eng)

            @block.gpsimd
            def _(eng):
                run("pool", eng)


class Arena:
    def __init__(self, big, nbytes):
        self.big = big
        self.nbytes = nbytes
        self.off = 0
        self.marks = []

    def mark(self):
        self.marks.append(self.off)

    def release(self):
        self.off = self.marks.pop()

    def alloc(self, cols, dt, parts=128):
        esz = 4 if dt == F32 else 2
        self.off = (self.off + 63) // 64 * 64
        nb = cols * esz
        assert self.off + nb <= self.nbytes, ("arena overflow", self.off, nb, self.nbytes)
        ap = self.big[0:parts, self.off // 2:(self.off + nb) // 2]
        if dt == F32:
            ap = ap.bitcast(F32)
        self.off += nb
        return V(ap)


CF = {}
CB = {}


def _layout(spec, table):
    off = 0
    for name, n in spec:
        table[name] = (off, n)
        off += n
    return off


_CF_SPEC = [("ident", 128), ("m_sl", 128), ("m_su", 128), ("m_ui", 128), ("m_blk", 128), ("maskc", 8),
            ("negc_p", 1), ("negc_s", 1), ("rowm_s", 1), ("one", 1), ("eps6", 1), ("eps24", 1), ("gneps", 1),
            ("rope_s", 128), ("ge_j", 4), ("le_j", 4), ("eq_j", 4)]
_CB_SPEC = [("ident", 128), ("am_ge", 128), ("am_le", 128)]
NCF = _layout(_CF_SPEC, CF)
NCB = _layout(_CB_SPEC, CB)


def host_consts(T):
    NT = T // 128
    p = np.arange(128)
    row = p[:, None]
    col = p[None, :]
    same = (row // 16) == (col // 16)
    cf = np.zeros((128, NCF), np.float32)

    def put(name, arr):
        o, n = CF[name]
        cf[:, o:o + n] = arr

    put("ident", np.eye(128))
    put("m_sl", same & (col < row))
    put("m_su", same & (row < col))
    put("m_ui", same & (row <= col))
    put("m_blk", same)
    put("maskc", (row // 16) == np.arange(8)[None, :])
    slot = p % 16
    rowm = ((slot >= 2) & (slot < 6)).astype(np.float32)[:, None]
    put("negc_p", -EXPM05)
    put("negc_s", -EXPM05 * rowm)
    put("rowm_s", rowm)
    put("one", 1.0)
    put("eps6", 1e-6)
    put("eps24", 1e-24)
    put("gneps", 64e-5)
    jj = np.arange(4)[None, :]
    put("ge_j", row >= jj)
    put("le_j", row <= jj)
    put("eq_j", row == jj)
    half = DH // 2
    inv = np.exp(-math.log(10000.0) * np.arange(half, dtype=np.float32) * 2.0 / DH).astype(np.float32)
    pos_s = (PAST + np.clip(slot - 2, 0, 3)).astype(np.float32)
    ang = pos_s[:, None] * inv[None, :]
    put("rope_s", np.concatenate([np.cos(ang), np.cos(ang), np.sin(ang), np.sin(ang)], 1))
    pos = (np.arange(NT)[None, :] * 128 + p[:, None]).astype(np.float32)
    angp = pos[:, :, None] * inv[None, None, :]
    rope_p = np.concatenate([np.cos(angp), np.cos(angp), np.sin(angp), np.sin(angp)], 2).astype(np.float32).reshape(128, NT * 128)
    cb = np.zeros((128, NCB), np.float32)

    def putb(name, arr):
        o, n = CB[name]
        cb[:, o:o + n] = arr

    putb("ident", np.eye(128))
    putb("am_ge", row >= col)
    putb("am_le", row <= col)
    return cf, cb.astype(ml_dtypes.bfloat16), rope_p


class Builder:
    def __init__(self, T, NS=16, phases="ABCD", debug=False):
        self.T = T
        self.NS = NS
        self.NT = T // 128
        self.NST = NS * 16 // 128
        self.NTT = self.NT + self.NST
        self.phases = phases
        self.debug = debug
        self.nc = bass.Bass("TRN2", target_bir_lowering=False)
        self.dr = {}

    def din(self, name, shape, dt=F32):
        self.dr[name] = V(self.nc.dram_tensor(name, list(shape), dt, kind="ExternalInput").ap())
        return self.dr[name]

    def dout(self, name, shape):
        self.dr[name] = V(self.nc.dram_tensor(name, list(shape), F32, kind="ExternalOutput").ap())
        return self.dr[name]

    def dscr(self, name, shape, dt=F32, dbg=False):
        if dbg and self.debug:
            return self.dout(name, shape)
        self.dr[name] = V(self.nc.dram_tensor(name, list(shape), dt).ap())
        return self.dr[name]

    def declare(self):
        T, NS, NTT = self.T, self.NS, self.NTT
        d = self.din
        d("xp", [T, D]); d("xs", [NS * 4, D]); d("st_shift", [NS, D]); d("st_wkv", [NS, H, DH, DH])
        for g, w in enumerate(WINDOWS):
            d("ck%d" % g, [NS, w, D]); d("cv%d" % g, [NS, w, D])
        d("st_conv", [2, NS, 2, FF])
        d("norm_mix", [2, D]); d("norm_ffn", [2, D]); d("norm_final", [1, D])
        d("rwkv_mu", [6, D]); d("rwkv_w_rkv", [3, D, D]); d("rwkv_w0", [1, D]); d("rwkv_w1", [D, 64]); d("rwkv_w2", [64, D])
        d("rwkv_a0", [1, D]); d("rwkv_a1", [D, 64]); d("rwkv_a2", [64, D]); d("rwkv_g1", [D, 128]); d("rwkv_g2", [128, D])
        d("rwkv_k_k", [1, D]); d("rwkv_k_a", [1, D]); d("rwkv_r_k", [1, D]); d("rwkv_ln_w", [1, D]); d("rwkv_ln_b", [1, D])
        d("rwkv_w_o", [D, D]); d("attn_w_in", [D, 9 * D]); d("attn_w_o", [D, D])
        d("ffn_w_up", [2, D, 2 * FF]); d("ffn_conv_w", [2, 3, FF]); d("ffn_conv_b", [2, FF]); d("ffn_w_down", [2, FF, D])
        d("cf", [128, NCF]); d("cb", [128, NCB], BF16); d("rope_p", [128, self.NT * 128])
        o = self.dout
        o("y_p", [T, D]); o("y_s", [NS * 4, D]); o("p_shift", [1, D]); o("p_wkv", [H, DH, DH])
        for g, w in enumerate(WINDOWS):
            o("p_k%d" % g, [min(w, T), D]); o("p_v%d" % g, [min(w, T), D])
        o("p_conv", [2, 2, FF]); o("s_shift", [NS, D]); o("s_wkv", [NS, H, DH, DH])
        for g in range(3):
            o("s_k%d" % g, [NS * 4, D]); o("s_v%d" % g, [NS * 4, D])
        o("s_conv", [2, NS, 2, FF])
        s = self.dscr
        s("xs_pad", [self.NST * 128, D])
        for nm, prod, cons in (("R1", "A", "B"), ("R2", "B", "C"), ("R3", "C", "D")):
            if prod in self.phases:
                s(nm, [NTT * 128, D], dbg=True)
            elif cons in self.phases:
                d(nm, [NTT * 128, D])
            else:
                s(nm, [NTT * 128, D])
        s("ys_pad", [self.NST * 128, D])
        if self.debug:
            o("dbg_y", [NTT * 128, D]); o("dbg_g", [NTT * 128, D]); o("dbg_yn", [NTT * 128, D])

    def res_src(self, name, i):
        if name == "R0":
            if i < self.NT:
                return self.dr["xp"][i * 128:(i + 1) * 128, :]
            return self.dr["xs_pad"][(i - self.NT) * 128:(i - self.NT + 1) * 128, :]
        return self.dr[name][i * 128:(i + 1) * 128, :]

    def build(self):
        nc = self.nc
        self.declare()
        with ExitStack() as st:
            self.P = P = Prog(nc, st)
            big = st.enter_context(nc.sbuf_tensor("arena", [128, 106000], BF16))
            self.A = A = Arena(big, 212000)
            pp = [st.enter_context(nc.psum_tensor("pp%d" % i, [128, 1024], F32)) for i in range(4)]
            self.bank = []
            for i in range(4):
                for h in range(2):
                    self.bank.append(V(pp[i][:, h * 512:(h + 1) * 512]))
            self.dbank = [V(pp[i][:, :], toks=self.bank[2 * i].toks + self.bank[2 * i + 1].toks) for i in range(4)]
            self.cf = A.alloc(NCF, F32)
            self.cb = A.alloc(NCB, BF16)
            P.dma("sp", self.cf, self.dr["cf"])
            P.dma("sp", self.cb, self.dr["cb"])
            self.zero = A.alloc(D, F32)
            P.memset("pool", self.zero, 0.0)
            for stl in range(self.NST):
                P.dma("sp", self.dr["xs_pad"][stl * 128:(stl + 1) * 128, :], self.zero)
            P.dma("sp", self.dr["xs_pad"].re("(s q) d -> s q d", q=16)[:, 2:6, :],
                  self.dr["xs"].re("(s j) d -> s j d", j=4))
            P.barrier()
            if "A" in self.phases:
                A.mark(); self.phase_rwkv(); A.release(); P.barrier()
            if "B" in self.phases:
                A.mark(); self.phase_ffn(0, "R1", "R2", final=False); A.release(); P.barrier()
            if "C" in self.phases:
                A.mark(); self.phase_attn(); A.release(); P.barrier()
            if "D" in self.phases:
                A.mark(); self.phase_ffn(1, "R3", None, final=True); A.release(); P.barrier()
            P.emit()
        return nc

    def c32(self, name):
        o, n = CF[name]
        return self.cf[:, o:o + n]

    def c16(self, name):
        o, n = CB[name]
        return self.cb[:, o:o + n]

    def load_bcast(self, dst, src_row):
        n = src_row.ap.shape[-1]
        self.P.dma("sp", dst, V(src_row.ap.to_broadcast([128, n]), src_row.toks))

    def rmsnorm(self, x, gbc, out, junk, small):
        P = self.P
        P.act(junk, x, AF.Square)
        P.red("dve", small[:, 0:1], junk)
        P.act(small[:, 1:2], small[:, 0:1], AF.Ln, scale=1.0 / D, bias=self.c32("eps6"))
        P.act(small[:, 2:3], small[:, 1:2], AF.Exp, scale=-0.5)
        P.stt("dve", out, x, small[:, 2:3], gbc, ALU.mult, ALU.mult)

    def transpose_bf(self, src, dstT, col0, bank):
        P = self.P
        pb = bank.bitcast(BF16)
        for kc in range(8):
            P.tr(pb[:, kc * 128:(kc + 1) * 128], src[:, kc * 128:(kc + 1) * 128], self.c16("ident"))
        P.copy("act", dstT[:, :, col0:col0 + 128], pb.re("p (k t) -> p k t", k=8))

    def phase_ffn(self, layer, rin, rout, final):
        P, A, NT, NST = self.P, self.A, self.NT, self.NST
        dr = self.dr
        wup = A.alloc(8 * 2 * FF, BF16)
        wdn = A.alloc(NFC * D, BF16)
        wup3 = wup.re("p (k n) -> p k n", k=8)
        wdn3 = wdn.re("p (k n) -> p k n", k=NFC)
        for k in range(8):
            P.dma("pool", wup3[:, k, :], dr["ffn_w_up"][layer, k * 128:(k + 1) * 128, :])
        for k0 in range(0, NFC, 6):
            k1 = min(NFC, k0 + 6)
            P.dma("pool", wdn3[:, k0:k1, :], dr["ffn_w_down"][layer, k0 * 128:k1 * 128, :].re("(k p) n -> p k n", p=128))
        import os
        STOP = os.environ.get("FFN_STOP", "")
        if STOP == "w":
            return
        gbc = A.alloc(D, F32)
        self.load_bcast(gbc, dr["norm_ffn"][layer:layer + 1, :])
        gfin = None
        if final:
            gfin = A.alloc(D, F32)
            self.load_bcast(gfin, dr["norm_final"][0:1, :])
        if STOP == "bc":
            return
        cw = A.alloc(NFC * 4, F32)
        cw3 = cw.re("p (c j) -> p c j", j=4)
        tmpF = A.alloc(512, F32)
        NSJ = self.NS * 2
        stT = A.alloc(NFC * NSJ, F32)
        stT3 = stT.re("p (c n) -> p c n", c=NFC)
        bk = self.bank[0]
        bk2 = self.bank[1]
        for c0 in range(0, NFC, 4):
            c1 = min(NFC, c0 + 4)
            w = (c1 - c0) * 128
            P.dma("sp", tmpF[0:3, 0:w], dr["ffn_conv_w"][layer, :, c0 * 128:c1 * 128])
            P.dma("sp", tmpF[3:4, 0:w], dr["ffn_conv_b"][layer:layer + 1, c0 * 128:c1 * 128])
            for c in range(c0, c1):
                P.tr(bk[:, c * 4:(c + 1) * 4], tmpF[0:4, (c - c0) * 128:(c - c0 + 1) * 128], self.c32("ident")[0:4, 0:4])
        P.copy("act", cw, bk[:, 0:NFC * 4])
        if STOP == "cw":
            return
        for c0 in range(0, NFC, 4):
            c1 = min(NFC, c0 + 4)
            w = (c1 - c0) * 128
            P.dma("sp", tmpF[0:NSJ, 0:w], dr["st_conv"][layer].re("s j f -> (s j) f")[:, c0 * 128:c1 * 128])
            for c in range(c0, c1):
                P.tr(bk2[:, (c - c0) * NSJ:(c - c0 + 1) * NSJ], tmpF[0:NSJ, (c - c0) * 128:(c - c0 + 1) * 128], self.c32("ident")[0:NSJ, 0:NSJ])
            P.copy("act", stT3[:, c0:c1, :], bk2[:, 0:(c1 - c0) * NSJ].re("p (c n) -> p c n", n=NSJ))
        if STOP == "st":
            return
        carry = A.alloc(NFC * 2, F32)
        carry3 = carry.re("p (c j) -> p c j", j=2)
        P.memset("pool", carry, 0.0)
        gsave = A.alloc(NFC * (NSJ + 2), F32)
        gsave3 = gsave.re("p (c n) -> p c n", c=NFC)
        xnT = A.alloc(8 * 512, BF16)
        xnT3 = xnT.re("p (k t) -> p k t", k=8)
        hT = A.alloc(NFC * 512, BF16)
        hT3 = hT.re("p (c t) -> p c t", c=NFC)
        xr = [A.alloc(D, F32) for _ in range(2)]
        xn = [A.alloc(D, BF16) for _ in range(2)]
        junk = hT[:, 0:2 * D].bitcast(F32)
        small = [A.alloc(4, F32) for _ in range(2)]
        gext = [A.alloc(514, F32) for _ in range(2)]
        acc = [A.alloc(512, F32) for _ in range(2)]
        yt = xr
        ysm = [A.alloc(4, F32) for _ in range(2)]
        macros = []
        i = 0
        while i < NT:
            n = min(4, NT - i)
            macros.append((i, n, False))
            i += n
        macros.append((NT, NST, True))
        cnt = 0
        for (t0, n, is_s) in macros:
            ntok = n * 128
            for j in range(n):
                b = cnt % 2
                cnt += 1
                P.dma("sp", xr[b], self.res_src(rin, t0 + j))
                if STOP == "L0":
                    continue
                self.rmsnorm(xr[b], gbc, xn[b], junk, small[b])
                if STOP == "L1":
                    continue
                self.transpose_bf(xn[b], xnT3, j * 128, self.bank[1])
            if STOP in ("L0", "L1", "L2"):
                continue
            for c in range(NFC):
                b = c % 2
                pg = self.bank[2 + b]
                pv = self.bank[4 + b]
                for k in range(8):
                    P.mm(pg[:, 0:ntok], wup3[:, k, c * 128:(c + 1) * 128], xnT3[:, k, 0:ntok], start=(k == 0), stop=(k == 7))
                for k in range(8):
                    P.mm(pv[:, 0:ntok], wup3[:, k, FF + c * 128:FF + (c + 1) * 128], xnT3[:, k, 0:ntok], start=(k == 0), stop=(k == 7))
                ge = gext[b]
                if STOP == "L3a":
                    continue
                P.copy("pool", ge[:, 0:2], carry3[:, c, :])
                P.copy("act", ge[:, 2:2 + ntok], pg[:, 0:ntok])
                if STOP == "L3":
                    continue
                if is_s:
                    P.copy("pool", ge[:, 2:2 + ntok].re("p (s q) -> p s q", q=16)[:, :, 0:2],
                           stT3[:, c, :].re("p (s j) -> p s j", j=2))
                a = acc[b]
                if STOP == "L3b":
                    continue
                P.ts("dve", a[:, 0:ntok], ge[:, 2:2 + ntok], cw3[:, c, 2:3], ALU.mult, cw3[:, c, 3:4], ALU.add)
                if STOP == "L3c":
                    continue
                P.stt("dve", a[:, 0:ntok], ge[:, 1:1 + ntok], cw3[:, c, 1:2], a[:, 0:ntok], ALU.mult, ALU.add)
                P.stt("dve", a[:, 0:ntok], ge[:, 0:ntok], cw3[:, c, 0:1], a[:, 0:ntok], ALU.mult, ALU.add)
                if STOP == "L4":
                    continue
                P.act(a[:, 0:ntok], a[:, 0:ntok], AF.Silu)
                P.tt("dve", hT3[:, c, 0:ntok], a[:, 0:ntok], pv[:, 0:ntok], ALU.mult)
                if is_s:
                    P.copy("pool", gsave3[:, c, 0:NSJ].re("p (s j) -> p s j", j=2),
                           ge[:, 2:2 + ntok].re("p (s q) -> p s q", q=16)[:, :, 4:6])
                else:
                    P.copy("pool", carry3[:, c, :], ge[:, ntok:ntok + 2])
                    if t0 + n == NT:
                        P.copy("pool", gsave3[:, c, NSJ:NSJ + 2], ge[:, ntok:ntok + 2])
            if STOP in ("L3a", "L3", "L3b", "L3c", "L4", "L6"):
                continue
            for j in range(n):
                b = cnt % 2
                cnt += 1
                ti = t0 + j
                P.dma("sp", xr[b], self.res_src(rin, ti))
                for hh in range(2):
                    pd = self.bank[6 + hh]
                    for c in range(NFC):
                        P.mm(pd, hT3[:, c, j * 128:(j + 1) * 128], wdn3[:, c, hh * 512:(hh + 1) * 512], start=(c == 0), stop=(c == NFC - 1))
                    P.tt("dve", yt[b][:, hh * 512:(hh + 1) * 512], pd, xr[b][:, hh * 512:(hh + 1) * 512], ALU.add)
                if not final:
                    P.dma("sp", self.res_src(rout, ti), yt[b])
                else:
                    self.rmsnorm(yt[b], gfin, yt[b], xnT[:, 0:2 * D].bitcast(F32), ysm[b])
                    if ti < NT:
                        P.dma("sp", dr["y_p"][ti * 128:(ti + 1) * 128, :], yt[b])
                    else:
                        P.dma("sp", dr["ys_pad"][(ti - NT) * 128:(ti - NT + 1) * 128, :], yt[b])
        if STOP == "main":
            return
        if final:
            P.dma("sp", dr["y_s"].re("(s j) d -> s j d", j=4), dr["ys_pad"].re("(s q) d -> s q d", q=16)[:, 2:6, :])
        NG = NSJ + 2
        sc_rows = dr["s_conv"][layer].re("s j f -> (s j) f")
        for c0 in range(0, NFC, 4):
            c1 = min(NFC, c0 + 4)
            w = (c1 - c0) * 128
            bk = self.bank[0]
            for c in range(c0, c1):
                P.tr(bk[0:NG, (c - c0) * 128:(c - c0 + 1) * 128], gsave3[:, c, :], self.c32("ident"))
            P.copy("act", tmpF[0:NG, 0:w], bk[0:NG, 0:w])
            P.dma("sp", sc_rows[:, c0 * 128:c1 * 128], tmpF[0:NSJ, 0:w])
            P.dma("sp", dr["p_conv"][layer][:, c0 * 128:c1 * 128], tmpF[NSJ:NSJ + 2, 0:w])


    def nb(self):
        self._nb = (getattr(self, "_nb", -1) + 1) % 6
        return self.bank[self._nb]

    def phase_rwkv(self):
        P, A, NT, NST, NS = self.P, self.A, self.NT, self.NST, self.NS
        dr = self.dr
        ident = self.c32("ident")
        identb = self.c16("ident")
        m_sl, m_su, m_ui, m_blk, maskc = (self.c32(k) for k in ("m_sl", "m_su", "m_ui", "m_blk", "maskc"))
        o_su = CF["m_su"][0]
        mask2 = self.cf[:, o_su:o_su + 256]
        wrkv = A.alloc(3 * 8 * D, BF16); wrkv4 = wrkv.re("p (q k n) -> p q k n", q=3, k=8)
        wo = A.alloc(8 * D, BF16); wo3 = wo.re("p (k n) -> p k n", k=8)
        w1 = A.alloc(8 * 64, BF16); w13 = w1.re("p (k n) -> p k n", k=8)
        a1 = A.alloc(8 * 64, BF16); a13 = a1.re("p (k n) -> p k n", k=8)
        g1 = A.alloc(8 * 128, BF16); g13 = g1.re("p (k n) -> p k n", k=8)
        w2 = A.alloc(D, BF16); a2 = A.alloc(D, BF16); g2 = A.alloc(D, BF16)
        for q in range(3):
            P.dma("pool", wrkv4[:, q, :, :], dr["rwkv_w_rkv"][q].re("(k p) n -> p k n", p=128))
        P.dma("pool", wo3, dr["rwkv_w_o"].re("(k p) n -> p k n", p=128))
        P.dma("pool", w13, dr["rwkv_w1"].re("(k p) n -> p k n", p=128))
        P.dma("pool", a13, dr["rwkv_a1"].re("(k p) n -> p k n", p=128))
        P.dma("pool", g13, dr["rwkv_g1"].re("(k p) n -> p k n", p=128))
        P.dma("pool", w2[0:64, :], dr["rwkv_w2"])
        P.dma("pool", a2[0:64, :], dr["rwkv_a2"])
        P.dma("pool", g2, dr["rwkv_g2"])
        bc = {}
        for nm, src in (("gmix", dr["norm_mix"][0:1, :]), ("w0", dr["rwkv_w0"]), ("a0", dr["rwkv_a0"]), ("kk", dr["rwkv_k_k"]),
                        ("ka", dr["rwkv_k_a"]), ("rk", dr["rwkv_r_k"]), ("lnw", dr["rwkv_ln_w"]), ("lnb", dr["rwkv_ln_b"])):
            bc[nm] = A.alloc(D, F32)
            self.load_bcast(bc[nm], src)
        FW = 8 * 129
        Ft = [A.alloc(FW, F32) for _ in range(8)]
        F = [f[:, 0:D] for f in Ft]
        mu_sb = A.alloc(48, F32); mu3 = mu_sb.re("p (q k) -> p q k", q=6)
        P.dma("sp", F[0][0:6, :], dr["rwkv_mu"])
        bk = self.nb()
        for k in range(8):
            P.tr(bk[:, k * 6:(k + 1) * 6], F[0][0:6, k * 128:(k + 1) * 128], ident[0:6, 0:6])
        P.copy("act", mu_sb.re("p (q k) -> p k q", q=6), bk[:, 0:48].re("p (k q) -> p k q", q=6))
        carry = A.alloc(8, F32); P.memset("pool", carry, 0.0)
        small = A.alloc(64, F32)
        PcT = A.alloc(64, F32); PcT3 = PcT.re("p (a c) -> p a c", a=8)
        prodb = A.alloc(6 * 8 * 128, BF16)
        xsT = prodb.re("p (q k t) -> p q k t", q=6, k=8)
        At, Bt, Kt, Rt, Bp = (prodb[:, i * D:(i + 1) * D] for i in range(5))
        yb = prodb[:, 0:D]
        yT = prodb[:, D:2 * D]; yT3 = yT.re("p (k t) -> p k t", k=8)
        QT = A.alloc(8 * 4 * 128, BF16); QT4 = QT.re("p (a q t) -> p a q t", a=8, q=4)
        Vblk = A.alloc(8 * 2 * 128, BF16); Vblk4 = Vblk.re("p (a h n) -> p a h n", a=8, h=2)
        P.memset("pool", Vblk, 0.0)
        S32 = A.alloc(D, F32); S323 = S32.re("p (a n) -> p a n", a=8)
        Sbf = A.alloc(D, BF16); Sbf3 = Sbf.re("p (a n) -> p a n", a=8)
        P.memset("pool", S32, 0.0); P.memset("pool", Sbf, 0.0)
        fence_scr = A.alloc(2, F32)
        alias_toks = []

        def sub(fk, off, cols, dt):
            if dt == BF16:
                ap = Ft[fk].ap.bitcast(BF16)[:, off // 2:off // 2 + cols]
            else:
                ap = Ft[fk].ap[:, off // 4:off // 4 + cols]
            v_ = V(ap, toks=[Tok()])
            alias_toks.extend(v_.toks)
            return v_
        XBs, MBs, ARKs, P2s, P4s, P8s, NHs, Bblks, tmpDs = [], [], [], [], [], [], [], [], []
        for hs in range(4):
            if hs < 2:
                XBs.append((A.alloc(320, BF16), A.alloc(320, BF16)))
                MBs.append(A.alloc(256, BF16)); ARKs.append(A.alloc(128, BF16))
                P2s.append(A.alloc(256, BF16)); P4s.append(A.alloc(256, BF16)); P8s.append(A.alloc(128, BF16))
                NHs.append(A.alloc(128, BF16)); Bblks.append(A.alloc(512, BF16)); tmpDs.append(A.alloc(64, F32))
            else:
                fk = 1 if hs == 2 else 2
                XBs.append((sub(fk, 0, 320, BF16), sub(fk, 640, 320, BF16)))
                MBs.append(sub(fk, 1280, 256, BF16)); ARKs.append(sub(fk, 1792, 128, BF16))
                P2s.append(sub(fk, 2048, 256, BF16)); P4s.append(sub(fk, 2560, 256, BF16)); P8s.append(sub(fk, 3072, 128, BF16))
                NHs.append(sub(fk, 3328, 128, BF16))
                o6 = (hs - 2) * 1280
                Bblks.append(sub(6, o6, 512, BF16)); tmpDs.append(sub(6, o6 + 1024, 64, F32))
        Apair = [A.alloc(128, BF16) for _ in range(2)]
        TTp = [A.alloc(D, BF16) for _ in range(2)]; TTp3 = [t_.re("p (c n) -> p c n", c=8) for t_ in TTp]
        Db = [[A.alloc(D, BF16) for _ in range(2)] for _ in range(2)]
        Db3 = [[d_.re("p (c n) -> p c n", c=8) for d_ in dd_] for dd_ in Db]
        RTb = [A.alloc(1152, BF16) for _ in range(2)]
        WT = [[A.alloc(128, BF16) for _ in range(2)] for _ in range(2)]
        hw = A.alloc(128, BF16); ha = A.alloc(128, BF16); hg = A.alloc(128, BF16)
        Sc32 = [A.alloc(128, F32) for _ in range(2)]
        Scb = [A.alloc(128, BF16) for _ in range(2)]
        for t_ in TTp + Db[0] + Db[1] + RTb:
            P.memset("pool", t_, 0.0)

        def fence():
            f_ = fence_scr.ap
            P.add("pool", lambda e: e.memset(f_, 0.0), wr=F[1].toks + F[2].toks + F[6].toks + alias_toks + fence_scr.toks, cost=150.0)
        ST_in = self.dscr("wkv_in_T", [NS * 8 * 128, 128])
        ST_out = self.dscr("wkv_out_T", [NS * 8 * 128, 128])
        Z = F[1]; Z3 = Z.re("p (a n) -> p a n", a=8)
        Zo = F[2]; Zo3 = Zo.re("p (a n) -> p a n", a=8)
        P.memset("pool", Z, 0.0)
        for s in range(NS):
            for hh in range(2):
                P.dma("sp", Z3[64 * hh:64 * hh + 64, :, 64 * hh:64 * hh + 64],
                      dr["st_wkv"][s].re("(a h) v k -> h v a k", h=2)[hh])
            for a in range(8):
                bk = self.nb()
                P.tr(bk[:, 0:128], Z3[:, a, :], ident)
                P.copy("act", Zo3[:, a, :], bk[:, 0:128])
            P.dma("sp", ST_in[s * 1024:(s + 1) * 1024, :].re("(a p) n -> p a n", p=128), Zo3)

        for ti in range(self.NTT):
            is_s = ti >= NT
            stl = ti - NT
            negc = self.c32("negc_s") if is_s else self.c32("negc_p")
            rowm = self.c32("rowm_s")
            x = F[0]
            P.dma("sp", x, self.res_src("R0", ti))
            xn32 = F[5]
            self.rmsnorm(x, bc["gmix"], xn32, F[7], small[:, 0:4])
            if is_s:
                for s_ in range(8):
                    sq = stl * 8 + s_
                    P.dma("sp", xn32[16 * s_ + 1:16 * s_ + 2, :], dr["st_shift"][sq:sq + 1, :])
                for s_ in range(8):
                    sq = stl * 8 + s_
                    P.dma("sp", dr["s_shift"][sq:sq + 1, :], xn32[16 * s_ + 5:16 * s_ + 6, :])
            elif ti == NT - 1:
                P.dma("sp", dr["p_shift"], xn32[127:128, :])
            xnTe = Ft[4].re("p (k t) -> p k t", k=8)
            for hf in range(2):
                bk = self.nb()
                for k in range(4):
                    kk_ = hf * 4 + k
                    P.tr(bk[:, k * 128:(k + 1) * 128], xn32[:, kk_ * 128:(kk_ + 1) * 128], ident)
                P.copy("act", xnTe[:, hf * 4:hf * 4 + 4, 1:129], bk.re("p (k t) -> p k t", k=4))
            P.copy("pool", xnTe[:, :, 0:1], carry.re("p (k o) -> p k o", o=1))
            P.copy("pool", carry.re("p (k o) -> p k o", o=1), xnTe[:, :, 128:129])
            xx = F[6].re("p (k t) -> p k t", k=8)
            mt = F[7].re("p (k t) -> p k t", k=8)
            P.tt("pool", xx, xnTe[:, :, 0:128], xnTe[:, :, 1:129], ALU.subtract)
            for q in range(6):
                P.tt("pool", mt, xx, mu3[:, q, :].bc(2, [128, 8, 128]), ALU.mult)
                P.tt("dve", xsT[:, q, :, :], mt, xnTe[:, :, 1:129], ALU.add)
            bk = self.nb()
            for k in range(8):
                P.mm(bk[0:64, 0:128], w13[:, k, :], xsT[:, 3, k, :], start=(k == 0), stop=(k == 7))
            for k in range(8):
                P.mm(bk[0:64, 128:256], a13[:, k, :], xsT[:, 4, k, :], start=(k == 0), stop=(k == 7))
            for k in range(8):
                P.mm(bk[:, 256:384], g13[:, k, :], xsT[:, 5, k, :], start=(k == 0), stop=(k == 7))
            P.act(hw[0:64, :], bk[0:64, 0:128], AF.Tanh)
            P.copy("act", ha[0:64, :], bk[0:64, 128:256])
            P.act(hg, bk[:, 256:384], AF.Sigmoid)
            r32, k32, v32, g32, lw, a32, kkn = F[0], F[1], F[3], F[4], F[5], F[6], F[2]
            for q, dst in ((0, r32), (1, k32), (2, v32)):
                for hf in range(2):
                    bk = self.nb()
                    for k in range(8):
                        P.mm(bk, xsT[:, q, k, :], wrkv4[:, q, k, hf * 512:(hf + 1) * 512], start=(k == 0), stop=(k == 7))
                    P.copy("act", dst[:, hf * 512:(hf + 1) * 512], bk)
            for hf in range(2):
                cs = slice(hf * 512, (hf + 1) * 512)
                bk = self.nb()
                P.mm(bk, hw[0:64, :], w2[0:64, cs])
                P.tt("dve", lw[:, cs], bk, bc["w0"][:, cs], ALU.add)
                bk = self.nb()
                P.mm(bk, ha[0:64, :], a2[0:64, cs])
                P.tt("dve", a32[:, cs], bk, bc["a0"][:, cs], ALU.add)
                bk = self.nb()
                P.mm(bk, hg, g2[:, cs])
                P.copy("act", g32[:, cs], bk)
            P.act(lw, lw, AF.Sigmoid)
            P.ts("dve", lw, lw, negc, ALU.mult)
            P.act(a32, a32, AF.Sigmoid)
            if is_s:
                P.ts("pool", v32, v32, rowm, ALU.mult)
            v4 = v32.re("p (a h n) -> p a h n", a=8, h=2)
            P.copy("pool", Vblk4[:, :, 0, 0:64], v4[:, :, 0, :])
            P.copy("pool", Vblk4[:, :, 1, 64:128], v4[:, :, 1, :])
            t3 = F[7]
            P.tt("pool", kkn, k32, bc["kk"], ALU.mult)
            P.tt("pool", t3, kkn, kkn, ALU.mult)
            P.red("dve", small[:, 16:32], t3.re("p (h n) -> p h n", h=16))
            P.act(small[:, 16:32], small[:, 16:32], AF.Ln, bias=self.c32("eps24"))
            P.act(small[:, 16:32], small[:, 16:32], AF.Exp, scale=-0.5)
            if is_s:
                P.ts("dve", small[:, 16:32], small[:, 16:32], rowm, ALU.mult)
            P.tt("dve", kkn.re("p (h n) -> p h n", h=16), kkn.re("p (h n) -> p h n", h=16),
                 small[:, 16:32].bc(2, [128, 16, 64]), ALU.mult)
            P.stt("dve", t3, a32, -1.0, bc["ka"], ALU.add, ALU.mult)
            P.stt("dve", k32, t3, 1.0, k32, ALU.add, ALU.mult)
            if is_s:
                P.ts("pool", k32, k32, rowm, ALU.mult)
            P.tt("pool", t3, r32, bc["rk"], ALU.mult)
            P.tt("pool", t3, t3, k32, ALU.mult)
            P.red("dve", small[:, 32:48], t3.re("p (h n) -> p h n", h=16))
            P.tt("pool", a32, a32, kkn, ALU.mult)
            bk = self.nb()
            for a in range(8):
                P.mm(bk[:, a * 8:(a + 1) * 8], lw[:, a * 128:(a + 1) * 128], maskc)
            P.act(PcT, bk[:, 0:64], AF.Exp)
            def cum(mask, hf):
                b_ = self.nb()
                P.mm(b_, mask, lw[:, hf * 512:(hf + 1) * 512])
                return b_
            for hf in range(2):
                cs = slice(hf * 512, (hf + 1) * 512)
                bL = cum(m_ui, hf)
                P.act(t3[:, cs], bL, AF.Exp)
                P.tt("dve", Rt[:, cs], r32[:, cs], t3[:, cs], ALU.mult)
            for hf in range(2):
                cs = slice(hf * 512, (hf + 1) * 512)
                bL = cum(m_ui, hf)
                P.act(t3[:, cs], bL, AF.Exp, scale=-1.0)
                P.tt("dve", Bt[:, cs], a32[:, cs], t3[:, cs], ALU.mult)
                P.tt("pool", Kt[:, cs], k32[:, cs], t3[:, cs], ALU.mult)
            for hf in range(2):
                cs = slice(hf * 512, (hf + 1) * 512)
                bL = cum(m_su, hf)
                P.act(t3[:, cs], bL, AF.Exp)
                P.stt("dve", At[:, cs], kkn[:, cs], -1.0, t3[:, cs], ALU.mult, ALU.mult)
            Kp = lw
            for hf in range(2):
                cs = slice(hf * 512, (hf + 1) * 512)
                bL = cum(m_sl, hf)
                P.act(t3[:, cs], bL, AF.Exp)
                P.tt("dve", Bp[:, cs], a32[:, cs], t3[:, cs], ALU.mult)
            P.tt("pool", Kp, k32, t3, ALU.mult)
            for q, src in enumerate((Kt, Bt, At, Rt)):
                bk = self.nb()
                pb = bk.bitcast(BF16)
                for a in range(8):
                    P.tr(pb[:, a * 128:(a + 1) * 128], src[:, a * 128:(a + 1) * 128], identb)
                P.copy("act", QT4[:, :, q, :], pb.re("p (a t) -> p a t", a=8))
            Yall = F[0]
            fence()
            for a in range(8):
                pp_ = a % 2
                for hh in range(2):
                    hs = (2 * a + hh) % 4
                    MB_, ARK_, P2_, P4_, P8_, NH_ = MBs[hs], ARKs[hs], P2s[hs], P4s[hs], P8s[hs], NHs[hs]
                    rows = slice(64 * hh, 64 * hh + 64)
                    hc = slice((2 * a + hh) * 64, (2 * a + hh) * 64 + 64)
                    xb0, xb1 = XBs[hs]
                    P.copy("pool", xb0[:, 0:64], At[:, hc])
                    bk = self.nb()
                    P.mm(bk[:, 0:256], QT4[rows, a, 2, :], QT4[rows, a, 0:2, :])
                    P.tt("dve", xb0[:, 64:320].re("p (b t) -> p b t", b=2), bk[:, 0:256].re("p (b t) -> p b t", b=2),
                         m_sl.bc(1, [128, 2, 128]), ALU.mult)
                    Nak, Nm = xb0[:, 64:192], xb0[:, 192:320]
                    bk = self.nb()
                    P.mm(bk[:, 0:256], QT4[rows, a, 1, :], QT4[rows, a, 2:4, :])
                    P.tt("dve", MB_, bk[:, 0:256], mask2, ALU.mult)
                    Mm, Arb = MB_[:, 0:128], MB_[:, 128:256]
                    bk = self.nb()
                    P.mm(bk[:, 0:128], QT4[rows, a, 0, :], QT4[rows, a, 3, :])
                    P.tt("dve", ARK_, bk[:, 0:128], m_ui, ALU.mult)
                    bk = self.nb()
                    P.mm(bk[:, 0:128], Nm, Mm)
                    P.mm(bk[:, 128:256], Mm, Nm)
                    P.copy("act", P2_, bk[:, 0:256])
                    bk = self.nb()
                    P.mm(bk[:, 0:128], P2_[:, 128:256], P2_[:, 0:128])
                    P.mm(bk[:, 128:256], P2_[:, 0:128], P2_[:, 128:256])
                    P.copy("act", P4_, bk[:, 0:256])
                    bk = self.nb()
                    P.mm(bk[:, 0:128], P4_[:, 128:256], P4_[:, 0:128])
                    P.copy("act", P8_, bk[:, 0:128])
                    cur, nxt = xb0, xb1
                    for lv, Ml in enumerate((Mm, P2_[:, 0:128], P4_[:, 0:128], P8_)):
                        bk = self.nb()
                        P.mm(bk[:, 0:192], Ml, cur[:, 0:192])
                        if lv < 3:
                            P.tt("dve", nxt[:, 0:192], bk[:, 0:192], cur[:, 0:192], ALU.add)
                            cur, nxt = nxt, cur
                        else:
                            P.tt("dve", Apair[pp_][:, 64 * hh:64 * hh + 64], bk[:, 0:64], cur[:, 0:64], ALU.add)
                            P.tt("dve", NH_, bk[:, 64:192], cur[:, 64:192], ALU.add)
                for hh in range(2):
                    rows = slice(64 * hh, 64 * hh + 64)
                    hc = slice((2 * a + hh) * 64, (2 * a + hh) * 64 + 64)
                    hs = (2 * a + hh) % 4
                    Arb = MBs[hs][:, 128:256]
                    bb = Bblks[hs].re("p (c n) -> p c n", c=8)
                    P.tt("pool", bb, Bp[:, hc].bc(1, [128, 8, 64]), maskc.bc(2, [128, 8, 64]), ALU.mult)
                    bk = self.nb()
                    P.mm(bk, Apair[pp_], Bblks[hs])
                    P.copy("act", TTp3[pp_][rows, :, 64 * hh:64 * hh + 64], bk[rows, :].re("p (c n) -> p c n", c=8))
                    bk = self.nb()
                    P.mm(bk[:, 0:64], NHs[hs], Bp[:, hc])
                    P.tt("dve", tmpDs[hs], bk[:, 0:64], Kp[:, hc], ALU.add)
                    P.tt("pool", Db3[pp_][hh][:, :, 64 * hh:64 * hh + 64], tmpDs[hs].bc(1, [128, 8, 64]), maskc.bc(2, [128, 8, 64]), ALU.mult)
                    bk = self.nb()
                    P.mm(bk[:, 0:128], Apair[pp_], Arb)
                    P.tt("dve", RTb[pp_][rows, :].re("p (c s) -> p c s", s=144)[:, :, 0:16],
                         bk[rows, 0:128].re("p (c t) -> p c t", t=16), QT4[rows, a, 3, :].re("p (c t) -> p c t", t=16), ALU.add)
                    bk = self.nb()
                    P.mm(bk[:, 0:128], NHs[hs], Arb)
                    P.tt("dve", WT[pp_][hh], bk[:, 0:128], ARKs[hs], ALU.add)
                ybk = self.bank[6 + a % 2]
                P.mm(ybk[:, 0:128], WT[pp_][0], Vblk4[:, a, 0, :], start=True, stop=False)
                P.mm(ybk[:, 0:128], WT[pp_][1], Vblk4[:, a, 1, :], start=False, stop=False)
                for c in range(8):
                    if is_s:
                        sq = stl * 8 + c
                        sc32, scb = Sc32[c % 2], Scb[c % 2]
                        r0 = (sq * 8 + a) * 128
                        P.dma("sp", sc32, ST_in[r0:r0 + 128, :])
                        P.copy("act", scb, sc32)
                        s_b, s_32 = scb, sc32
                    else:
                        s_b, s_32 = Sbf3[:, a, :], S323[:, a, :]
                    P.mm(ybk[:, 0:128], RTb[pp_][:, c * 128:(c + 1) * 128], s_b, start=False, stop=(c == 7))
                    sbk = self.nb()
                    P.mm(sbk[:, 0:128], TTp3[pp_][:, c, :], s_b, start=True, stop=False)
                    P.mm(sbk[:, 0:128], Db3[pp_][0][:, c, :], Vblk4[:, a, 0, :], start=False, stop=False)
                    P.mm(sbk[:, 0:128], Db3[pp_][1][:, c, :], Vblk4[:, a, 1, :], start=False, stop=True)
                    P.stt("dve", s_32, s_32, PcT3[:, a, c:c + 1], sbk[:, 0:128], ALU.mult, ALU.add)
                    if is_s:
                        P.dma("sp", ST_out[r0:r0 + 128, :], s_32)
                    else:
                        P.copy("act", s_b, s_32)
                P.copy("act", Yall[:, a * 128:(a + 1) * 128], ybk[:, 0:128])
            fence()
            if self.debug:
                P.dma("sp", self.dr["dbg_y"][ti * 128:(ti + 1) * 128, :], Yall)
                P.dma("sp", self.dr["dbg_g"][ti * 128:(ti + 1) * 128, :], g32)
            Y3 = Yall.re("p (h n) -> p h n", h=16)
            sm = small
            P.red("dve", sm[:, 0:16], Y3)
            P.tt("pool", t3, Yall, Yall, ALU.mult)
            P.red("dve", sm[:, 48:64], t3.re("p (h n) -> p h n", h=16))
            P.ts("dve", sm[:, 0:16], sm[:, 0:16], 1.0 / 64, ALU.mult)
            P.tt("dve", sm[:, 16:32], sm[:, 0:16], sm[:, 0:16], ALU.mult)
            P.stt("dve", sm[:, 48:64], sm[:, 48:64], 1.0 / 64, sm[:, 16:32], ALU.mult, ALU.subtract)
            P.act(sm[:, 48:64], sm[:, 48:64], AF.Ln, bias=self.c32("gneps"))
            P.act(sm[:, 48:64], sm[:, 48:64], AF.Exp, scale=-0.5)
            P.tt("dve", Y3, Y3, sm[:, 0:16].bc(2, [128, 16, 64]), ALU.subtract)
            P.tt("dve", Y3, Y3, sm[:, 48:64].bc(2, [128, 16, 64]), ALU.mult)
            P.tt("pool", Yall, Yall, bc["lnw"], ALU.mult)
            P.tt("pool", Yall, Yall, bc["lnb"], ALU.add)
            P.tt("pool", t3.re("p (h n) -> p h n", h=16), v32.re("p (h n) -> p h n", h=16),
                 sm[:, 32:48].bc(2, [128, 16, 64]), ALU.mult)
            P.tt("dve", Yall, Yall, t3, ALU.add)
            if self.debug:
                P.dma("sp", self.dr["dbg_yn"][ti * 128:(ti + 1) * 128, :], Yall)
            P.tt("dve", yb, Yall, g32, ALU.mult)
            self.transpose_bf(yb, yT3, 0, self.nb())
            xo = F[0]
            xr_ = F[7]
            P.dma("sp", xr_, self.res_src("R0", ti))
            for hf in range(2):
                cs = slice(hf * 512, (hf + 1) * 512)
                bk = self.nb()
                for k in range(8):
                    P.mm(bk, yT3[:, k, :], wo3[:, k, cs], start=(k == 0), stop=(k == 7))
                P.tt("dve", xo[:, cs], bk, xr_[:, cs], ALU.add)
            P.dma("sp", self.res_src("R1", ti), xo)
        Zt = F[1]; Zt3 = Zt.re("p (a n) -> p a n", a=8)
        for a in range(8):
            bk = self.nb()
            P.tr(bk[:, 0:128], S323[:, a, :], ident)
            P.copy("act", Zt3[:, a, :], bk[:, 0:128])
        for hh in range(2):
            P.dma("sp", dr["p_wkv"].re("(a h) v k -> h v a k", h=2)[hh], Zt3[64 * hh:64 * hh + 64, :, 64 * hh:64 * hh + 64])
        for s in range(NS):
            Zi = F[3].re("p (a n) -> p a n", a=8)
            Zo2 = F[4 + s % 2].re("p (a n) -> p a n", a=8)
            P.dma("sp", Zi, ST_out[s * 1024:(s + 1) * 1024, :].re("(a p) n -> p a n", p=128))
            for a in range(8):
                bk = self.nb()
                P.tr(bk[:, 0:128], Zi[:, a, :], ident)
                P.copy("act", Zo2[:, a, :], bk[:, 0:128])
            for hh in range(2):
                P.dma("sp", dr["s_wkv"][s].re("(a h) v k -> h v a k", h=2)[hh], Zo2[64 * hh:64 * hh + 64, :, 64 * hh:64 * hh + 64])


    def phase_attn(self):
        P, A, NT, NST, NS, NTT, T = self.P, self.A, self.NT, self.NST, self.NS, self.NTT, self.T
        dr = self.dr
        identb = self.c16("ident")
        QTs = [[self.dscr("qT%d_%d" % (g, j), [8 * 128, NTT * 128], BF16) for j in range(2)] for g in range(3)]
        Vs = [self.dscr("vs%d" % g, [NTT * 128, H * HS], BF16) for g in range(3)]
        Os = [self.dscr("os%d" % g, [NTT * 128, H * 65], F32) for g in range(3)]
        qs = [self.dscr("qs%d" % g, [NST * 128, D], F32) for g in range(3)]
        kpad = [self.dscr("kpad%d" % g, [NST * 128, D], F32) for g in range(3)]
        vpad = [self.dscr("vpad%d" % g, [NST * 128, D], F32) for g in range(3)]
        A.mark()
        win = A.alloc(8 * 9 * D, BF16); win3 = win.re("p (k n) -> p k n", k=8)
        for k in range(8):
            P.dma("pool", win3[:, k, :], dr["attn_w_in"][k * 128:(k + 1) * 128, :])
        gbc = A.alloc(D, F32)
        self.load_bcast(gbc, dr["norm_mix"][1:2, :])
        rope = A.alloc(NT * 128, F32); rope3 = rope.re("p (t n) -> p t n", n=128)
        P.dma("sp", rope, dr["rope_p"])
        x = A.alloc(D, F32)
        xn = A.alloc(D, BF16)
        xnT = A.alloc(D, BF16); xnT3 = xnT.re("p (k t) -> p k t", k=8)
        small = A.alloc(4, F32)
        xq = A.alloc(D, F32); A32 = A.alloc(D, F32); B32 = A.alloc(D, F32)
        ob = [A.alloc(D, BF16) for _ in range(2)]
        vb = [A.alloc(H * HS, BF16) for _ in range(2)]
        for v_ in vb:
            P.memset("pool", v_, 1.0)
        QTt = [A.alloc(D, BF16) for _ in range(2)]
        cnt = 0
        for ti in range(NTT):
            is_s = ti >= NT
            stl = ti - NT
            P.dma("sp", x, self.res_src("R2", ti))
            self.rmsnorm(x, gbc, xn, A32, small)
            self.transpose_bf(xn, xnT3, 0, self.bank[6])
            if is_s:
                o_r = CF["rope_s"][0]
                cs2 = self.cf[:, o_r:o_r + 64]
                sn2 = self.cf[:, o_r + 64:o_r + 128]
            else:
                cs2 = rope3[:, ti, 0:64]
                sn2 = rope3[:, ti, 64:128]
            for g in range(3):
                for j in range(3):
                    col0 = g * 3 * D + j * D
                    bks = []
                    for hf in range(2):
                        bk = self.bank[(2 * cnt + hf) % 6]
                        bks.append(bk)
                        for k in range(8):
                            P.mm(bk, xnT3[:, k, :], win3[:, k, col0 + hf * 512:col0 + (hf + 1) * 512], start=(k == 0), stop=(k == 7))
                    cnt += 1
                    if j == 2:
                        for hf in range(2):
                            P.copy("act", xq[:, hf * 512:(hf + 1) * 512], bks[hf])
                        vbb = vb[cnt % 2]
                        P.copy("pool", vbb.re("p (h n) -> p h n", n=HS)[:, :, 0:64], xq.re("p (h n) -> p h n", n=64))
                        P.dma("sp", Vs[g][ti * 128:(ti + 1) * 128, :], vbb)
                        if is_s:
                            P.dma("sp", vpad[g][stl * 128:(stl + 1) * 128, :], xq)
                        else:
                            keep = min(WINDOWS[g], T) // 128
                            if ti >= NT - keep:
                                r0 = (ti - (NT - keep)) * 128
                                P.dma("sp", dr["p_v%d" % g][r0:r0 + 128, :], xq)
                        continue
                    for hf in range(2):
                        cs = slice(hf * 512, (hf + 1) * 512)
                        P.copy("act", xq[:, cs], bks[hf])
                        P.tt("dve", A32[:, cs].re("p (h n) -> p h n", n=64), xq[:, cs].re("p (h n) -> p h n", n=64),
                             cs2.bc(1, [128, 8, 64]), ALU.mult)
                    P.tt("pool", B32.re("p (h n) -> p h n", n=64), xq.re("p (h n) -> p h n", n=64), sn2.bc(1, [128, 16, 64]), ALU.mult)
                    A4 = A32.re("p (h b n) -> p h b n", b=2, n=32)
                    B4 = B32.re("p (h b n) -> p h b n", b=2, n=32)
                    P.tt("dve", A4[:, :, 0, :], A4[:, :, 0, :], B4[:, :, 1, :], ALU.subtract)
                    P.tt("pool", A4[:, :, 1, :], A4[:, :, 1, :], B4[:, :, 0, :], ALU.add)
                    if j == 1:
                        if is_s:
                            P.dma("sp", kpad[g][stl * 128:(stl + 1) * 128, :], A32)
                        else:
                            keep = min(WINDOWS[g], T) // 128
                            if ti >= NT - keep:
                                r0 = (ti - (NT - keep)) * 128
                                P.dma("sp", dr["p_k%d" % g][r0:r0 + 128, :], A32)
                    elif is_s:
                        P.dma("sp", qs[g][stl * 128:(stl + 1) * 128, :], A32)
                    obb = ob[cnt % 2]
                    P.copy("act", obb, A32)
                    qt = QTt[cnt % 2]
                    bk = self.bank[7]
                    pb = bk.bitcast(BF16)
                    for a in range(8):
                        P.tr(pb[:, a * 128:(a + 1) * 128], obb[:, a * 128:(a + 1) * 128], identb)
                    P.copy("act", qt, pb)
                    P.dma("sp", QTs[g][j][:, ti * 128:(ti + 1) * 128].re("(a p) t -> p a t", p=128), qt.re("p (a t) -> p a t", a=8))
        A.release()
        P.barrier()
        import os
        ASTOP = os.environ.get("ATT_STOP", "")
        if ASTOP == "s1":
            return
        for g in range(3):
            P.dma("sp", dr["s_k%d" % g].re("(s j) d -> s j d", j=4), kpad[g].re("(s q) d -> s q d", q=16)[:, 2:6, :])
            P.dma("sp", dr["s_v%d" % g].re("(s j) d -> s j d", j=4), vpad[g].re("(s q) d -> s q d", q=16)[:, 2:6, :])
        A.mark()
        NBLK = NT
        Vg = A.alloc(NBLK * H * HS, BF16); Vg4 = Vg.re("p (b h n) -> p b h n", b=NBLK, n=HS)
        Qn = [A.alloc(T, BF16) for _ in range(2)]
        Kn = [A.alloc(T, BF16) for _ in range(2)]
        Qc = [A.alloc(T, BF16) for _ in range(2)]
        Kc = [A.alloc(T, BF16) for _ in range(2)]
        Obuf = [A.alloc(NBLK * 130, F32) for _ in range(2)]
        pT = [A.alloc(512, BF16) for _ in range(3)]
        o_m = CB["am_ge"][0]
        am2 = self.cb[:, o_m:o_m + 256]
        it = 0
        for g in range(3):
            d = DILS[g]
            L = T // d
            nbc = L // 128
            assert nbc >= 1
            for c in range(d):
                for n in range(nbc):
                    blk = c * nbc + n
                    r0 = c + d * 128 * n
                    src = Vs[g][r0:r0 + d * 127 + 1, :]
                    if d > 1:
                        src = V(Vs[g].ap[r0:r0 + d * 128 - (d - 1):d, :], Vs[g].toks) if False else None
                    if d == 1:
                        P.dma("sp", Vg4[:, blk, :, :], Vs[g][r0:r0 + 128, :].re("p (h n) -> p h n", n=HS))
                    else:
                        rows = Vs[g][0:T, :].re("(l c) n -> c l n", c=d)[c, n * 128:(n + 1) * 128, :]
                        P.dma("sp", Vg4[:, blk, :, :], rows.re("p (h n) -> p h n", n=HS))
            if ASTOP == "s2a":
                continue
            for a in range(8):
                b = a % 2
                P.dma("sp", Qn[b], QTs[g][0][a * 128:(a + 1) * 128, 0:T])
                P.dma("sp", Kn[b], QTs[g][1][a * 128:(a + 1) * 128, 0:T])
                if d == 1:
                    qc3 = Qn[b].re("p (c l) -> p c l", c=1)
                    kc3 = Kn[b].re("p (c l) -> p c l", c=1)
                else:
                    qc3 = Qc[b].re("p (c l) -> p c l", c=d)
                    kc3 = Kc[b].re("p (c l) -> p c l", c=d)
                    P.copy("pool", qc3, Qn[b].re("p (l c) -> p c l", c=d))
                    P.copy("pool", kc3, Kn[b].re("p (l c) -> p c l", c=d))
                Ob3 = Obuf[b].re("p (b n) -> p b n", n=130)
                if ASTOP == "s2q":
                    continue
                for c in range(d):
                    for n in range(nbc):
                        blk = c * nbc + n
                        it += 1
                        pt = pT[it % 3]
                        for hh in range(2):
                            bk = self.bank[(2 * it + hh) % 4]
                            rows = slice(64 * hh, 64 * hh + 64)
                            if n > 0:
                                P.mm(bk[:, 0:128], kc3[rows, c, (n - 1) * 128:n * 128], qc3[rows, c, n * 128:(n + 1) * 128])
                            P.mm(bk[:, 128:256], kc3[rows, c, n * 128:(n + 1) * 128], qc3[rows, c, n * 128:(n + 1) * 128])
                            if n > 0:
                                P.act(pt[:, hh * 256:(hh + 1) * 256], bk[:, 0:256], AF.Exp, scale=0.125)
                            else:
                                P.act(pt[:, hh * 256 + 128:(hh + 1) * 256], bk[:, 128:256], AF.Exp, scale=0.125)
                        if n > 0:
                            P.tt("pool", pt.re("p (h m) -> p h m", h=2), pt.re("p (h m) -> p h m", h=2), am2.bc(1, [128, 2, 256]), ALU.mult)
                        else:
                            ptv = pt.re("p (h m) -> p h m", h=2)[:, :, 128:256]
                            P.tt("pool", ptv, ptv, am2[:, 128:256].bc(1, [128, 2, 128]), ALU.mult)
                        if ASTOP == "s2m":
                            continue
                        okb = self.bank[4 + it % 4]
                        for hh in range(2):
                            hd = 2 * a + hh
                            if n > 0:
                                P.mm(okb[:, hh * 65:hh * 65 + 65], pt[:, hh * 256:hh * 256 + 128], Vg4[:, blk - 1, hd, 0:65], start=True, stop=False)
                            P.mm(okb[:, hh * 65:hh * 65 + 65], pt[:, hh * 256 + 128:hh * 256 + 256], Vg4[:, blk, hd, 0:65], start=(n == 0), stop=True)
                        P.copy("act", Ob3[:, blk, :], okb[:, 0:130])
                if ASTOP in ("s2m", "s2p"):
                    continue
                dst = Os[g][0:T, a * 130:(a + 1) * 130].re("(n i c) m -> i c n m", i=128, c=d)
                srcv = Obuf[b].re("p (c n m) -> p c n m", c=d, m=130)
                for c in range(d):
                    for n0 in range(0, nbc, 8):
                        n1 = min(nbc, n0 + 8)
                        P.dma("sp", dst[:, c, n0:n1, :], srcv[:, c, n0:n1, :])
        A.release()
        P.barrier()
        if ASTOP in ("s2", "s2a", "s2q", "s2m", "s2p"):
            return
        A.mark()
        self.attn_sample(Os, qs, kpad, vpad)
        A.release()
        P.barrier()
        if ASTOP == "s2b":
            return
        A.mark()
        wo = A.alloc(8 * D, BF16); wo3 = wo.re("p (k n) -> p k n", k=8)
        P.dma("pool", wo3, dr["attn_w_o"].re("(k p) n -> p k n", p=128))
        O3 = [[A.alloc(H * 65, F32) for _ in range(3)] for _ in range(2)]
        xr = [A.alloc(D, F32) for _ in range(2)]
        obf = [A.alloc(D, BF16) for _ in range(2)]
        oT = [A.alloc(D, BF16) for _ in range(2)]
        rden = [A.alloc(16, F32) for _ in range(2)]
        for ti in range(NTT):
            b = ti % 2
            for g in range(3):
                P.dma("sp", O3[b][g], Os[g][ti * 128:(ti + 1) * 128, :])
            P.dma("sp", xr[b], self.res_src("R2", ti))
            P.tt("dve", O3[b][0], O3[b][0], O3[b][1], ALU.add)
            P.tt("pool", O3[b][0], O3[b][0], O3[b][2], ALU.add)
            o3 = O3[b][0].re("p (h n) -> p h n", n=65)
            P.add("dve", (lambda o_, i_: (lambda e: e.reciprocal(out=o_, in_=i_)))(rden[b].ap.rearrange("p (h o) -> p h o", o=1), o3.ap[:, :, 64:65]),
                  rd=o3.toks, wr=rden[b].toks)
            P.tt("dve", obf[b].re("p (h n) -> p h n", n=64), o3[:, :, 0:64], rden[b].bc(2, [128, 16, 64]), ALU.mult)
            oT3 = oT[b].re("p (k t) -> p k t", k=8)
            self.transpose_bf(obf[b], oT3, 0, self.bank[6])
            for hf in range(2):
                cs = slice(hf * 512, (hf + 1) * 512)
                bk = self.bank[hf + 2 * (ti % 2)]
                for k in range(8):
                    P.mm(bk, oT3[:, k, :], wo3[:, k, cs], start=(k == 0), stop=(k == 7))
                P.tt("dve", xr[b][:, cs], bk, xr[b][:, cs], ALU.add)
            P.dma("sp", self.res_src("R3", ti), xr[b])
        A.release()

    def attn_sample(self, Os, qs, kpad, vpad):
        P, A, NS, NT = self.P, self.A, self.NS, self.NT
        dr = self.dr
        ones = A.alloc(H * 65, F32)
        P.memset("pool", ones, 1.0)
        for g in range(3):
            for stl in range(self.NST):
                r0 = (NT + stl) * 128
                P.dma("sp", Os[g][r0:r0 + 128, :], ones)
        Kt = [A.alloc(D, F32) for _ in range(2)]
        Vt = [A.alloc(D, F32) for _ in range(2)]
        Vb = [A.alloc(H * HS, BF16) for _ in range(2)]
        qbc = [A.alloc(D, F32) for _ in range(2)]
        prod = [A.alloc(D, F32) for _ in range(2)]
        sc = [A.alloc(16, F32) for _ in range(2)]
        pb = [A.alloc(16, BF16) for _ in range(2)]
        Kn = [A.alloc(D, F32) for _ in range(2)]
        Vn = [A.alloc(D, F32) for _ in range(2)]
        Vnb = [A.alloc(H * HS, BF16) for _ in range(2)]
        prodn = A.alloc(D, F32)
        scn = A.alloc(16, F32)
        pnb = [A.alloc(16, BF16) for _ in range(2)]
        orow = [A.alloc(H * 65, F32) for _ in range(2)]
        for v_ in Vb + Vnb:
            P.memset("pool", v_, 1.0)
        o_ge, o_le, o_eq = CF["ge_j"][0], CF["le_j"][0], CF["eq_j"][0]
        HCH = ((0, 7), (7, 14), (14, 16))
        u = 0
        kv = 0
        for s in range(NS):
            for g in range(3):
                d = DILS[g]
                W = WINDOWS[g]
                nb_ = (s * 3 + g) % 2
                rown = NT * 128 + 16 * s + 2
                P.dma("sp", Kn[nb_][0:4, :], kpad[g][16 * s + 2:16 * s + 6, :])
                P.dma("sp", Vn[nb_][0:4, :], vpad[g][16 * s + 2:16 * s + 6, :])
                P.copy("pool", Vnb[nb_][0:4, :].re("p (h n) -> p h n", n=HS)[:, :, 0:64], Vn[nb_][0:4, :].re("p (h n) -> p h n", n=64))
                for j in range(4):
                    if g > 0 or j == 0:
                        kb = kv % 2
                        kv += 1
                        if g == 0:
                            ksrc = dr["ck0"][s]
                            vsrc = dr["cv0"][s]
                        else:
                            ksrc = dr["ck%d" % g][s].re("(l c) n -> c l n", c=d)[j]
                            vsrc = dr["cv%d" % g][s].re("(l c) n -> c l n", c=d)[j]
                        P.dma("sp", Kt[kb], ksrc)
                        P.dma("sp", Vt[kb], vsrc)
                        P.copy("pool", Vb[kb].re("p (h n) -> p h n", n=HS)[:, :, 0:64], Vt[kb].re("p (h n) -> p h n", n=64))
                    ub = u % 2
                    u += 1
                    qrow = qs[g][16 * s + 2 + j:16 * s + 3 + j, :]
                    P.dma("sp", qbc[ub], V(qrow.ap.to_broadcast([128, D]), qrow.toks))
                    P.tt("dve", prod[ub], Kt[kb], qbc[ub], ALU.mult)
                    P.red("dve", sc[ub], prod[ub].re("p (h n) -> p h n", h=16))
                    P.act(pb[ub], sc[ub], AF.Exp, scale=0.125)
                    if g == 0 and j > 0:
                        P.ts("dve", pb[ub], pb[ub], self.cf[:, o_ge + j:o_ge + j + 1], ALU.mult)
                    P.tt("pool", prodn[0:4, :], Kn[nb_][0:4, :], qbc[ub][0:4, :], ALU.mult)
                    P.red("dve", scn[0:4, :], prodn[0:4, :].re("p (h n) -> p h n", h=16))
                    P.act(pnb[ub][0:4, :], scn[0:4, :], AF.Exp, scale=0.125)
                    om = (o_le if g == 0 else o_eq) + j
                    P.ts("dve", pnb[ub][0:4, :], pnb[ub][0:4, :], self.cf[0:4, om:om + 1], ALU.mult)
                    Vb3 = Vb[kb].re("p (h n) -> p h n", n=HS)
                    Vn3 = Vnb[nb_].re("p (h n) -> p h n", n=HS)
                    for ci, (h0, h1) in enumerate(HCH):
                        bk = self.bank[(3 * u + ci) % 8]
                        for h in range(h0, h1):
                            oc = slice((h - h0) * 65, (h - h0 + 1) * 65)
                            P.mm(bk[0:1, oc], pb[ub][:, h:h + 1], Vb3[:, h, 0:65], start=True, stop=False)
                            P.mm(bk[0:1, oc], pnb[ub][0:4, h:h + 1], Vn3[0:4, h, 0:65], start=False, stop=True)
                        P.copy("act", orow[ub][0:1, h0 * 65:h1 * 65], bk[0:1, 0:(h1 - h0) * 65])
                    P.dma("sp", Os[g][rown + j:rown + j + 1, :], orow[ub][0:1, :])


def core_inputs(inp, b, s0, NS, T, consts):
    cf, cb, rope_p = consts
    f = lambda a: np.ascontiguousarray(a, dtype=np.float32)
    m = {}
    m["xp"] = f(inp["x_prompt"][b])
    m["xs"] = f(inp["x_sample"][s0:s0 + NS]).reshape(NS * 4, D)
    m["st_shift"] = f(inp["state_rwkv_shift"][0, s0:s0 + NS])
    m["st_wkv"] = f(inp["state_rwkv_wkv"][0, s0:s0 + NS])
    caches = ((inp["cache_k_w128"], inp["cache_v_w128"]), (inp["cache_k_w512"], inp["cache_v_w512"]),
              (inp["cache_k_w2048"], inp["cache_v_w2048"]))
    for g, (ck, cv) in enumerate(caches):
        m["ck%d" % g] = f(ck[0, s0:s0 + NS]).reshape(NS, -1, D)
        m["cv%d" % g] = f(cv[0, s0:s0 + NS]).reshape(NS, -1, D)
    m["st_conv"] = f(inp["state_ffn_conv"][:, s0:s0 + NS])
    m["norm_mix"] = f(inp["norm_mix"]); m["norm_ffn"] = f(inp["norm_ffn"]); m["norm_final"] = f(inp["norm_final"]).reshape(1, D)
    m["rwkv_mu"] = f(inp["rwkv_mu"][0]); m["rwkv_w_rkv"] = f(inp["rwkv_w_rkv"][0])
    for k in ("w0", "a0", "k_k", "k_a", "ln_w", "ln_b"):
        m["rwkv_" + k] = f(inp["rwkv_" + k][0]).reshape(1, D)
    m["rwkv_r_k"] = f(inp["rwkv_r_k"][0]).reshape(1, D)
    for k in ("w1", "w2", "a1", "a2", "g1", "g2", "w_o"):
        m["rwkv_" + k] = f(inp["rwkv_" + k][0])
    m["attn_w_in"] = f(inp["attn_w_in"][0]); m["attn_w_o"] = f(inp["attn_w_o"][0])
    m["ffn_w_up"] = f(inp["ffn_w_up"]); m["ffn_conv_w"] = f(inp["ffn_conv_w"]); m["ffn_conv_b"] = f(inp["ffn_conv_b"])
    m["ffn_w_down"] = f(inp["ffn_w_down"])
    m["cf"] = cf; m["cb"] = cb; m["rope_p"] = rope_p
    return m


_CACHE = {}


def kernel(**inputs):
    inp = {k: np.asarray(v) for k, v in inputs.items()}
    T = inp["x_prompt"].shape[1]
    NB = inp["x_prompt"].shape[0]
    NSEQ = inp["x_sample"].shape[0]
    NS = NSEQ // NCORES
    key = (T, NS)
    if key not in _CACHE:
        _CACHE[key] = (Builder(T, NS).build(), host_consts(T))
    nc, consts = _CACHE[key]
    in_maps = [core_inputs(inp, c % NB, c * NS, NS, T, consts) for c in range(NCORES)]
    res = run_bass_kernel_spmd(nc, in_maps, core_ids=list(range(NCORES)))
    R = res.results
    f = np.float32
    pr = lambda name, shape: np.stack([np.asarray(R[b][name], f).reshape(shape) for b in range(NB)])
    sm = lambda name, shape: np.concatenate([np.asarray(R[c][name], f).reshape(shape) for c in range(NCORES)], 0)
    outs = [pr("y_p", (T, D)), sm("y_s", (NS, 4, D)),
            pr("p_shift", (D,))[None], pr("p_wkv", (H, DH, DH))[None]]
    for g, w in enumerate(WINDOWS):
        kp = min(w, T)
        outs.append(pr("p_k%d" % g, (kp, H, DH))[None])
        outs.append(pr("p_v%d" % g, (kp, H, DH))[None])
    outs.append(np.stack([np.asarray(R[b]["p_conv"], f).reshape(2, 2, FF) for b in range(NB)], 1))
    outs.append(sm("s_shift", (NS, D))[None])
    outs.append(sm("s_wkv", (NS, H, DH, DH))[None])
    for g in range(3):
        outs.append(sm("s_k%d" % g, (NS, 4, H, DH))[None])
        outs.append(sm("s_v%d" % g, (NS, 4, H, DH))[None])
    outs.append(np.concatenate([np.asarray(R[c]["s_conv"], f).reshape(2, NS, 2, FF) for c in range(NCORES)], 1))
    return tuple(outs)
```

```python
import math
import os
import numpy as np
import ml_dtypes
from contextlib import ExitStack
import concourse.bass as bass
import concourse.mybir as mybir
from concourse.bass_utils import run_bass_kernel_spmd

F32 = mybir.dt.float32
BF16 = mybir.dt.bfloat16
ALU = mybir.AluOpType
AF = mybir.ActivationFunctionType
AX = mybir.AxisListType

D = 1024
H = 16
DH = 64
FF = 2816
NFC = FF // 128
NGRP = 3
WINDOWS = (128, 512, 2048)
DILS = (1, 4, 16)
PAST = 2048
NCORES = 8
HS = 68
EXPM05 = math.exp(-0.5)
DEBUG_SITES = False
SITES = {}


class Tok:
    __slots__ = ("w", "r")

    def __init__(self):
        self.w = None
        self.r = []


class Op:
    __slots__ = ("eng", "fn", "deps", "need_inc", "cnt", "is_dma", "dsem", "dval", "prev_dval", "site",
                 "sdeps", "cost", "idx", "succ", "indeg", "ready", "fin")


class V:
    def __init__(self, ap, toks=None):
        self.ap = ap
        self.toks = toks if toks is not None else [Tok()]

    def __getitem__(self, k):
        return V(self.ap[k], self.toks)

    def re(self, pat, **kw):
        return V(self.ap.rearrange(pat, **kw), self.toks)

    def bc(self, axis, shape):
        return V(self.ap.unsqueeze(axis).to_broadcast(list(shape)), self.toks)

    def bitcast(self, dt):
        return V(self.ap.bitcast(dt), self.toks)


def _ap(x):
    return x.ap if isinstance(x, V) else x


def _fsz(x):
    sh = _ap(x).shape
    n = 1
    for d_ in sh[1:]:
        n *= int(d_)
    return n


def _toks(*xs):
    r = []
    for x in xs:
        if isinstance(x, V):
            r.extend(x.toks)
    return r


class Prog:
    ENGS = ("pe", "act", "dve", "pool", "sp")
    NDSEM = 14

    def __init__(self, nc, stack):
        self.nc = nc
        self.ops = {e: [] for e in self.ENGS}
        self.esem = {e: stack.enter_context(nc.semaphore("es_" + e)) for e in ("pe", "act", "dve", "pool")}
        self.dsems = {e: [stack.enter_context(nc.semaphore("ds_%s%d" % (e, i))) for i in range(self.NDSEM)]
                      for e in ("sp", "act", "pool")}
        self.ndma = {e: 0 for e in ("sp", "act", "pool")}
        self.regions = []
        self.cur = []
        self.nins = 0

    def add(self, eng, fn, rd=(), wr=(), dma=False, extra=(), cost=100.0):
        o = Op()
        o.eng = eng
        o.fn = fn
        o.site = None
        if DEBUG_SITES:
            import traceback
            o.site = [f.lineno for f in traceback.extract_stack(limit=5)[:-1]]
        o.need_inc = False
        o.cnt = 0
        o.is_dma = dma
        o.cost = cost
        o.idx = self.nins
        deps = list(extra)
        for t in rd:
            if t.w is not None:
                deps.append(t.w)
        for t in wr:
            if t.w is not None:
                deps.append(t.w)
            deps.extend(t.r)
        dd = []
        sd = []
        seen = set()
        for d in deps:
            if id(d) in seen or d is o:
                continue
            seen.add(id(d))
            sd.append(d)
            if (not d.is_dma) and d.eng == "pe" and eng == "pe" and not dma:
                continue
            dd.append(d)
        o.deps = dd
        o.sdeps = sd
        for t in rd:
            t.r.append(o)
        for t in wr:
            t.w = o
            t.r = []
        self.cur.append(o)
        self.nins += 1
        return o

    def barrier(self):
        if self.cur:
            self.regions.append(self.cur)
            self.cur = []

    def mm(self, out, lhsT, rhs, start=True, stop=True, extra_rd=()):
        o, l, r = _ap(out), _ap(lhsT), _ap(rhs)
        c_ = max(_fsz(rhs), 64) / 1.6 + 20.0
        if l.dtype == F32:
            c_ *= 4.0
        return self.add("pe", lambda e: e.matmul(o, lhsT=l, rhs=r, start=start, stop=stop),
                        rd=_toks(lhsT, rhs) + list(extra_rd), wr=_toks(out), cost=c_)

    def tr(self, out, in_, ident):
        o, i, d = _ap(out), _ap(in_), _ap(ident)
        c_ = 110.0 * (4.0 if i.dtype == F32 else 1.0)
        return self.add("pe", lambda e: e.transpose(out=o, in_=i, identity=d), rd=_toks(in_, ident), wr=_toks(out), cost=c_)

    def tt(self, eng, out, in0, in1, op):
        o, a, b = _ap(out), _ap(in0), _ap(in1)
        return self.add(eng, lambda e: e.tensor_tensor(out=o, in0=a, in1=b, op=op), rd=_toks(in0, in1), wr=_toks(out),
                        cost=self.ecost(eng, out))

    def ts(self, eng, out, in0, s1, op0, s2=None, op1=None):
        o, a = _ap(out), _ap(in0)
        s1a, s2a = _ap(s1), _ap(s2)
        if op1 is None:
            fn = lambda e: e.tensor_scalar(out=o, in0=a, scalar1=s1a, scalar2=None, op0=op0)
        else:
            fn = lambda e: e.tensor_scalar(out=o, in0=a, scalar1=s1a, scalar2=s2a, op0=op0, op1=op1)
        return self.add(eng, fn, rd=_toks(in0, s1, s2), wr=_toks(out), cost=self.ecost(eng, out))

    def stt(self, eng, out, in0, scalar, in1, op0, op1):
        o, a, s, b = _ap(out), _ap(in0), _ap(scalar), _ap(in1)
        return self.add(eng, lambda e: e.scalar_tensor_tensor(out=o, in0=a, scalar=s, in1=b, op0=op0, op1=op1),
                        rd=_toks(in0, scalar, in1), wr=_toks(out), cost=self.ecost(eng, out))

    def copy(self, eng, out, in_):
        o, i = _ap(out), _ap(in_)
        if eng == "act":
            fn = lambda e: e.copy(out=o, in_=i)
        else:
            fn = lambda e: e.tensor_copy(out=o, in_=i)
        return self.add(eng, fn, rd=_toks(in_), wr=_toks(out), cost=self.ecost(eng, out))

    def ecost(self, eng, out):
        n = _fsz(out)
        if eng == "act":
            return 230.0 + 0.85 * n
        if eng == "pool":
            return 120.0 + 2.0 * n
        return 70.0 + 1.05 * n

    def act(self, out, in_, func, scale=1.0, bias=None, accum=None):
        o, i, b, ac = _ap(out), _ap(in_), _ap(bias), _ap(accum)
        kw = {}
        if bias is not None:
            kw["bias"] = b
        if accum is not None:
            kw["accum_out"] = ac
        sc = _ap(scale)
        return self.add("act", lambda e: e.activation(out=o, in_=i, func=func, scale=sc, **kw),
                        rd=_toks(in_, bias, scale), wr=_toks(out, accum), cost=self.ecost("act", out))

    def red(self, eng, out, in_, op=None, axis=None):
        o, i = _ap(out), _ap(in_)
        op = op or ALU.add
        axis = axis or AX.X
        return self.add(eng, lambda e: e.tensor_reduce(out=o, in_=i, axis=axis, op=op), rd=_toks(in_), wr=_toks(out),
                        cost=self.ecost(eng, in_))

    def memset(self, eng, out, val):
        o = _ap(out)
        return self.add(eng, lambda e: e.memset(o, val), wr=_toks(out), cost=self.ecost(eng, out))

    def dma(self, q, out, in_, **kw):
        o, i = _ap(out), _ap(in_)
        nb_ = _fsz(out) * int(o.shape[0]) * (4 if o.dtype == F32 else 2)
        return self.add(q, lambda e: e.dma_start(out=o, in_=i, **kw), rd=_toks(in_), wr=_toks(out), dma=True,
                        cost=2200.0 + nb_ / 150.0)

    LAT = 300.0

    def schedule_region(self, ops):
        engs = self.ENGS
        pend = {e: [] for e in engs}
        inreg = set(id(o) for o in ops)
        for o in ops:
            o.succ = []
            o.indeg = 0
            o.ready = 0.0
        for o in ops:
            for d in o.sdeps:
                if id(d) in inreg:
                    d.succ.append(o)
                    o.indeg += 1
            pend[o.eng].append(o)
        free = {e: 0.0 for e in engs}
        out = {e: [] for e in engs}
        n = len(ops)
        W = int(os.environ.get("BASS_W", "40"))
        LAT = self.LAT
        while n > 0:
            bst = None
            bo = None
            be = None
            bi = 0
            for e in engs:
                lst = pend[e]
                if not lst:
                    continue
                fe = free[e]
                lim = W if len(lst) > W else len(lst)
                for i in range(lim):
                    o = lst[i]
                    if o.indeg == 0:
                        st = o.ready if o.ready > fe else fe
                        if bo is None or st < bst or (st == bst and o.idx < bo.idx):
                            bst, bo, be, bi = st, o, e, i
                        if st <= fe:
                            break
            o = bo
            del pend[be][bi]
            out[be].append(o)
            if o.is_dma:
                free[be] = bst + 120.0
                fin = bst + o.cost
            elif be == "pe":
                free[be] = bst + o.cost
                fin = bst + o.cost + 120.0
            else:
                free[be] = bst + o.cost
                fin = free[be]
            o.fin = (bst, fin, o.ready >= bst - 1e-6)
            for s_ in o.succ:
                s_.indeg -= 1
                r = fin if (s_.eng == be and not o.is_dma) else fin + LAT
                if r > s_.ready:
                    s_.ready = r
                    s_.dsem = o
            n -= 1
        if os.environ.get("BASS_SCHED_STATS"):
            busy = {e: sum(o.cost if not o.is_dma else 120.0 for o in out[e]) for e in engs}
            print("SCHED region: n=%d makespan=%.0f us busy(us)=%s" % (len(ops), max(free.values()) / 1e3, {e: round(v / 1e3) for e, v in busy.items()}))
        return out

    def finalize(self):
        if self.cur:
            self.regions.append(self.cur)
            self.cur = []
        sched = os.environ.get("BASS_NOSCHED", "") == ""
        for ops in self.regions:
            if sched:
                out = self.schedule_region(ops)
            else:
                out = {e: [o for o in ops if o.eng == e] for e in self.ENGS}
            last = {}
            dmas = []
            for e in self.ENGS:
                for o in out[e]:
                    if o.is_dma:
                        dmas.append(o)
                    else:
                        last[e] = o
                self.ops[e].extend(out[e])
            for e in self.ENGS:
                bo = Op()
                bo.eng = e
                bo.fn = None
                bo.site = None
                bo.need_inc = False
                bo.cnt = 0
                bo.is_dma = False
                bo.deps = [o for k, o in last.items() if k != e] + dmas
                self.ops[e].append(bo)
        for e in self.ndma:
            i = 0
            for o in self.ops[e]:
                if o.is_dma:
                    o.dsem = self.dsems[e][i % self.NDSEM]
                    o.dval = 16 * (i // self.NDSEM + 1)
                    o.prev_dval = o.dval - 16
                    i += 1
            self.ndma[e] = i
        for e in self.ENGS:
            for o in self.ops[e]:
                for d in o.deps:
                    if not d.is_dma:
                        d.need_inc = True

    def emit(self):
        nc = self.nc
        self.finalize()
        for e in ("pe", "act", "dve", "pool"):
            c = 0
            for o in self.ops[e]:
                if o.is_dma or o.fn is None:
                    o.cnt = c
                    continue
                if o.need_inc:
                    c += 1
                o.cnt = c
        prog = self

        def run(e, eng):
            known = {}
            for o in prog.ops[e]:
                waits = []
                for d in o.deps:
                    if d.is_dma:
                        waits.append((d.dsem, d.dval))
                    else:
                        waits.append((prog.esem[d.eng], d.cnt))
                if o.is_dma and o.prev_dval > 0:
                    waits.append((o.dsem, o.prev_dval))
                for (s, v) in waits:
                    k = id(s)
                    if known.get(k, 0) < v:
                        eng.wait_ge(s, v)
                        known[k] = v
                if o.fn is None:
                    continue
                ins = o.fn(eng)
                if DEBUG_SITES:
                    SITES[str(ins.ins.name)] = o.site
                if o.is_dma:
                    ins.then_inc(o.dsem, 16)
                elif o.need_inc:
                    ins.then_inc(prog.esem[e], 1)
            if e in prog.ndma:
                n = prog.ndma[e]
                for i in range(min(n, prog.NDSEM)):
                    uses = (n - 1 - i) // prog.NDSEM + 1
                    s = prog.dsems[e][i]
                    if known.get(id(s), 0) < 16 * uses:
                        eng.wait_ge(s, 16 * uses)

        with nc.Block() as block:
            @block.sync
            def _(eng):
                run("sp", eng)

            @block.tensor
            def _(eng):
                run("pe", eng)

            @block.scalar
            def _(eng):
                run("act", eng)

            @block.vector
            def _(eng):
                run("dve", eng)

            @block.gpsimd
            def _(eng):
                run("pool", eng)


class Arena:
    def __init__(self, big, nbytes):
        self.big = big
        self.nbytes = nbytes
        self.off = 0
        self.marks = []

    def mark(self):
        self.marks.append(self.off)

    def release(self):
        self.off = self.marks.pop()

    def alloc(self, cols, dt, parts=128):
        esz = 4 if dt == F32 else 2
        self.off = (self.off + 63) // 64 * 64
        nb = cols * esz
        assert self.off + nb <= self.nbytes, ("arena overflow", self.off, nb, self.nbytes)
        ap = self.big[0:parts, self.off // 2:(self.off + nb) // 2]
        if dt == F32:
            ap = ap.bitcast(F32)
        self.off += nb
        return V(ap)


CF = {}
CB = {}


def _layout(spec, table):
    off = 0
    for name, n in spec:
        table[name] = (off, n)
        off += n
    return off


_CF_SPEC = [("ident", 128), ("m_sl", 128), ("m_su", 128), ("m_ui", 128), ("m_blk", 128), ("maskc", 8),
            ("negc_p", 1), ("negc_s", 1), ("rowm_s", 1), ("one", 1), ("eps6", 1), ("eps24", 1), ("gneps", 1),
            ("rope_s", 128), ("ge_j", 4), ("le_j", 4), ("eq_j", 4)]
_CB_SPEC = [("ident", 128), ("am_ge", 128), ("am_le", 128)]
NCF = _layout(_CF_SPEC, CF)
NCB = _layout(_CB_SPEC, CB)


def host_consts(T):
    NT = T // 128
    p = np.arange(128)
    row = p[:, None]
    col = p[None, :]
    same = (row // 16) == (col // 16)
    cf = np.zeros((128, NCF), np.float32)

    def put(name, arr):
        o, n = CF[name]
        cf[:, o:o + n] = arr

    put("ident", np.eye(128))
    put("m_sl", same & (col < row))
    put("m_su", same & (row < col))
    put("m_ui", same & (row <= col))
    put("m_blk", same)
    put("maskc", (row // 16) == np.arange(8)[None, :])
    slot = p % 16
    rowm = ((slot >= 2) & (slot < 6)).astype(np.float32)[:, None]
    put("negc_p", -EXPM05)
    put("negc_s", -EXPM05 * rowm)
    put("rowm_s", rowm)
    put("one", 1.0)
    put("eps6", 1e-6)
    put("eps24", 1e-24)
    put("gneps", 64e-5)
    jj = np.arange(4)[None, :]
    put("ge_j", row >= jj)
    put("le_j", row <= jj)
    put("eq_j", row == jj)
    half = DH // 2
    inv = np.exp(-math.log(10000.0) * np.arange(half, dtype=np.float32) * 2.0 / DH).astype(np.float32)
    pos_s = (PAST + np.clip(slot - 2, 0, 3)).astype(np.float32)
    ang = pos_s[:, None] * inv[None, :]
    put("rope_s", np.concatenate([np.cos(ang), np.cos(ang), np.sin(ang), np.sin(ang)], 1))
    pos = (np.arange(NT)[None, :] * 128 + p[:, None]).astype(np.float32)
    angp = pos[:, :, None] * inv[None, None, :]
    rope_p = np.concatenate([np.cos(angp), np.cos(angp), np.sin(angp), np.sin(angp)], 2).astype(np.float32).reshape(128, NT * 128)
    cb = np.zeros((128, NCB), np.float32)

    def putb(name, arr):
        o, n = CB[name]
        cb[:, o:o + n] = arr

    putb("ident", np.eye(128))
    putb("am_ge", row >= col)
    putb("am_le", row <= col)
    return cf, cb.astype(ml_dtypes.bfloat16), rope_p


class Builder:
    def __init__(self, T, NS=16, phases="ABCD", debug=False):
        self.T = T
        self.NS = NS
        self.NT = T // 128
        self.NST = NS * 16 // 128
        self.NTT = self.NT + self.NST
        self.phases = phases
        self.debug = debug
        self.nc = bass.Bass("TRN2", target_bir_lowering=False)
        self.dr = {}

    def din(self, name, shape, dt=F32):
        self.dr[name] = V(self.nc.dram_tensor(name, list(shape), dt, kind="ExternalInput").ap())
        return self.dr[name]

    def dout(self, name, shape):
        self.dr[name] = V(self.nc.dram_tensor(name, list(shape), F32, kind="ExternalOutput").ap())
        return self.dr[name]

    def dscr(self, name, shape, dt=F32, dbg=False):
        if dbg and self.debug:
            return self.dout(name, shape)
        self.dr[name] = V(self.nc.dram_tensor(name, list(shape), dt).ap())
        return self.dr[name]

    def declare(self):
        T, NS, NTT = self.T, self.NS, self.NTT
        d = self.din
        d("xp", [T, D]); d("xs", [NS * 4, D]); d("st_shift", [NS, D]); d("st_wkv", [NS, H, DH, DH])
        for g, w in enumerate(WINDOWS):
            d("ck%d" % g, [NS, w, D]); d("cv%d" % g, [NS, w, D])
        d("st_conv", [2, NS, 2, FF])
        d("norm_mix", [2, D]); d("norm_ffn", [2, D]); d("norm_final", [1, D])
        d("rwkv_mu", [6, D]); d("rwkv_w_rkv", [3, D, D]); d("rwkv_w0", [1, D]); d("rwkv_w1", [D, 64]); d("rwkv_w2", [64, D])
        d("rwkv_a0", [1, D]); d("rwkv_a1", [D, 64]); d("rwkv_a2", [64, D]); d("rwkv_g1", [D, 128]); d("rwkv_g2", [128, D])
        d("rwkv_k_k", [1, D]); d("rwkv_k_a", [1, D]); d("rwkv_r_k", [1, D]); d("rwkv_ln_w", [1, D]); d("rwkv_ln_b", [1, D])
        d("rwkv_w_o", [D, D]); d("attn_w_in", [D, 9 * D]); d("attn_w_o", [D, D])
        d("ffn_w_up", [2, D, 2 * FF]); d("ffn_conv_w", [2, 3, FF]); d("ffn_conv_b", [2, FF]); d("ffn_w_down", [2, FF, D])
        d("cf", [128, NCF]); d("cb", [128, NCB], BF16); d("rope_p", [128, self.NT * 128])
        o = self.dout
        o("y_p", [T, D]); o("y_s", [NS * 4, D]); o("p_shift", [1, D]); o("p_wkv", [H, DH, DH])
        for g, w in enumerate(WINDOWS):
            o("p_k%d" % g, [min(w, T), D]); o("p_v%d" % g, [min(w, T), D])
        o("p_conv", [2, 2, FF]); o("s_shift", [NS, D]); o("s_wkv", [NS, H, DH, DH])
        for g in range(3):
            o("s_k%d" % g, [NS * 4, D]); o("s_v%d" % g, [NS * 4, D])
        o("s_conv", [2, NS, 2, FF])
        s = self.dscr
        s("xs_pad", [self.NST * 128, D])
        for nm, prod, cons in (("R1", "A", "B"), ("R2", "B", "C"), ("R3", "C", "D")):
            if prod in self.phases:
                s(nm, [NTT * 128, D], dbg=True)
            elif cons in self.phases:
                d(nm, [NTT * 128, D])
            else:
                s(nm, [NTT * 128, D])
        s("ys_pad", [self.NST * 128, D])
        if self.debug:
            o("dbg_y", [NTT * 128, D]); o("dbg_g", [NTT * 128, D]); o("dbg_yn", [NTT * 128, D])

    def res_src(self, name, i):
        if name == "R0":
            if i < self.NT:
                return self.dr["xp"][i * 128:(i + 1) * 128, :]
            return self.dr["xs_pad"][(i - self.NT) * 128:(i - self.NT + 1) * 128, :]
        return self.dr[name][i * 128:(i + 1) * 128, :]

    def build(self):
        nc = self.nc
        self.declare()
        with ExitStack() as st:
            self.P = P = Prog(nc, st)
            big = st.enter_context(nc.sbuf_tensor("arena", [128, 106000], BF16))
            self.A = A = Arena(big, 212000)
            pp = [st.enter_context(nc.psum_tensor("pp%d" % i, [128, 1024], F32)) for i in range(4)]
            self.bank = []
            for i in range(4):
                for h in range(2):
                    self.bank.append(V(pp[i][:, h * 512:(h + 1) * 512]))
            self.dbank = [V(pp[i][:, :], toks=self.bank[2 * i].toks + self.bank[2 * i + 1].toks) for i in range(4)]
            self.cf = A.alloc(NCF, F32)
            self.cb = A.alloc(NCB, BF16)
            P.dma("sp", self.cf, self.dr["cf"])
            P.dma("sp", self.cb, self.dr["cb"])
            self.zero = A.alloc(D, F32)
            P.memset("pool", self.zero, 0.0)
            for stl in range(self.NST):
                P.dma("sp", self.dr["xs_pad"][stl * 128:(stl + 1) * 128, :], self.zero)
            P.dma("sp", self.dr["xs_pad"].re("(s q) d -> s q d", q=16)[:, 2:6, :],
                  self.dr["xs"].re("(s j) d -> s j d", j=4))
            P.barrier()
            if "A" in self.phases:
                A.mark(); self.phase_rwkv(); A.release(); P.barrier()
            if "B" in self.phases:
                A.mark(); self.phase_ffn(0, "R1", "R2", final=False); A.release(); P.barrier()
            if "C" in self.phases:
                A.mark(); self.phase_attn(); A.release(); P.barrier()
            if "D" in self.phases:
                A.mark(); self.phase_ffn(1, "R3", None, final=True); A.release(); P.barrier()
            P.emit()
        return nc

    def c32(self, name):
        o, n = CF[name]
        return self.cf[:, o:o + n]

    def c16(self, name):
        o, n = CB[name]
        return self.cb[:, o:o + n]

    def load_bcast(self, dst, src_row):
        n = src_row.ap.shape[-1]
        self.P.dma("sp", dst, V(src_row.ap.to_broadcast([128, n]), src_row.toks))

    def rmsnorm(self, x, gbc, out, junk, small):
        P = self.P
        P.act(junk, x, AF.Square)
        P.red("dve", small[:, 0:1], junk)
        P.act(small[:, 1:2], small[:, 0:1], AF.Ln, scale=1.0 / D, bias=self.c32("eps6"))
        P.act(small[:, 2:3], small[:, 1:2], AF.Exp, scale=-0.5)
        P.stt("dve", out, x, small[:, 2:3], gbc, ALU.mult, ALU.mult)

    def transpose_bf(self, src, dstT, col0, bank):
        P = self.P
        pb = bank.bitcast(BF16)
        for kc in range(8):
            P.tr(pb[:, kc * 128:(kc + 1) * 128], src[:, kc * 128:(kc + 1) * 128], self.c16("ident"))
        P.copy("act", dstT[:, :, col0:col0 + 128], pb.re("p (k t) -> p k t", k=8))

    def phase_ffn(self, layer, rin, rout, final):
        P, A, NT, NST = self.P, self.A, self.NT, self.NST
        dr = self.dr
        wup = A.alloc(8 * 2 * FF, BF16)
        wdn = A.alloc(NFC * D, BF16)
        wup3 = wup.re("p (k n) -> p k n", k=8)
        wdn3 = wdn.re("p (k n) -> p k n", k=NFC)
        for k in range(8):
            P.dma("pool", wup3[:, k, :], dr["ffn_w_up"][layer, k * 128:(k + 1) * 128, :])
        for k0 in range(0, NFC, 6):
            k1 = min(NFC, k0 + 6)
            P.dma("pool", wdn3[:, k0:k1, :], dr["ffn_w_down"][layer, k0 * 128:k1 * 128, :].re("(k p) n -> p k n", p=128))
        import os
        STOP = os.environ.get("FFN_STOP", "")
        if STOP == "w":
            return
        gbc = A.alloc(D, F32)
        self.load_bcast(gbc, dr["norm_ffn"][layer:layer + 1, :])
        gfin = None
        if final:
            gfin = A.alloc(D, F32)
            self.load_bcast(gfin, dr["norm_final"][0:1, :])
        if STOP == "bc":
            return
        cw = A.alloc(NFC * 4, F32)
        cw3 = cw.re("p (c j) -> p c j", j=4)
        tmpF = A.alloc(512, F32)
        NSJ = self.NS * 2
        stT = A.alloc(NFC * NSJ, F32)
        stT3 = stT.re("p (c n) -> p c n", c=NFC)
        bk = self.bank[0]
        bk2 = self.bank[1]
        for c0 in range(0, NFC, 4):
            c1 = min(NFC, c0 + 4)
            w = (c1 - c0) * 128
            P.dma("sp", tmpF[0:3, 0:w], dr["ffn_conv_w"][layer, :, c0 * 128:c1 * 128])
            P.dma("sp", tmpF[3:4, 0:w], dr["ffn_conv_b"][layer:layer + 1, c0 * 128:c1 * 128])
            for c in range(c0, c1):
                P.tr(bk[:, c * 4:(c + 1) * 4], tmpF[0:4, (c - c0) * 128:(c - c0 + 1) * 128], self.c32("ident")[0:4, 0:4])
        P.copy("act", cw, bk[:, 0:NFC * 4])
        if STOP == "cw":
            return
        for c0 in range(0, NFC, 4):
            c1 = min(NFC, c0 + 4)
            w = (c1 - c0) * 128
            P.dma("sp", tmpF[0:NSJ, 0:w], dr["st_conv"][layer].re("s j f -> (s j) f")[:, c0 * 128:c1 * 128])
            for c in range(c0, c1):
                P.tr(bk2[:, (c - c0) * NSJ:(c - c0 + 1) * NSJ], tmpF[0:NSJ, (c - c0) * 128:(c - c0 + 1) * 128], self.c32("ident")[0:NSJ, 0:NSJ])
            P.copy("act", stT3[:, c0:c1, :], bk2[:, 0:(c1 - c0) * NSJ].re("p (c n) -> p c n", n=NSJ))
        if STOP == "st":
            return
        carry = A.alloc(NFC * 2, F32)
        carry3 = carry.re("p (c j) -> p c j", j=2)
        P.memset("pool", carry, 0.0)
        gsave = A.alloc(NFC * (NSJ + 2), F32)
        gsave3 = gsave.re("p (c n) -> p c n", c=NFC)
        xnT = A.alloc(8 * 512, BF16)
        xnT3 = xnT.re("p (k t) -> p k t", k=8)
        hT = A.alloc(NFC * 512, BF16)
        hT3 = hT.re("p (c t) -> p c t", c=NFC)
        xr = [A.alloc(D, F32) for _ in range(2)]
        xn = [A.alloc(D, BF16) for _ in range(2)]
        junk = hT[:, 0:2 * D].bitcast(F32)
        small = [A.alloc(4, F32) for _ in range(2)]
        gext = [A.alloc(514, F32) for _ in range(2)]
        acc = [A.alloc(512, F32) for _ in range(2)]
        yt = xr
        ysm = [A.alloc(4, F32) for _ in range(2)]
        macros = []
        i = 0
        while i < NT:
            n = min(4, NT - i)
            macros.append((i, n, False))
            i += n
        macros.append((NT, NST, True))
        cnt = 0
        for (t0, n, is_s) in macros:
            ntok = n * 128
            for j in range(n):
                b = cnt % 2
                cnt += 1
                P.dma("sp", xr[b], self.res_src(rin, t0 + j))
                if STOP == "L0":
                    continue
                self.rmsnorm(xr[b], gbc, xn[b], junk, small[b])
                if STOP == "L1":
                    continue
                self.transpose_bf(xn[b], xnT3, j * 128, self.bank[1])
            if STOP in ("L0", "L1", "L2"):
                continue
            for c in range(NFC):
                b = c % 2
                pg = self.bank[2 + b]
                pv = self.bank[4 + b]
                for k in range(8):
                    P.mm(pg[:, 0:ntok], wup3[:, k, c * 128:(c + 1) * 128], xnT3[:, k, 0:ntok], start=(k == 0), stop=(k == 7))
                for k in range(8):
                    P.mm(pv[:, 0:ntok], wup3[:, k, FF + c * 128:FF + (c + 1) * 128], xnT3[:, k, 0:ntok], start=(k == 0), stop=(k == 7))
                ge = gext[b]
                if STOP == "L3a":
                    continue
                P.copy("pool", ge[:, 0:2], carry3[:, c, :])
                P.copy("act", ge[:, 2:2 + ntok], pg[:, 0:ntok])
                if STOP == "L3":
                    continue
                if is_s:
                    P.copy("pool", ge[:, 2:2 + ntok].re("p (s q) -> p s q", q=16)[:, :, 0:2],
                           stT3[:, c, :].re("p (s j) -> p s j", j=2))
                a = acc[b]
                if STOP == "L3b":
                    continue
                P.ts("dve", a[:, 0:ntok], ge[:, 2:2 + ntok], cw3[:, c, 2:3], ALU.mult, cw3[:, c, 3:4], ALU.add)
                if STOP == "L3c":
                    continue
                P.stt("dve", a[:, 0:ntok], ge[:, 1:1 + ntok], cw3[:, c, 1:2], a[:, 0:ntok], ALU.mult, ALU.add)
                P.stt("dve", a[:, 0:ntok], ge[:, 0:ntok], cw3[:, c, 0:1], a[:, 0:ntok], ALU.mult, ALU.add)
                if STOP == "L4":
                    continue
                P.act(a[:, 0:ntok], a[:, 0:ntok], AF.Silu)
                P.tt("dve", hT3[:, c, 0:ntok], a[:, 0:ntok], pv[:, 0:ntok], ALU.mult)
                if is_s:
                    P.copy("pool", gsave3[:, c, 0:NSJ].re("p (s j) -> p s j", j=2),
                           ge[:, 2:2 + ntok].re("p (s q) -> p s q", q=16)[:, :, 4:6])
                else:
                    P.copy("pool", carry3[:, c, :], ge[:, ntok:ntok + 2])
                    if t0 + n == NT:
                        P.copy("pool", gsave3[:, c, NSJ:NSJ + 2], ge[:, ntok:ntok + 2])
            if STOP in ("L3a", "L3", "L3b", "L3c", "L4", "L6"):
                continue
            for j in range(n):
                b = cnt % 2
                cnt += 1
                ti = t0 + j
                P.dma("sp", xr[b], self.res_src(rin, ti))
                for hh in range(2):
                    pd = self.bank[6 + hh]
                    for c in range(NFC):
                        P.mm(pd, hT3[:, c, j * 128:(j + 1) * 128], wdn3[:, c, hh * 512:(hh + 1) * 512], start=(c == 0), stop=(c == NFC - 1))
                    P.tt("dve", yt[b][:, hh * 512:(hh + 1) * 512], pd, xr[b][:, hh * 512:(hh + 1) * 512], ALU.add)
                if not final:
                    P.dma("sp", self.res_src(rout, ti), yt[b])
                else:
                    self.rmsnorm(yt[b], gfin, yt[b], xnT[:, 0:2 * D].bitcast(F32), ysm[b])
                    if ti < NT:
                        P.dma("sp", dr["y_p"][ti * 128:(ti + 1) * 128, :], yt[b])
                    else:
                        P.dma("sp", dr["ys_pad"][(ti - NT) * 128:(ti - NT + 1) * 128, :], yt[b])
        if STOP == "main":
            return
        if final:
            P.dma("sp", dr["y_s"].re("(s j) d -> s j d", j=4), dr["ys_pad"].re("(s q) d -> s q d", q=16)[:, 2:6, :])
        NG = NSJ + 2
        sc_rows = dr["s_conv"][layer].re("s j f -> (s j) f")
        for c0 in range(0, NFC, 4):
            c1 = min(NFC, c0 + 4)
            w = (c1 - c0) * 128
            bk = self.bank[0]
            for c in range(c0, c1):
                P.tr(bk[0:NG, (c - c0) * 128:(c - c0 + 1) * 128], gsave3[:, c, :], self.c32("ident"))
            P.copy("act", tmpF[0:NG, 0:w], bk[0:NG, 0:w])
            P.dma("sp", sc_rows[:, c0 * 128:c1 * 128], tmpF[0:NSJ, 0:w])
            P.dma("sp", dr["p_conv"][layer][:, c0 * 128:c1 * 128], tmpF[NSJ:NSJ + 2, 0:w])


    def nb(self):
        self._nb = (getattr(self, "_nb", -1) + 1) % 6
        return self.bank[self._nb]

    def phase_rwkv(self):
        P, A, NT, NST, NS = self.P, self.A, self.NT, self.NST, self.NS
        dr = self.dr
        ident = self.c32("ident")
        identb = self.c16("ident")
        m_sl, m_su, m_ui, m_blk, maskc = (self.c32(k) for k in ("m_sl", "m_su", "m_ui", "m_blk", "maskc"))
        o_su = CF["m_su"][0]
        mask2 = self.cf[:, o_su:o_su + 256]
        wrkv = A.alloc(3 * 8 * D, BF16); wrkv4 = wrkv.re("p (q k n) -> p q k n", q=3, k=8)
        wo = A.alloc(8 * D, BF16); wo3 = wo.re("p (k n) -> p k n", k=8)
        w1 = A.alloc(8 * 64, BF16); w13 = w1.re("p (k n) -> p k n", k=8)
        a1 = A.alloc(8 * 64, BF16); a13 = a1.re("p (k n) -> p k n", k=8)
        g1 = A.alloc(8 * 128, BF16); g13 = g1.re("p (k n) -> p k n", k=8)
        w2 = A.alloc(D, BF16); a2 = A.alloc(D, BF16); g2 = A.alloc(D, BF16)
        for q in range(3):
            P.dma("pool", wrkv4[:, q, :, :], dr["rwkv_w_rkv"][q].re("(k p) n -> p k n", p=128))
        P.dma("pool", wo3, dr["rwkv_w_o"].re("(k p) n -> p k n", p=128))
        P.dma("pool", w13, dr["rwkv_w1"].re("(k p) n -> p k n", p=128))
        P.dma("pool", a13, dr["rwkv_a1"].re("(k p) n -> p k n", p=128))
        P.dma("pool", g13, dr["rwkv_g1"].re("(k p) n -> p k n", p=128))
        P.dma("pool", w2[0:64, :], dr["rwkv_w2"])
        P.dma("pool", a2[0:64, :], dr["rwkv_a2"])
        P.dma("pool", g2, dr["rwkv_g2"])
        bc = {}
        for nm, src in (("gmix", dr["norm_mix"][0:1, :]), ("w0", dr["rwkv_w0"]), ("a0", dr["rwkv_a0"]), ("kk", dr["rwkv_k_k"]),
                        ("ka", dr["rwkv_k_a"]), ("rk", dr["rwkv_r_k"]), ("lnw", dr["rwkv_ln_w"]), ("lnb", dr["rwkv_ln_b"])):
            bc[nm] = A.alloc(D, F32)
            self.load_bcast(bc[nm], src)
        FW = 8 * 129
        Ft = [A.alloc(FW, F32) for _ in range(8)]
        F = [f[:, 0:D] for f in Ft]
        mu_sb = A.alloc(48, F32); mu3 = mu_sb.re("p (q k) -> p q k", q=6)
        P.dma("sp", F[0][0:6, :], dr["rwkv_mu"])
        bk = self.nb()
        for k in range(8):
            P.tr(bk[:, k * 6:(k + 1) * 6], F[0][0:6, k * 128:(k + 1) * 128], ident[0:6, 0:6])
        P.copy("act", mu_sb.re("p (q k) -> p k q", q=6), bk[:, 0:48].re("p (k q) -> p k q", q=6))
        carry = A.alloc(8, F32); P.memset("pool", carry, 0.0)
        small = A.alloc(64, F32)
        PcT = A.alloc(64, F32); PcT3 = PcT.re("p (a c) -> p a c", a=8)
        prodb = A.alloc(6 * 8 * 128, BF16)
        xsT = prodb.re("p (q k t) -> p q k t", q=6, k=8)
        At, Bt, Kt, Rt, Bp = (prodb[:, i * D:(i + 1) * D] for i in range(5))
        yb = prodb[:, 0:D]
        yT = prodb[:, D:2 * D]; yT3 = yT.re("p (k t) -> p k t", k=8)
        QT = A.alloc(8 * 4 * 128, BF16); QT4 = QT.re("p (a q t) -> p a q t", a=8, q=4)
        Vblk = A.alloc(8 * 2 * 128, BF16); Vblk4 = Vblk.re("p (a h n) -> p a h n", a=8, h=2)
        P.memset("pool", Vblk, 0.0)
        S32p = [A.alloc(128, F32) for _ in range(8)]
        Sbfp = [A.alloc(128, BF16) for _ in range(8)]
        for t_ in S32p + Sbfp:
            P.memset("pool", t_, 0.0)
        fence_scr = A.alloc(2, F32)
        alias_toks = []

        def sub(fk, off, cols, dt):
            if dt == BF16:
                ap = Ft[fk].ap.bitcast(BF16)[:, off // 2:off // 2 + cols]
            else:
                ap = Ft[fk].ap[:, off // 4:off // 4 + cols]
            v_ = V(ap, toks=[Tok()])
            alias_toks.extend(v_.toks)
            return v_
        XBs, MBs, ARKs, P2s, P4s, P8s, NHs, Bblks, tmpDs = [], [], [], [], [], [], [], [], []
        for hs in range(4):
            if hs < 2:
                XBs.append((A.alloc(320, BF16), A.alloc(320, BF16)))
                MBs.append(A.alloc(256, BF16)); ARKs.append(A.alloc(128, BF16))
                P2s.append(A.alloc(256, BF16)); P4s.append(A.alloc(256, BF16)); P8s.append(A.alloc(128, BF16))
                NHs.append(A.alloc(128, BF16)); Bblks.append(A.alloc(512, BF16)); tmpDs.append(A.alloc(64, F32))
            else:
                fk = 1 if hs == 2 else 2
                XBs.append((sub(fk, 0, 320, BF16), sub(fk, 640, 320, BF16)))
                MBs.append(sub(fk, 1280, 256, BF16)); ARKs.append(sub(fk, 1792, 128, BF16))
                P2s.append(sub(fk, 2048, 256, BF16)); P4s.append(sub(fk, 2560, 256, BF16)); P8s.append(sub(fk, 3072, 128, BF16))
                NHs.append(sub(fk, 3328, 128, BF16))
                o6 = (hs - 2) * 1280
                Bblks.append(sub(6, o6, 512, BF16)); tmpDs.append(sub(6, o6 + 1024, 64, F32))
        Apair = [A.alloc(128, BF16) for _ in range(2)]
        TTp = [A.alloc(D, BF16) for _ in range(2)]; TTp3 = [t_.re("p (c n) -> p c n", c=8) for t_ in TTp]
        Db = [[A.alloc(D, BF16) for _ in range(2)] for _ in range(2)]
        Db3 = [[d_.re("p (c n) -> p c n", c=8) for d_ in dd_] for dd_ in Db]
        RTb = [A.alloc(1152, BF16) for _ in range(2)]
        WT = [[A.alloc(128, BF16) for _ in range(2)] for _ in range(2)]
        hw = A.alloc(128, BF16); ha = A.alloc(128, BF16); hg = A.alloc(128, BF16)
        NSC = 2
        Sc32 = [A.alloc(128, F32) for _ in range(NSC)]
        Scb = [A.alloc(128, BF16) for _ in range(NSC)]
        for t_ in TTp + Db[0] + Db[1] + RTb:
            P.memset("pool", t_, 0.0)

        def fence():
            f_ = fence_scr.ap
            P.add("pool", lambda e: e.memset(f_, 0.0), wr=F[1].toks + F[2].toks + F[6].toks + alias_toks + fence_scr.toks, cost=150.0)
        ST_in = self.dscr("wkv_in_T", [NS * 8 * 128, 128])
        ST_out = self.dscr("wkv_out_T", [NS * 8 * 128, 128])
        Z = F[1]; Z3 = Z.re("p (a n) -> p a n", a=8)
        Zo = F[2]; Zo3 = Zo.re("p (a n) -> p a n", a=8)
        P.memset("pool", Z, 0.0)
        for s in range(NS):
            for hh in range(2):
                P.dma("sp", Z3[64 * hh:64 * hh + 64, :, 64 * hh:64 * hh + 64],
                      dr["st_wkv"][s].re("(a h) v k -> h v a k", h=2)[hh])
            for a in range(8):
                bk = self.nb()
                P.tr(bk[:, 0:128], Z3[:, a, :], ident)
                P.copy("act", Zo3[:, a, :], bk[:, 0:128])
            P.dma("sp", ST_in[s * 1024:(s + 1) * 1024, :].re("(a p) n -> p a n", p=128), Zo3)

        for ti in range(self.NTT):
            is_s = ti >= NT
            stl = ti - NT
            negc = self.c32("negc_s") if is_s else self.c32("negc_p")
            rowm = self.c32("rowm_s")
            x = F[0]
            P.dma("sp", x, self.res_src("R0", ti))
            xn32 = F[5]
            self.rmsnorm(x, bc["gmix"], xn32, F[7], small[:, 0:4])
            if is_s:
                for s_ in range(8):
                    sq = stl * 8 + s_
                    P.dma("sp", xn32[16 * s_ + 1:16 * s_ + 2, :], dr["st_shift"][sq:sq + 1, :])
                for s_ in range(8):
                    sq = stl * 8 + s_
                    P.dma("sp", dr["s_shift"][sq:sq + 1, :], xn32[16 * s_ + 5:16 * s_ + 6, :])
            elif ti == NT - 1:
                P.dma("sp", dr["p_shift"], xn32[127:128, :])
            xnTe = Ft[4].re("p (k t) -> p k t", k=8)
            for hf in range(2):
                bk = self.nb()
                for k in range(4):
                    kk_ = hf * 4 + k
                    P.tr(bk[:, k * 128:(k + 1) * 128], xn32[:, kk_ * 128:(kk_ + 1) * 128], ident)
                P.copy("act", xnTe[:, hf * 4:hf * 4 + 4, 1:129], bk.re("p (k t) -> p k t", k=4))
            P.copy("pool", xnTe[:, :, 0:1], carry.re("p (k o) -> p k o", o=1))
            P.copy("pool", carry.re("p (k o) -> p k o", o=1), xnTe[:, :, 128:129])
            xx = F[6].re("p (k t) -> p k t", k=8)
            mt = F[7].re("p (k t) -> p k t", k=8)
            P.tt("pool", xx, xnTe[:, :, 0:128], xnTe[:, :, 1:129], ALU.subtract)
            for q in range(6):
                for k in range(8):
                    P.stt("dve", xsT[:, q, k, :], xx[:, k, :], mu3[:, q, k:k + 1], xnTe[:, k, 1:129], ALU.mult, ALU.add)
            bk = self.nb()
            for k in range(8):
                P.mm(bk[0:64, 0:128], w13[:, k, :], xsT[:, 3, k, :], start=(k == 0), stop=(k == 7))
            for k in range(8):
                P.mm(bk[0:64, 128:256], a13[:, k, :], xsT[:, 4, k, :], start=(k == 0), stop=(k == 7))
            for k in range(8):
                P.mm(bk[:, 256:384], g13[:, k, :], xsT[:, 5, k, :], start=(k == 0), stop=(k == 7))
            P.act(hw[0:64, :], bk[0:64, 0:128], AF.Tanh)
            P.copy("act", ha[0:64, :], bk[0:64, 128:256])
            P.act(hg, bk[:, 256:384], AF.Sigmoid)
            r32, k32, v32, g32, lw, a32, kkn = F[0], F[1], F[3], F[4], F[5], F[6], F[2]
            for q, dst in ((0, r32), (1, k32), (2, v32)):
                for hf in range(2):
                    bk = self.nb()
                    for k in range(8):
                        P.mm(bk, xsT[:, q, k, :], wrkv4[:, q, k, hf * 512:(hf + 1) * 512], start=(k == 0), stop=(k == 7))
                    P.copy("act", dst[:, hf * 512:(hf + 1) * 512], bk)
            for hf in range(2):
                cs = slice(hf * 512, (hf + 1) * 512)
                bk = self.nb()
                P.mm(bk, hw[0:64, :], w2[0:64, cs])
                P.tt("dve", lw[:, cs], bk, bc["w0"][:, cs], ALU.add)
                bk = self.nb()
                P.mm(bk, ha[0:64, :], a2[0:64, cs])
                P.tt("dve", a32[:, cs], bk, bc["a0"][:, cs], ALU.add)
                bk = self.nb()
                P.mm(bk, hg, g2[:, cs])
                P.copy("act", g32[:, cs], bk)
            P.act(lw, lw, AF.Sigmoid)
            P.ts("dve", lw, lw, negc, ALU.mult)
            P.act(a32, a32, AF.Sigmoid)
            if is_s:
                P.ts("pool", v32, v32, rowm, ALU.mult)
            v4 = v32.re("p (a h n) -> p a h n", a=8, h=2)
            P.copy("pool", Vblk4[:, :, 0, 0:64], v4[:, :, 0, :])
            P.copy("pool", Vblk4[:, :, 1, 64:128], v4[:, :, 1, :])
            t3 = F[7]
            P.tt("pool", kkn, k32, bc["kk"], ALU.mult)
            P.tt("pool", t3, kkn, kkn, ALU.mult)
            P.red("dve", small[:, 16:32], t3.re("p (h n) -> p h n", h=16))
            P.act(small[:, 16:32], small[:, 16:32], AF.Ln, bias=self.c32("eps24"))
            P.act(small[:, 16:32], small[:, 16:32], AF.Exp, scale=-0.5)
            if is_s:
                P.ts("dve", small[:, 16:32], small[:, 16:32], rowm, ALU.mult)
            P.tt("dve", kkn.re("p (h n) -> p h n", h=16), kkn.re("p (h n) -> p h n", h=16),
                 small[:, 16:32].bc(2, [128, 16, 64]), ALU.mult)
            P.stt("dve", t3, a32, -1.0, bc["ka"], ALU.add, ALU.mult)
            P.stt("dve", k32, t3, 1.0, k32, ALU.add, ALU.mult)
            if is_s:
                P.ts("pool", k32, k32, rowm, ALU.mult)
            P.tt("pool", t3, r32, bc["rk"], ALU.mult)
            P.tt("pool", t3, t3, k32, ALU.mult)
            P.red("dve", small[:, 32:48], t3.re("p (h n) -> p h n", h=16))
            P.tt("pool", a32, a32, kkn, ALU.mult)
            bk = self.nb()
            for a in range(8):
                P.mm(bk[:, a * 8:(a + 1) * 8], lw[:, a * 128:(a + 1) * 128], maskc)
            P.act(PcT, bk[:, 0:64], AF.Exp)
            def cum(mask, hf):
                b_ = self.nb()
                P.mm(b_, mask, lw[:, hf * 512:(hf + 1) * 512])
                return b_
            for hf in range(2):
                cs = slice(hf * 512, (hf + 1) * 512)
                bL = cum(m_ui, hf)
                P.act(t3[:, cs], bL, AF.Exp)
                P.tt("dve", Rt[:, cs], r32[:, cs], t3[:, cs], ALU.mult)
            for hf in range(2):
                cs = slice(hf * 512, (hf + 1) * 512)
                bL = cum(m_ui, hf)
                P.act(t3[:, cs], bL, AF.Exp, scale=-1.0)
                P.tt("dve", Bt[:, cs], a32[:, cs], t3[:, cs], ALU.mult)
                P.tt("pool", Kt[:, cs], k32[:, cs], t3[:, cs], ALU.mult)
            for hf in range(2):
                cs = slice(hf * 512, (hf + 1) * 512)
                bL = cum(m_su, hf)
                P.act(t3[:, cs], bL, AF.Exp)
                P.stt("dve", At[:, cs], kkn[:, cs], -1.0, t3[:, cs], ALU.mult, ALU.mult)
            Kp = lw
            for hf in range(2):
                cs = slice(hf * 512, (hf + 1) * 512)
                bL = cum(m_sl, hf)
                P.act(t3[:, cs], bL, AF.Exp)
                P.tt("dve", Bp[:, cs], a32[:, cs], t3[:, cs], ALU.mult)
            P.tt("pool", Kp, k32, t3, ALU.mult)
            for q, src in enumerate((Kt, Bt, At, Rt)):
                bk = self.nb()
                pb = bk.bitcast(BF16)
                for a in range(8):
                    P.tr(pb[:, a * 128:(a + 1) * 128], src[:, a * 128:(a + 1) * 128], identb)
                P.copy("act", QT4[:, :, q, :], pb.re("p (a t) -> p a t", a=8))
            Yall = F[0]
            fence()
            for a in range(8):
                pp_ = a % 2
                for hh in range(2):
                    hs = (2 * a + hh) % 4
                    MB_, ARK_, P2_, P4_, P8_, NH_ = MBs[hs], ARKs[hs], P2s[hs], P4s[hs], P8s[hs], NHs[hs]
                    rows = slice(64 * hh, 64 * hh + 64)
                    hc = slice((2 * a + hh) * 64, (2 * a + hh) * 64 + 64)
                    xb0, xb1 = XBs[hs]
                    P.copy("pool", xb0[:, 0:64], At[:, hc])
                    bk = self.nb()
                    P.mm(bk[:, 0:256], QT4[rows, a, 2, :], QT4[rows, a, 0:2, :])
                    P.tt("dve", xb0[:, 64:320].re("p (b t) -> p b t", b=2), bk[:, 0:256].re("p (b t) -> p b t", b=2),
                         m_sl.bc(1, [128, 2, 128]), ALU.mult)
                    Nak, Nm = xb0[:, 64:192], xb0[:, 192:320]
                    bk = self.nb()
                    P.mm(bk[:, 0:256], QT4[rows, a, 1, :], QT4[rows, a, 2:4, :])
                    P.tt("dve", MB_, bk[:, 0:256], mask2, ALU.mult)
                    Mm, Arb = MB_[:, 0:128], MB_[:, 128:256]
                    bk = self.nb()
                    P.mm(bk[:, 0:128], QT4[rows, a, 0, :], QT4[rows, a, 3, :])
                    P.tt("dve", ARK_, bk[:, 0:128], m_ui, ALU.mult)
                    bk = self.nb()
                    P.mm(bk[:, 0:128], Nm, Mm)
                    P.mm(bk[:, 128:256], Mm, Nm)
                    P.copy("act", P2_, bk[:, 0:256])
                    bk = self.nb()
                    P.mm(bk[:, 0:128], P2_[:, 128:256], P2_[:, 0:128])
                    P.mm(bk[:, 128:256], P2_[:, 0:128], P2_[:, 128:256])
                    P.copy("act", P4_, bk[:, 0:256])
                    bk = self.nb()
                    P.mm(bk[:, 0:128], P4_[:, 128:256], P4_[:, 0:128])
                    P.copy("act", P8_, bk[:, 0:128])
                    cur, nxt = xb0, xb1
                    for lv, Ml in enumerate((Mm, P2_[:, 0:128], P4_[:, 0:128], P8_)):
                        bk = self.nb()
                        P.mm(bk[:, 0:192], Ml, cur[:, 0:192])
                        if lv < 3:
                            P.tt("dve", nxt[:, 0:192], bk[:, 0:192], cur[:, 0:192], ALU.add)
                            cur, nxt = nxt, cur
                        else:
                            P.tt("dve", Apair[pp_][:, 64 * hh:64 * hh + 64], bk[:, 0:64], cur[:, 0:64], ALU.add)
                            P.tt("dve", NH_, bk[:, 64:192], cur[:, 64:192], ALU.add)
                for hh in range(2):
                    rows = slice(64 * hh, 64 * hh + 64)
                    hc = slice((2 * a + hh) * 64, (2 * a + hh) * 64 + 64)
                    hs = (2 * a + hh) % 4
                    Arb = MBs[hs][:, 128:256]
                    bb = Bblks[hs].re("p (c n) -> p c n", c=8)
                    P.tt("dve", bb, Bp[:, hc].bc(1, [128, 8, 64]), maskc.bc(2, [128, 8, 64]), ALU.mult)
                    bk = self.nb()
                    P.mm(bk, Apair[pp_], Bblks[hs])
                    P.copy("act", TTp3[pp_][rows, :, 64 * hh:64 * hh + 64], bk[rows, :].re("p (c n) -> p c n", c=8))
                    bk = self.nb()
                    P.mm(bk[:, 0:64], NHs[hs], Bp[:, hc])
                    P.tt("dve", tmpDs[hs], bk[:, 0:64], Kp[:, hc], ALU.add)
                    P.tt("dve", Db3[pp_][hh][:, :, 64 * hh:64 * hh + 64], tmpDs[hs].bc(1, [128, 8, 64]), maskc.bc(2, [128, 8, 64]), ALU.mult)
                    bk = self.nb()
                    P.mm(bk[:, 0:128], Apair[pp_], Arb)
                    P.tt("dve", RTb[pp_][rows, :].re("p (c s) -> p c s", s=144)[:, :, 0:16],
                         bk[rows, 0:128].re("p (c t) -> p c t", t=16), QT4[rows, a, 3, :].re("p (c t) -> p c t", t=16), ALU.add)
                    bk = self.nb()
                    P.mm(bk[:, 0:128], NHs[hs], Arb)
                    P.tt("dve", WT[pp_][hh], bk[:, 0:128], ARKs[hs], ALU.add)
                ybk = self.bank[6 + a % 2]
                P.mm(ybk[:, 0:128], WT[pp_][0], Vblk4[:, a, 0, :], start=True, stop=False)
                P.mm(ybk[:, 0:128], WT[pp_][1], Vblk4[:, a, 1, :], start=False, stop=False)
                for c in range(8):
                    if is_s:
                        sq = stl * 8 + c
                        sc32, scb = Sc32[c % NSC], Scb[c % NSC]
                        r0 = (sq * 8 + a) * 128
                        P.dma("sp", sc32, ST_in[r0:r0 + 128, :])
                        P.copy("act", scb, sc32)
                        s_b, s_32 = scb, sc32
                    else:
                        s_b, s_32 = Sbfp[a], S32p[a]
                    P.mm(ybk[:, 0:128], RTb[pp_][:, c * 128:(c + 1) * 128], s_b, start=False, stop=(c == 7))
                    sbk = self.nb()
                    P.mm(sbk[:, 0:128], TTp3[pp_][:, c, :], s_b, start=True, stop=False)
                    P.mm(sbk[:, 0:128], Db3[pp_][0][:, c, :], Vblk4[:, a, 0, :], start=False, stop=False)
                    P.mm(sbk[:, 0:128], Db3[pp_][1][:, c, :], Vblk4[:, a, 1, :], start=False, stop=True)
                    if is_s:
                        P.stt("dve", s_32, s_32, PcT3[:, a, c:c + 1], sbk[:, 0:128], ALU.mult, ALU.add)
                        P.dma("sp", ST_out[r0:r0 + 128, :], s_32)
                    else:
                        P.stt("dve", s_b, s_32, PcT3[:, a, c:c + 1], sbk[:, 0:128], ALU.mult, ALU.add)
                        P.stt("dve", s_32, s_32, PcT3[:, a, c:c + 1], sbk[:, 0:128], ALU.mult, ALU.add)
                P.copy("act", Yall[:, a * 128:(a + 1) * 128], ybk[:, 0:128])
            fence()
            if self.debug:
                P.dma("sp", self.dr["dbg_y"][ti * 128:(ti + 1) * 128, :], Yall)
                P.dma("sp", self.dr["dbg_g"][ti * 128:(ti + 1) * 128, :], g32)
            Y3 = Yall.re("p (h n) -> p h n", h=16)
            sm = small
            P.red("dve", sm[:, 0:16], Y3)
            P.tt("pool", t3, Yall, Yall, ALU.mult)
            P.red("dve", sm[:, 48:64], t3.re("p (h n) -> p h n", h=16))
            P.ts("dve", sm[:, 0:16], sm[:, 0:16], 1.0 / 64, ALU.mult)
            P.tt("dve", sm[:, 16:32], sm[:, 0:16], sm[:, 0:16], ALU.mult)
            P.stt("dve", sm[:, 48:64], sm[:, 48:64], 1.0 / 64, sm[:, 16:32], ALU.mult, ALU.subtract)
            P.act(sm[:, 48:64], sm[:, 48:64], AF.Ln, bias=self.c32("gneps"))
            P.act(sm[:, 48:64], sm[:, 48:64], AF.Exp, scale=-0.5)
            P.tt("dve", Y3, Y3, sm[:, 0:16].bc(2, [128, 16, 64]), ALU.subtract)
            P.tt("dve", Y3, Y3, sm[:, 48:64].bc(2, [128, 16, 64]), ALU.mult)
            P.tt("pool", Yall, Yall, bc["lnw"], ALU.mult)
            P.tt("pool", Yall, Yall, bc["lnb"], ALU.add)
            P.tt("pool", t3.re("p (h n) -> p h n", h=16), v32.re("p (h n) -> p h n", h=16),
                 sm[:, 32:48].bc(2, [128, 16, 64]), ALU.mult)
            P.tt("dve", Yall, Yall, t3, ALU.add)
            if self.debug:
                P.dma("sp", self.dr["dbg_yn"][ti * 128:(ti + 1) * 128, :], Yall)
            P.tt("dve", yb, Yall, g32, ALU.mult)
            self.transpose_bf(yb, yT3, 0, self.nb())
            xo = F[0]
            xr_ = F[7]
            P.dma("sp", xr_, self.res_src("R0", ti))
            for hf in range(2):
                cs = slice(hf * 512, (hf + 1) * 512)
                bk = self.nb()
                for k in range(8):
                    P.mm(bk, yT3[:, k, :], wo3[:, k, cs], start=(k == 0), stop=(k == 7))
                P.tt("dve", xo[:, cs], bk, xr_[:, cs], ALU.add)
            P.dma("sp", self.res_src("R1", ti), xo)
        Zt = F[1]; Zt3 = Zt.re("p (a n) -> p a n", a=8)
        for a in range(8):
            bk = self.nb()
            P.tr(bk[:, 0:128], S32p[a], ident)
            P.copy("act", Zt3[:, a, :], bk[:, 0:128])
        for hh in range(2):
            P.dma("sp", dr["p_wkv"].re("(a h) v k -> h v a k", h=2)[hh], Zt3[64 * hh:64 * hh + 64, :, 64 * hh:64 * hh + 64])
        for s in range(NS):
            Zi = F[3].re("p (a n) -> p a n", a=8)
            Zo2 = F[4 + s % 2].re("p (a n) -> p a n", a=8)
            P.dma("sp", Zi, ST_out[s * 1024:(s + 1) * 1024, :].re("(a p) n -> p a n", p=128))
            for a in range(8):
                bk = self.nb()
                P.tr(bk[:, 0:128], Zi[:, a, :], ident)
                P.copy("act", Zo2[:, a, :], bk[:, 0:128])
            for hh in range(2):
                P.dma("sp", dr["s_wkv"][s].re("(a h) v k -> h v a k", h=2)[hh], Zo2[64 * hh:64 * hh + 64, :, 64 * hh:64 * hh + 64])


    def phase_attn(self):
        P, A, NT, NST, NS, NTT, T = self.P, self.A, self.NT, self.NST, self.NS, self.NTT, self.T
        dr = self.dr
        identb = self.c16("ident")
        QTs = [[self.dscr("qT%d_%d" % (g, j), [8 * 128, NTT * 128], BF16) for j in range(2)] for g in range(3)]
        Vs = [self.dscr("vs%d" % g, [NTT * 128, H * HS], BF16) for g in range(3)]
        Os = [self.dscr("os%d" % g, [NTT * 128, H * 65], F32) for g in range(3)]
        qs = [self.dscr("qs%d" % g, [NST * 128, D], F32) for g in range(3)]
        kpad = [self.dscr("kpad%d" % g, [NST * 128, D], F32) for g in range(3)]
        vpad = [self.dscr("vpad%d" % g, [NST * 128, D], F32) for g in range(3)]
        A.mark()
        win = A.alloc(8 * 9 * D, BF16); win3 = win.re("p (k n) -> p k n", k=8)
        for k in range(8):
            P.dma("pool", win3[:, k, :], dr["attn_w_in"][k * 128:(k + 1) * 128, :])
        gbc = A.alloc(D, F32)
        self.load_bcast(gbc, dr["norm_mix"][1:2, :])
        rope = A.alloc(NT * 128, F32); rope3 = rope.re("p (t n) -> p t n", n=128)
        P.dma("sp", rope, dr["rope_p"])
        x = A.alloc(D, F32)
        xn = A.alloc(D, BF16)
        xnT = A.alloc(D, BF16); xnT3 = xnT.re("p (k t) -> p k t", k=8)
        small = A.alloc(4, F32)
        xq = A.alloc(D, F32); A32 = A.alloc(D, F32); B32 = A.alloc(D, F32)
        ob = [A.alloc(D, BF16) for _ in range(2)]
        vb = [A.alloc(H * HS, BF16) for _ in range(2)]
        for v_ in vb:
            P.memset("pool", v_, 1.0)
        QTt = [A.alloc(D, BF16) for _ in range(2)]
        cnt = 0
        for ti in range(NTT):
            is_s = ti >= NT
            stl = ti - NT
            P.dma("sp", x, self.res_src("R2", ti))
            self.rmsnorm(x, gbc, xn, A32, small)
            self.transpose_bf(xn, xnT3, 0, self.bank[6])
            if is_s:
                o_r = CF["rope_s"][0]
                cs2 = self.cf[:, o_r:o_r + 64]
                sn2 = self.cf[:, o_r + 64:o_r + 128]
            else:
                cs2 = rope3[:, ti, 0:64]
                sn2 = rope3[:, ti, 64:128]
            for g in range(3):
                for j in range(3):
                    col0 = g * 3 * D + j * D
                    bks = []
                    for hf in range(2):
                        bk = self.bank[(2 * cnt + hf) % 6]
                        bks.append(bk)
                        for k in range(8):
                            P.mm(bk, xnT3[:, k, :], win3[:, k, col0 + hf * 512:col0 + (hf + 1) * 512], start=(k == 0), stop=(k == 7))
                    cnt += 1
                    if j == 2:
                        for hf in range(2):
                            P.copy("act", xq[:, hf * 512:(hf + 1) * 512], bks[hf])
                        vbb = vb[cnt % 2]
                        P.copy("pool", vbb.re("p (h n) -> p h n", n=HS)[:, :, 0:64], xq.re("p (h n) -> p h n", n=64))
                        P.dma("sp", Vs[g][ti * 128:(ti + 1) * 128, :], vbb)
                        if is_s:
                            P.dma("sp", vpad[g][stl * 128:(stl + 1) * 128, :], xq)
                        else:
                            keep = min(WINDOWS[g], T) // 128
                            if ti >= NT - keep:
                                r0 = (ti - (NT - keep)) * 128
                                P.dma("sp", dr["p_v%d" % g][r0:r0 + 128, :], xq)
                        continue
                    for hf in range(2):
                        cs = slice(hf * 512, (hf + 1) * 512)
                        P.copy("act", xq[:, cs], bks[hf])
                        P.tt("dve", A32[:, cs].re("p (h n) -> p h n", n=64), xq[:, cs].re("p (h n) -> p h n", n=64),
                             cs2.bc(1, [128, 8, 64]), ALU.mult)
                    P.tt("pool", B32.re("p (h n) -> p h n", n=64), xq.re("p (h n) -> p h n", n=64), sn2.bc(1, [128, 16, 64]), ALU.mult)
                    A4 = A32.re("p (h b n) -> p h b n", b=2, n=32)
                    B4 = B32.re("p (h b n) -> p h b n", b=2, n=32)
                    P.tt("dve", A4[:, :, 0, :], A4[:, :, 0, :], B4[:, :, 1, :], ALU.subtract)
                    P.tt("pool", A4[:, :, 1, :], A4[:, :, 1, :], B4[:, :, 0, :], ALU.add)
                    if j == 1:
                        if is_s:
                            P.dma("sp", kpad[g][stl * 128:(stl + 1) * 128, :], A32)
                        else:
                            keep = min(WINDOWS[g], T) // 128
                            if ti >= NT - keep:
                                r0 = (ti - (NT - keep)) * 128
                                P.dma("sp", dr["p_k%d" % g][r0:r0 + 128, :], A32)
                    elif is_s:
                        P.dma("sp", qs[g][stl * 128:(stl + 1) * 128, :], A32)
                    obb = ob[cnt % 2]
                    P.copy("act", obb, A32)
                    qt = QTt[cnt % 2]
                    bk = self.bank[7]
                    pb = bk.bitcast(BF16)
                    for a in range(8):
                        P.tr(pb[:, a * 128:(a + 1) * 128], obb[:, a * 128:(a + 1) * 128], identb)
                    P.copy("act", qt, pb)
                    P.dma("sp", QTs[g][j][:, ti * 128:(ti + 1) * 128].re("(a p) t -> p a t", p=128), qt.re("p (a t) -> p a t", a=8))
        A.release()
        P.barrier()
        import os
        ASTOP = os.environ.get("ATT_STOP", "")
        if ASTOP == "s1":
            return
        for g in range(3):
            P.dma("sp", dr["s_k%d" % g].re("(s j) d -> s j d", j=4), kpad[g].re("(s q) d -> s q d", q=16)[:, 2:6, :])
            P.dma("sp", dr["s_v%d" % g].re("(s j) d -> s j d", j=4), vpad[g].re("(s q) d -> s q d", q=16)[:, 2:6, :])
        A.mark()
        NBLK = NT
        Vg = A.alloc(NBLK * H * HS, BF16); Vg4 = Vg.re("p (b h n) -> p b h n", b=NBLK, n=HS)
        Qn = [A.alloc(T, BF16) for _ in range(2)]
        Kn = [A.alloc(T, BF16) for _ in range(2)]
        Qc = [A.alloc(T, BF16) for _ in range(2)]
        Kc = [A.alloc(T, BF16) for _ in range(2)]
        Obuf = [A.alloc(NBLK * 130, F32) for _ in range(2)]
        pT = [A.alloc(512, BF16) for _ in range(3)]
        o_m = CB["am_ge"][0]
        am2 = self.cb[:, o_m:o_m + 256]
        it = 0
        for g in range(3):
            d = DILS[g]
            L = T // d
            nbc = L // 128
            assert nbc >= 1
            for c in range(d):
                for n in range(nbc):
                    blk = c * nbc + n
                    r0 = c + d * 128 * n
                    src = Vs[g][r0:r0 + d * 127 + 1, :]
                    if d > 1:
                        src = V(Vs[g].ap[r0:r0 + d * 128 - (d - 1):d, :], Vs[g].toks) if False else None
                    if d == 1:
                        P.dma("sp", Vg4[:, blk, :, :], Vs[g][r0:r0 + 128, :].re("p (h n) -> p h n", n=HS))
                    else:
                        rows = Vs[g][0:T, :].re("(l c) n -> c l n", c=d)[c, n * 128:(n + 1) * 128, :]
                        P.dma("sp", Vg4[:, blk, :, :], rows.re("p (h n) -> p h n", n=HS))
            if ASTOP == "s2a":
                continue
            for a in range(8):
                b = a % 2
                P.dma("sp", Qn[b], QTs[g][0][a * 128:(a + 1) * 128, 0:T])
                P.dma("sp", Kn[b], QTs[g][1][a * 128:(a + 1) * 128, 0:T])
                if d == 1:
                    qc3 = Qn[b].re("p (c l) -> p c l", c=1)
                    kc3 = Kn[b].re("p (c l) -> p c l", c=1)
                else:
                    qc3 = Qc[b].re("p (c l) -> p c l", c=d)
                    kc3 = Kc[b].re("p (c l) -> p c l", c=d)
                    P.copy("pool", qc3, Qn[b].re("p (l c) -> p c l", c=d))
                    P.copy("pool", kc3, Kn[b].re("p (l c) -> p c l", c=d))
                Ob3 = Obuf[b].re("p (b n) -> p b n", n=130)
                if ASTOP == "s2q":
                    continue
                for c in range(d):
                    for n in range(nbc):
                        blk = c * nbc + n
                        it += 1
                        pt = pT[it % 3]
                        for hh in range(2):
                            bk = self.bank[(2 * it + hh) % 4]
                            rows = slice(64 * hh, 64 * hh + 64)
                            if n > 0:
                                P.mm(bk[:, 0:128], kc3[rows, c, (n - 1) * 128:n * 128], qc3[rows, c, n * 128:(n + 1) * 128])
                            P.mm(bk[:, 128:256], kc3[rows, c, n * 128:(n + 1) * 128], qc3[rows, c, n * 128:(n + 1) * 128])
                            if n > 0:
                                P.act(pt[:, hh * 256:(hh + 1) * 256], bk[:, 0:256], AF.Exp, scale=0.125)
                            else:
                                P.act(pt[:, hh * 256 + 128:(hh + 1) * 256], bk[:, 128:256], AF.Exp, scale=0.125)
                        if n > 0:
                            P.tt("dve", pt.re("p (h m) -> p h m", h=2), pt.re("p (h m) -> p h m", h=2), am2.bc(1, [128, 2, 256]), ALU.mult)
                        else:
                            ptv = pt.re("p (h m) -> p h m", h=2)[:, :, 128:256]
                            P.tt("dve", ptv, ptv, am2[:, 128:256].bc(1, [128, 2, 128]), ALU.mult)
                        if ASTOP == "s2m":
                            continue
                        okb = self.bank[4 + it % 4]
                        for hh in range(2):
                            hd = 2 * a + hh
                            if n > 0:
                                P.mm(okb[:, hh * 65:hh * 65 + 65], pt[:, hh * 256:hh * 256 + 128], Vg4[:, blk - 1, hd, 0:65], start=True, stop=False)
                            P.mm(okb[:, hh * 65:hh * 65 + 65], pt[:, hh * 256 + 128:hh * 256 + 256], Vg4[:, blk, hd, 0:65], start=(n == 0), stop=True)
                        P.copy("act", Ob3[:, blk, :], okb[:, 0:130])
                if ASTOP in ("s2m", "s2p"):
                    continue
                dst = Os[g][0:T, a * 130:(a + 1) * 130].re("(n i c) m -> i c n m", i=128, c=d)
                srcv = Obuf[b].re("p (c n m) -> p c n m", c=d, m=130)
                for c in range(d):
                    for n0 in range(0, nbc, 8):
                        n1 = min(nbc, n0 + 8)
                        P.dma("sp", dst[:, c, n0:n1, :], srcv[:, c, n0:n1, :])
        A.release()
        P.barrier()
        if ASTOP in ("s2", "s2a", "s2q", "s2m", "s2p"):
            return
        A.mark()
        self.attn_sample(Os, qs, kpad, vpad)
        A.release()
        P.barrier()
        if ASTOP == "s2b":
            return
        A.mark()
        wo = A.alloc(8 * D, BF16); wo3 = wo.re("p (k n) -> p k n", k=8)
        P.dma("pool", wo3, dr["attn_w_o"].re("(k p) n -> p k n", p=128))
        O3 = [[A.alloc(H * 65, F32) for _ in range(3)] for _ in range(2)]
        xr = [A.alloc(D, F32) for _ in range(2)]
        obf = [A.alloc(D, BF16) for _ in range(2)]
        oT = [A.alloc(D, BF16) for _ in range(2)]
        rden = [A.alloc(16, F32) for _ in range(2)]
        for ti in range(NTT):
            b = ti % 2
            for g in range(3):
                P.dma("sp", O3[b][g], Os[g][ti * 128:(ti + 1) * 128, :])
            P.dma("sp", xr[b], self.res_src("R2", ti))
            P.tt("dve", O3[b][0], O3[b][0], O3[b][1], ALU.add)
            P.tt("pool", O3[b][0], O3[b][0], O3[b][2], ALU.add)
            o3 = O3[b][0].re("p (h n) -> p h n", n=65)
            P.add("dve", (lambda o_, i_: (lambda e: e.reciprocal(out=o_, in_=i_)))(rden[b].ap.rearrange("p (h o) -> p h o", o=1), o3.ap[:, :, 64:65]),
                  rd=o3.toks, wr=rden[b].toks)
            P.tt("dve", obf[b].re("p (h n) -> p h n", n=64), o3[:, :, 0:64], rden[b].bc(2, [128, 16, 64]), ALU.mult)
            oT3 = oT[b].re("p (k t) -> p k t", k=8)
            self.transpose_bf(obf[b], oT3, 0, self.bank[6])
            for hf in range(2):
                cs = slice(hf * 512, (hf + 1) * 512)
                bk = self.bank[hf + 2 * (ti % 2)]
                for k in range(8):
                    P.mm(bk, oT3[:, k, :], wo3[:, k, cs], start=(k == 0), stop=(k == 7))
                P.tt("dve", xr[b][:, cs], bk, xr[b][:, cs], ALU.add)
            P.dma("sp", self.res_src("R3", ti), xr[b])
        A.release()

    def attn_sample(self, Os, qs, kpad, vpad):
        P, A, NS, NT = self.P, self.A, self.NS, self.NT
        dr = self.dr
        ones = A.alloc(H * 65, F32)
        P.memset("pool", ones, 1.0)
        for g in range(3):
            for stl in range(self.NST):
                r0 = (NT + stl) * 128
                P.dma("sp", Os[g][r0:r0 + 128, :], ones)
        Kt = [A.alloc(D, F32) for _ in range(2)]
        Vt = [A.alloc(D, F32) for _ in range(2)]
        Vb = [A.alloc(H * HS, BF16) for _ in range(2)]
        qbc = [A.alloc(D, F32) for _ in range(2)]
        prod = [A.alloc(D, F32) for _ in range(2)]
        sc = [A.alloc(16, F32) for _ in range(2)]
        pb = [A.alloc(16, BF16) for _ in range(2)]
        Kn = [A.alloc(D, F32) for _ in range(2)]
        Vn = [A.alloc(D, F32) for _ in range(2)]
        Vnb = [A.alloc(H * HS, BF16) for _ in range(2)]
        prodn = A.alloc(D, F32)
        scn = A.alloc(16, F32)
        pnb = [A.alloc(16, BF16) for _ in range(2)]
        orow = [A.alloc(H * 65, F32) for _ in range(2)]
        for v_ in Vb + Vnb:
            P.memset("pool", v_, 1.0)
        o_ge, o_le, o_eq = CF["ge_j"][0], CF["le_j"][0], CF["eq_j"][0]
        HCH = ((0, 7), (7, 14), (14, 16))
        u = 0
        kv = 0
        for s in range(NS):
            for g in range(3):
                d = DILS[g]
                W = WINDOWS[g]
                nb_ = (s * 3 + g) % 2
                rown = NT * 128 + 16 * s + 2
                P.dma("sp", Kn[nb_][0:4, :], kpad[g][16 * s + 2:16 * s + 6, :])
                P.dma("sp", Vn[nb_][0:4, :], vpad[g][16 * s + 2:16 * s + 6, :])
                P.copy("pool", Vnb[nb_][0:4, :].re("p (h n) -> p h n", n=HS)[:, :, 0:64], Vn[nb_][0:4, :].re("p (h n) -> p h n", n=64))
                for j in range(4):
                    if g > 0 or j == 0:
                        kb = kv % 2
                        kv += 1
                        if g == 0:
                            ksrc = dr["ck0"][s]
                            vsrc = dr["cv0"][s]
                        else:
                            ksrc = dr["ck%d" % g][s].re("(l c) n -> c l n", c=d)[j]
                            vsrc = dr["cv%d" % g][s].re("(l c) n -> c l n", c=d)[j]
                        P.dma("sp", Kt[kb], ksrc)
                        P.dma("sp", Vt[kb], vsrc)
                        P.copy("pool", Vb[kb].re("p (h n) -> p h n", n=HS)[:, :, 0:64], Vt[kb].re("p (h n) -> p h n", n=64))
                    ub = u % 2
                    u += 1
                    qrow = qs[g][16 * s + 2 + j:16 * s + 3 + j, :]
                    P.dma("sp", qbc[ub], V(qrow.ap.to_broadcast([128, D]), qrow.toks))
                    P.tt("dve", prod[ub], Kt[kb], qbc[ub], ALU.mult)
                    P.red("dve", sc[ub], prod[ub].re("p (h n) -> p h n", h=16))
                    P.act(pb[ub], sc[ub], AF.Exp, scale=0.125)
                    if g == 0 and j > 0:
                        P.ts("dve", pb[ub], pb[ub], self.cf[:, o_ge + j:o_ge + j + 1], ALU.mult)
                    P.tt("pool", prodn[0:4, :], Kn[nb_][0:4, :], qbc[ub][0:4, :], ALU.mult)
                    P.red("dve", scn[0:4, :], prodn[0:4, :].re("p (h n) -> p h n", h=16))
                    P.act(pnb[ub][0:4, :], scn[0:4, :], AF.Exp, scale=0.125)
                    om = (o_le if g == 0 else o_eq) + j
                    P.ts("dve", pnb[ub][0:4, :], pnb[ub][0:4, :], self.cf[0:4, om:om + 1], ALU.mult)
                    Vb3 = Vb[kb].re("p (h n) -> p h n", n=HS)
                    Vn3 = Vnb[nb_].re("p (h n) -> p h n", n=HS)
                    for ci, (h0, h1) in enumerate(HCH):
                        bk = self.bank[(3 * u + ci) % 8]
                        for h in range(h0, h1):
                            oc = slice((h - h0) * 65, (h - h0 + 1) * 65)
                            P.mm(bk[0:1, oc], pb[ub][:, h:h + 1], Vb3[:, h, 0:65], start=True, stop=False)
                            P.mm(bk[0:1, oc], pnb[ub][0:4, h:h + 1], Vn3[0:4, h, 0:65], start=False, stop=True)
                        P.copy("act", orow[ub][0:1, h0 * 65:h1 * 65], bk[0:1, 0:(h1 - h0) * 65])
                    P.dma("sp", Os[g][rown + j:rown + j + 1, :], orow[ub][0:1, :])


def core_inputs(inp, b, s0, NS, T, consts):
    cf, cb, rope_p = consts
    f = lambda a: np.ascontiguousarray(a, dtype=np.float32)
    m = {}
    m["xp"] = f(inp["x_prompt"][b])
    m["xs"] = f(inp["x_sample"][s0:s0 + NS]).reshape(NS * 4, D)
    m["st_shift"] = f(inp["state_rwkv_shift"][0, s0:s0 + NS])
    m["st_wkv"] = f(inp["state_rwkv_wkv"][0, s0:s0 + NS])
    caches = ((inp["cache_k_w128"], inp["cache_v_w128"]), (inp["cache_k_w512"], inp["cache_v_w512"]),
              (inp["cache_k_w2048"], inp["cache_v_w2048"]))
    for g, (ck, cv) in enumerate(caches):
        m["ck%d" % g] = f(ck[0, s0:s0 + NS]).reshape(NS, -1, D)
        m["cv%d" % g] = f(cv[0, s0:s0 + NS]).reshape(NS, -1, D)
    m["st_conv"] = f(inp["state_ffn_conv"][:, s0:s0 + NS])
    m["norm_mix"] = f(inp["norm_mix"]); m["norm_ffn"] = f(inp["norm_ffn"]); m["norm_final"] = f(inp["norm_final"]).reshape(1, D)
    m["rwkv_mu"] = f(inp["rwkv_mu"][0]); m["rwkv_w_rkv"] = f(inp["rwkv_w_rkv"][0])
    for k in ("w0", "a0", "k_k", "k_a", "ln_w", "ln_b"):
        m["rwkv_" + k] = f(inp["rwkv_" + k][0]).reshape(1, D)
    m["rwkv_r_k"] = f(inp["rwkv_r_k"][0]).reshape(1, D)
    for k in ("w1", "w2", "a1", "a2", "g1", "g2", "w_o"):
        m["rwkv_" + k] = f(inp["rwkv_" + k][0])
    m["attn_w_in"] = f(inp["attn_w_in"][0]); m["attn_w_o"] = f(inp["attn_w_o"][0])
    m["ffn_w_up"] = f(inp["ffn_w_up"]); m["ffn_conv_w"] = f(inp["ffn_conv_w"]); m["ffn_conv_b"] = f(inp["ffn_conv_b"])
    m["ffn_w_down"] = f(inp["ffn_w_down"])
    m["cf"] = cf; m["cb"] = cb; m["rope_p"] = rope_p
    return m


_CACHE = {}


def kernel(**inputs):
    inp = {k: np.asarray(v) for k, v in inputs.items()}
    T = inp["x_prompt"].shape[1]
    NB = inp["x_prompt"].shape[0]
    NSEQ = inp["x_sample"].shape[0]
    NS = NSEQ // NCORES
    key = (T, NS)
    if key not in _CACHE:
        _CACHE[key] = (Builder(T, NS).build(), host_consts(T))
    nc, consts = _CACHE[key]
    in_maps = [core_inputs(inp, c % NB, c * NS, NS, T, consts) for c in range(NCORES)]
    res = run_bass_kernel_spmd(nc, in_maps, core_ids=list(range(NCORES)))
    R = res.results
    f = np.float32
    pr = lambda name, shape: np.stack([np.asarray(R[b][name], f).reshape(shape) for b in range(NB)])
    sm = lambda name, shape: np.concatenate([np.asarray(R[c][name], f).reshape(shape) for c in range(NCORES)], 0)
    outs = [pr("y_p", (T, D)), sm("y_s", (NS, 4, D)),
            pr("p_shift", (D,))[None], pr("p_wkv", (H, DH, DH))[None]]
    for g, w in enumerate(WINDOWS):
        kp = min(w, T)
        outs.append(pr("p_k%d" % g, (kp, H, DH))[None])
        outs.append(pr("p_v%d" % g, (kp, H, DH))[None])
    outs.append(np.stack([np.asarray(R[b]["p_conv"], f).reshape(2, 2, FF) for b in range(NB)], 1))
    outs.append(sm("s_shift", (NS, D))[None])
    outs.append(sm("s_wkv", (NS, H, DH, DH))[None])
    for g in range(3):
        outs.append(sm("s_k%d" % g, (NS, 4, H, DH))[None])
        outs.append(sm("s_v%d" % g, (NS, 4, H, DH))[None])
    outs.append(np.concatenate([np.asarray(R[c]["s_conv"], f).reshape(2, NS, 2, FF) for c in range(NCORES)], 1))
    return tuple(outs)
```

```python
import math
import os
import numpy as np
import ml_dtypes
from contextlib import ExitStack
import concourse.bass as bass
import concourse.mybir as mybir
from concourse.bass_utils import run_bass_kernel_spmd

F32 = mybir.dt.float32
BF16 = mybir.dt.bfloat16
ALU = mybir.AluOpType
AF = mybir.ActivationFunctionType
AX = mybir.AxisListType

D = 1024
H = 16
DH = 64
FF = 2816
NFC = FF // 128
NGRP = 3
WINDOWS = (128, 512, 2048)
DILS = (1, 4, 16)
PAST = 2048
NCORES = 8
HS = 68
EXPM05 = math.exp(-0.5)
DEBUG_SITES = False
SITES = {}


class Tok:
    __slots__ = ("w", "r")

    def __init__(self):
        self.w = None
        self.r = []


class Op:
    __slots__ = ("eng", "fn", "deps", "need_inc", "cnt", "is_dma", "dsem", "dval", "prev_dval", "site",
                 "sdeps", "cost", "idx", "succ", "indeg", "ready", "fin")


class V:
    def __init__(self, ap, toks=None):
        self.ap = ap
        self.toks = toks if toks is not None else [Tok()]

    def __getitem__(self, k):
        return V(self.ap[k], self.toks)

    def re(self, pat, **kw):
        return V(self.ap.rearrange(pat, **kw), self.toks)

    def bc(self, axis, shape):
        return V(self.ap.unsqueeze(axis).to_broadcast(list(shape)), self.toks)

    def bitcast(self, dt):
        return V(self.ap.bitcast(dt), self.toks)


def _ap(x):
    return x.ap if isinstance(x, V) else x


def _fsz(x):
    sh = _ap(x).shape
    n = 1
    for d_ in sh[1:]:
        n *= int(d_)
    return n


def _toks(*xs):
    r = []
    for x in xs:
        if isinstance(x, V):
            r.extend(x.toks)
    return r


class Prog:
    ENGS = ("pe", "act", "dve", "pool", "sp")
    NDSEM = 14

    def __init__(self, nc, stack):
        self.nc = nc
        self.ops = {e: [] for e in self.ENGS}
        self.esem = {e: stack.enter_context(nc.semaphore("es_" + e)) for e in ("pe", "act", "dve", "pool")}
        self.dsems = {e: [stack.enter_context(nc.semaphore("ds_%s%d" % (e, i))) for i in range(self.NDSEM)]
                      for e in ("sp", "act", "pool")}
        self.ndma = {e: 0 for e in ("sp", "act", "pool")}
        self.regions = []
        self.cur = []
        self.nins = 0

    def add(self, eng, fn, rd=(), wr=(), dma=False, extra=(), cost=100.0):
        o = Op()
        o.eng = eng
        o.fn = fn
        o.site = None
        if DEBUG_SITES:
            import traceback
            o.site = [f.lineno for f in traceback.extract_stack(limit=5)[:-1]]
        o.need_inc = False
        o.cnt = 0
        o.is_dma = dma
        o.cost = cost
        o.idx = self.nins
        deps = list(extra)
        for t in rd:
            if t.w is not None:
                deps.append(t.w)
        for t in wr:
            if t.w is not None:
                deps.append(t.w)
            deps.extend(t.r)
        dd = []
        sd = []
        seen = set()
        for d in deps:
            if id(d) in seen or d is o:
                continue
            seen.add(id(d))
            sd.append(d)
            if (not d.is_dma) and d.eng == "pe" and eng == "pe" and not dma:
                continue
            dd.append(d)
        o.deps = dd
        o.sdeps = sd
        for t in rd:
            t.r.append(o)
        for t in wr:
            t.w = o
            t.r = []
        self.cur.append(o)
        self.nins += 1
        return o

    def barrier(self):
        if self.cur:
            self.regions.append(self.cur)
            self.cur = []

    def mm(self, out, lhsT, rhs, start=True, stop=True, extra_rd=()):
        o, l, r = _ap(out), _ap(lhsT), _ap(rhs)
        c_ = max(_fsz(rhs), 64) / 1.6 + 20.0
        if l.dtype == F32:
            c_ *= 4.0
        return self.add("pe", lambda e: e.matmul(o, lhsT=l, rhs=r, start=start, stop=stop),
                        rd=_toks(lhsT, rhs) + list(extra_rd), wr=_toks(out), cost=c_)

    def tr(self, out, in_, ident):
        o, i, d = _ap(out), _ap(in_), _ap(ident)
        c_ = 110.0 * (4.0 if i.dtype == F32 else 1.0)
        return self.add("pe", lambda e: e.transpose(out=o, in_=i, identity=d), rd=_toks(in_, ident), wr=_toks(out), cost=c_)

    def tt(self, eng, out, in0, in1, op):
        o, a, b = _ap(out), _ap(in0), _ap(in1)
        return self.add(eng, lambda e: e.tensor_tensor(out=o, in0=a, in1=b, op=op), rd=_toks(in0, in1), wr=_toks(out),
                        cost=self.ecost(eng, out))

    def ts(self, eng, out, in0, s1, op0, s2=None, op1=None):
        o, a = _ap(out), _ap(in0)
        s1a, s2a = _ap(s1), _ap(s2)
        if op1 is None:
            fn = lambda e: e.tensor_scalar(out=o, in0=a, scalar1=s1a, scalar2=None, op0=op0)
        else:
            fn = lambda e: e.tensor_scalar(out=o, in0=a, scalar1=s1a, scalar2=s2a, op0=op0, op1=op1)
        return self.add(eng, fn, rd=_toks(in0, s1, s2), wr=_toks(out), cost=self.ecost(eng, out))

    def stt(self, eng, out, in0, scalar, in1, op0, op1):
        o, a, s, b = _ap(out), _ap(in0), _ap(scalar), _ap(in1)
        return self.add(eng, lambda e: e.scalar_tensor_tensor(out=o, in0=a, scalar=s, in1=b, op0=op0, op1=op1),
                        rd=_toks(in0, scalar, in1), wr=_toks(out), cost=self.ecost(eng, out))

    def copy(self, eng, out, in_):
        o, i = _ap(out), _ap(in_)
        if eng == "act":
            fn = lambda e: e.copy(out=o, in_=i)
        else:
            fn = lambda e: e.tensor_copy(out=o, in_=i)
        return self.add(eng, fn, rd=_toks(in_), wr=_toks(out), cost=self.ecost(eng, out))

    def ecost(self, eng, out):
        n = _fsz(out)
        if eng == "act":
            return 230.0 + 0.85 * n
        if eng == "pool":
            return 120.0 + 2.0 * n
        return 70.0 + 1.05 * n

    def act(self, out, in_, func, scale=1.0, bias=None, accum=None):
        o, i, b, ac = _ap(out), _ap(in_), _ap(bias), _ap(accum)
        kw = {}
        if bias is not None:
            kw["bias"] = b
        if accum is not None:
            kw["accum_out"] = ac
        sc = _ap(scale)
        return self.add("act", lambda e: e.activation(out=o, in_=i, func=func, scale=sc, **kw),
                        rd=_toks(in_, bias, scale), wr=_toks(out, accum), cost=self.ecost("act", out))

    def red(self, eng, out, in_, op=None, axis=None):
        o, i = _ap(out), _ap(in_)
        op = op or ALU.add
        axis = axis or AX.X
        return self.add(eng, lambda e: e.tensor_reduce(out=o, in_=i, axis=axis, op=op), rd=_toks(in_), wr=_toks(out),
                        cost=self.ecost(eng, in_))

    def memset(self, eng, out, val):
        o = _ap(out)
        return self.add(eng, lambda e: e.memset(o, val), wr=_toks(out), cost=self.ecost(eng, out))

    def dma(self, q, out, in_, **kw):
        o, i = _ap(out), _ap(in_)
        nb_ = _fsz(out) * int(o.shape[0]) * (4 if o.dtype == F32 else 2)
        return self.add(q, lambda e: e.dma_start(out=o, in_=i, **kw), rd=_toks(in_), wr=_toks(out), dma=True,
                        cost=2200.0 + nb_ / 150.0)

    LAT = 300.0

    def schedule_region(self, ops):
        engs = self.ENGS
        pend = {e: [] for e in engs}
        inreg = set(id(o) for o in ops)
        for o in ops:
            o.succ = []
            o.indeg = 0
            o.ready = 0.0
        for o in ops:
            for d in o.sdeps:
                if id(d) in inreg:
                    d.succ.append(o)
                    o.indeg += 1
            pend[o.eng].append(o)
        free = {e: 0.0 for e in engs}
        out = {e: [] for e in engs}
        n = len(ops)
        W = int(os.environ.get("BASS_W", "96"))
        LAT = self.LAT
        PRIO = os.environ.get("BASS_PRIO", "1") == "1"
        for o in reversed(ops):
            b_ = 0.0
            for s_ in o.succ:
                v_ = s_.cnt + (0.0 if (s_.eng == o.eng and not o.is_dma) else LAT)
                if v_ > b_:
                    b_ = v_
            o.cnt = b_ + o.cost
        SLACK = 40.0
        while n > 0:
            bst = None
            bo = None
            be = None
            bi = 0
            for e in engs:
                lst = pend[e]
                if not lst:
                    continue
                fe = free[e]
                lim = W if len(lst) > W else len(lst)
                cst = None
                co = None
                ci = 0
                if PRIO:
                    mst = None
                    for i in range(lim):
                        o = lst[i]
                        if o.indeg == 0:
                            st = o.ready if o.ready > fe else fe
                            if mst is None or st < mst:
                                mst = st
                    if mst is None:
                        continue
                    for i in range(lim):
                        o = lst[i]
                        if o.indeg == 0:
                            st = o.ready if o.ready > fe else fe
                            if st <= mst + SLACK and (co is None or o.cnt > co.cnt):
                                cst, co, ci = st, o, i
                else:
                    for i in range(lim):
                        o = lst[i]
                        if o.indeg == 0:
                            st = o.ready if o.ready > fe else fe
                            if co is None or st < cst:
                                cst, co, ci = st, o, i
                            if st <= fe:
                                break
                    if co is None:
                        continue
                if bo is None or cst < bst or (cst == bst and co.idx < bo.idx):
                    bst, bo, be, bi = cst, co, e, ci
            o = bo
            del pend[be][bi]
            out[be].append(o)
            if o.is_dma:
                free[be] = bst + 120.0
                fin = bst + o.cost
            elif be == "pe":
                free[be] = bst + o.cost
                fin = bst + o.cost + 120.0
            else:
                free[be] = bst + o.cost
                fin = free[be]
            o.fin = (bst, fin, o.ready >= bst - 1e-6)
            for s_ in o.succ:
                s_.indeg -= 1
                r = fin if (s_.eng == be and not o.is_dma) else fin + LAT
                if r > s_.ready:
                    s_.ready = r
                    s_.dsem = o
            n -= 1
        if os.environ.get("BASS_SCHED_STATS"):
            busy = {e: sum(o.cost if not o.is_dma else 120.0 for o in out[e]) for e in engs}
            print("SCHED region: n=%d makespan=%.0f us busy(us)=%s" % (len(ops), max(free.values()) / 1e3, {e: round(v / 1e3) for e, v in busy.items()}))
        return out

    def finalize(self):
        if self.cur:
            self.regions.append(self.cur)
            self.cur = []
        sched = os.environ.get("BASS_NOSCHED", "") == ""
        for ops in self.regions:
            if sched:
                out = self.schedule_region(ops)
            else:
                out = {e: [o for o in ops if o.eng == e] for e in self.ENGS}
            last = {}
            dmas = []
            for e in self.ENGS:
                for o in out[e]:
                    if o.is_dma:
                        dmas.append(o)
                    else:
                        last[e] = o
                self.ops[e].extend(out[e])
            for e in self.ENGS:
                bo = Op()
                bo.eng = e
                bo.fn = None
                bo.site = None
                bo.need_inc = False
                bo.cnt = 0
                bo.is_dma = False
                bo.deps = [o for k, o in last.items() if k != e] + dmas
                self.ops[e].append(bo)
        for e in self.ndma:
            i = 0
            for o in self.ops[e]:
                if o.is_dma:
                    o.dsem = self.dsems[e][i % self.NDSEM]
                    o.dval = 16 * (i // self.NDSEM + 1)
                    o.prev_dval = o.dval - 16
                    i += 1
            self.ndma[e] = i
        for e in self.ENGS:
            for o in self.ops[e]:
                for d in o.deps:
                    if not d.is_dma:
                        d.need_inc = True

    def emit(self):
        nc = self.nc
        self.finalize()
        for e in ("pe", "act", "dve", "pool"):
            c = 0
            for o in self.ops[e]:
                if o.is_dma or o.fn is None:
                    o.cnt = c
                    continue
                if o.need_inc:
                    c += 1
                o.cnt = c
        prog = self

        def run(e, eng):
            known = {}
            for o in prog.ops[e]:
                waits = []
                for d in o.deps:
                    if d.is_dma:
                        waits.append((d.dsem, d.dval))
                    else:
                        waits.append((prog.esem[d.eng], d.cnt))
                if o.is_dma and o.prev_dval > 0:
                    waits.append((o.dsem, o.prev_dval))
                for (s, v) in waits:
                    k = id(s)
                    if known.get(k, 0) < v:
                        eng.wait_ge(s, v)
                        known[k] = v
                if o.fn is None:
                    continue
                ins = o.fn(eng)
                if DEBUG_SITES:
                    SITES[str(ins.ins.name)] = o.site
                if o.is_dma:
                    ins.then_inc(o.dsem, 16)
                elif o.need_inc:
                    ins.then_inc(prog.esem[e], 1)
            if e in prog.ndma:
                n = prog.ndma[e]
                for i in range(min(n, prog.NDSEM)):
                    uses = (n - 1 - i) // prog.NDSEM + 1
                    s = prog.dsems[e][i]
                    if known.get(id(s), 0) < 16 * uses:
                        eng.wait_ge(s, 16 * uses)

        with nc.Block() as block:
            @block.sync
            def _(eng):
                run("sp", eng)

            @block.tensor
            def _(eng):
                run("pe", eng)

            @block.scalar
            def _(eng):
                run("act", eng)

            @block.vector
            def _(eng):
                run("dve", eng)

            @block.gpsimd
            def _(eng):
                run("pool", eng)


class Arena:
    def __init__(self, big, nbytes):
        self.big = big
        self.nbytes = nbytes
        self.off = 0
        self.marks = []

    def mark(self):
        self.marks.append(self.off)

    def release(self):
        self.off = self.marks.pop()

    def alloc(self, cols, dt, parts=128):
        esz = 4 if dt == F32 else 2
        self.off = (self.off + 63) // 64 * 64
        nb = cols * esz
        assert self.off + nb <= self.nbytes, ("arena overflow", self.off, nb, self.nbytes)
        ap = self.big[0:parts, self.off // 2:(self.off + nb) // 2]
        if dt == F32:
            ap = ap.bitcast(F32)
        self.off += nb
        return V(ap)


CF = {}
CB = {}


def _layout(spec, table):
    off = 0
    for name, n in spec:
        table[name] = (off, n)
        off += n
    return off


_CF_SPEC = [("ident", 128), ("m_sl", 128), ("m_su", 128), ("m_ui", 128), ("m_blk", 128), ("maskc", 8),
            ("negc_p", 1), ("negc_s", 1), ("rowm_s", 1), ("one", 1), ("eps6", 1), ("eps24", 1), ("gneps", 1),
            ("rope_s", 128), ("ge_j", 4), ("le_j", 4), ("eq_j", 4)]
_CB_SPEC = [("ident", 128), ("am_ge", 128), ("am_le", 128)]
NCF = _layout(_CF_SPEC, CF)
NCB = _layout(_CB_SPEC, CB)


def host_consts(T):
    NT = T // 128
    p = np.arange(128)
    row = p[:, None]
    col = p[None, :]
    same = (row // 16) == (col // 16)
    cf = np.zeros((128, NCF), np.float32)

    def put(name, arr):
        o, n = CF[name]
        cf[:, o:o + n] = arr

    put("ident", np.eye(128))
    put("m_sl", same & (col < row))
    put("m_su", same & (row < col))
    put("m_ui", same & (row <= col))
    put("m_blk", same)
    put("maskc", (row // 16) == np.arange(8)[None, :])
    slot = p % 16
    rowm = ((slot >= 2) & (slot < 6)).astype(np.float32)[:, None]
    put("negc_p", -EXPM05)
    put("negc_s", -EXPM05 * rowm)
    put("rowm_s", rowm)
    put("one", 1.0)
    put("eps6", 1e-6)
    put("eps24", 1e-24)
    put("gneps", 64e-5)
    jj = np.arange(4)[None, :]
    put("ge_j", row >= jj)
    put("le_j", row <= jj)
    put("eq_j", row == jj)
    half = DH // 2
    inv = np.exp(-math.log(10000.0) * np.arange(half, dtype=np.float32) * 2.0 / DH).astype(np.float32)
    pos_s = (PAST + np.clip(slot - 2, 0, 3)).astype(np.float32)
    ang = pos_s[:, None] * inv[None, :]
    put("rope_s", np.concatenate([np.cos(ang), np.cos(ang), np.sin(ang), np.sin(ang)], 1))
    pos = (np.arange(NT)[None, :] * 128 + p[:, None]).astype(np.float32)
    angp = pos[:, :, None] * inv[None, None, :]
    rope_p = np.concatenate([np.cos(angp), np.cos(angp), np.sin(angp), np.sin(angp)], 2).astype(np.float32).reshape(128, NT * 128)
    cb = np.zeros((128, NCB), np.float32)

    def putb(name, arr):
        o, n = CB[name]
        cb[:, o:o + n] = arr

    putb("ident", np.eye(128))
    putb("am_ge", row >= col)
    putb("am_le", row <= col)
    return cf, cb.astype(ml_dtypes.bfloat16), rope_p


class Builder:
    def __init__(self, T, NS=16, phases="ABCD", debug=False):
        self.T = T
        self.NS = NS
        self.NT = T // 128
        self.NST = NS * 16 // 128
        self.NTT = self.NT + self.NST
        self.phases = phases
        self.debug = debug
        self.nc = bass.Bass("TRN2", target_bir_lowering=False)
        self.dr = {}

    def din(self, name, shape, dt=F32):
        self.dr[name] = V(self.nc.dram_tensor(name, list(shape), dt, kind="ExternalInput").ap())
        return self.dr[name]

    def dout(self, name, shape):
        self.dr[name] = V(self.nc.dram_tensor(name, list(shape), F32, kind="ExternalOutput").ap())
        return self.dr[name]

    def dscr(self, name, shape, dt=F32, dbg=False):
        if dbg and self.debug:
            return self.dout(name, shape)
        self.dr[name] = V(self.nc.dram_tensor(name, list(shape), dt).ap())
        return self.dr[name]

    def declare(self):
        T, NS, NTT = self.T, self.NS, self.NTT
        d = self.din
        d("xp", [T, D]); d("xs", [NS * 4, D]); d("st_shift", [NS, D]); d("st_wkv", [NS, H, DH, DH])
        for g, w in enumerate(WINDOWS):
            d("ck%d" % g, [NS, w, D]); d("cv%d" % g, [NS, w, D])
        d("st_conv", [2, NS, 2, FF])
        d("norm_mix", [2, D]); d("norm_ffn", [2, D]); d("norm_final", [1, D])
        d("rwkv_mu", [6, D]); d("rwkv_w_rkv", [3, D, D]); d("rwkv_w0", [1, D]); d("rwkv_w1", [D, 64]); d("rwkv_w2", [64, D])
        d("rwkv_a0", [1, D]); d("rwkv_a1", [D, 64]); d("rwkv_a2", [64, D]); d("rwkv_g1", [D, 128]); d("rwkv_g2", [128, D])
        d("rwkv_k_k", [1, D]); d("rwkv_k_a", [1, D]); d("rwkv_r_k", [1, D]); d("rwkv_ln_w", [1, D]); d("rwkv_ln_b", [1, D])
        d("rwkv_w_o", [D, D]); d("attn_w_in", [D, 9 * D]); d("attn_w_o", [D, D])
        d("ffn_w_up", [2, D, 2 * FF]); d("ffn_conv_w", [2, 3, FF]); d("ffn_conv_b", [2, FF]); d("ffn_w_down", [2, FF, D])
        d("cf", [128, NCF]); d("cb", [128, NCB], BF16); d("rope_p", [128, self.NT * 128])
        o = self.dout
        o("y_p", [T, D]); o("y_s", [NS * 4, D]); o("p_shift", [1, D]); o("p_wkv", [H, DH, DH])
        for g, w in enumerate(WINDOWS):
            o("p_k%d" % g, [min(w, T), D]); o("p_v%d" % g, [min(w, T), D])
        o("p_conv", [2, 2, FF]); o("s_shift", [NS, D]); o("s_wkv", [NS, H, DH, DH])
        for g in range(3):
            o("s_k%d" % g, [NS * 4, D]); o("s_v%d" % g, [NS * 4, D])
        o("s_conv", [2, NS, 2, FF])
        s = self.dscr
        s("xs_pad", [self.NST * 128, D])
        for nm, prod, cons in (("R1", "A", "B"), ("R2", "B", "C"), ("R3", "C", "D")):
            if prod in self.phases:
                s(nm, [NTT * 128, D], dbg=True)
            elif cons in self.phases:
                d(nm, [NTT * 128, D])
            else:
                s(nm, [NTT * 128, D])
        s("ys_pad", [self.NST * 128, D])
        if self.debug:
            o("dbg_y", [NTT * 128, D]); o("dbg_g", [NTT * 128, D]); o("dbg_yn", [NTT * 128, D])

    def res_src(self, name, i):
        if name == "R0":
            if i < self.NT:
                return self.dr["xp"][i * 128:(i + 1) * 128, :]
            return self.dr["xs_pad"][(i - self.NT) * 128:(i - self.NT + 1) * 128, :]
        return self.dr[name][i * 128:(i + 1) * 128, :]

    def build(self):
        nc = self.nc
        self.declare()
        with ExitStack() as st:
            self.P = P = Prog(nc, st)
            big = st.enter_context(nc.sbuf_tensor("arena", [128, 106000], BF16))
            self.A = A = Arena(big, 212000)
            pp = [st.enter_context(nc.psum_tensor("pp%d" % i, [128, 1024], F32)) for i in range(4)]
            self.bank = []
            for i in range(4):
                for h in range(2):
                    self.bank.append(V(pp[i][:, h * 512:(h + 1) * 512]))
            self.dbank = [V(pp[i][:, :], toks=self.bank[2 * i].toks + self.bank[2 * i + 1].toks) for i in range(4)]
            self.cf = A.alloc(NCF, F32)
            self.cb = A.alloc(NCB, BF16)
            P.dma("sp", self.cf, self.dr["cf"])
            P.dma("sp", self.cb, self.dr["cb"])
            self.zero = A.alloc(D, F32)
            P.memset("pool", self.zero, 0.0)
            for stl in range(self.NST):
                P.dma("sp", self.dr["xs_pad"][stl * 128:(stl + 1) * 128, :], self.zero)
            P.dma("sp", self.dr["xs_pad"].re("(s q) d -> s q d", q=16)[:, 2:6, :],
                  self.dr["xs"].re("(s j) d -> s j d", j=4))
            P.barrier()
            if "A" in self.phases:
                A.mark(); self.phase_rwkv(); A.release(); P.barrier()
            if "B" in self.phases:
                A.mark(); self.phase_ffn(0, "R1", "R2", final=False); A.release(); P.barrier()
            if "C" in self.phases:
                A.mark(); self.phase_attn(); A.release(); P.barrier()
            if "D" in self.phases:
                A.mark(); self.phase_ffn(1, "R3", None, final=True); A.release(); P.barrier()
            P.emit()
        return nc

    def c32(self, name):
        o, n = CF[name]
        return self.cf[:, o:o + n]

    def c16(self, name):
        o, n = CB[name]
        return self.cb[:, o:o + n]

    def load_bcast(self, dst, src_row):
        n = src_row.ap.shape[-1]
        self.P.dma("sp", dst, V(src_row.ap.to_broadcast([128, n]), src_row.toks))

    def rmsnorm(self, x, gbc, out, junk, small):
        P = self.P
        P.act(junk, x, AF.Square)
        P.red("dve", small[:, 0:1], junk)
        P.act(small[:, 1:2], small[:, 0:1], AF.Ln, scale=1.0 / D, bias=self.c32("eps6"))
        P.act(small[:, 2:3], small[:, 1:2], AF.Exp, scale=-0.5)
        P.stt("dve", out, x, small[:, 2:3], gbc, ALU.mult, ALU.mult)

    def transpose_bf(self, src, dstT, col0, bank):
        P = self.P
        pb = bank.bitcast(BF16)
        for kc in range(8):
            P.tr(pb[:, kc * 128:(kc + 1) * 128], src[:, kc * 128:(kc + 1) * 128], self.c16("ident"))
        P.copy("act", dstT[:, :, col0:col0 + 128], pb.re("p (k t) -> p k t", k=8))

    def phase_ffn(self, layer, rin, rout, final):
        P, A, NT, NST = self.P, self.A, self.NT, self.NST
        dr = self.dr
        wup = A.alloc(8 * 2 * FF, BF16)
        wdn = A.alloc(NFC * D, BF16)
        wup3 = wup.re("p (k n) -> p k n", k=8)
        wdn3 = wdn.re("p (k n) -> p k n", k=NFC)
        for k in range(8):
            P.dma("pool", wup3[:, k, :], dr["ffn_w_up"][layer, k * 128:(k + 1) * 128, :])
        for k0 in range(0, NFC, 6):
            k1 = min(NFC, k0 + 6)
            P.dma("pool", wdn3[:, k0:k1, :], dr["ffn_w_down"][layer, k0 * 128:k1 * 128, :].re("(k p) n -> p k n", p=128))
        import os
        STOP = os.environ.get("FFN_STOP", "")
        if STOP == "w":
            return
        gbc = A.alloc(D, F32)
        self.load_bcast(gbc, dr["norm_ffn"][layer:layer + 1, :])
        gfin = None
        if final:
            gfin = A.alloc(D, F32)
            self.load_bcast(gfin, dr["norm_final"][0:1, :])
        if STOP == "bc":
            return
        cw = A.alloc(NFC * 4, F32)
        cw3 = cw.re("p (c j) -> p c j", j=4)
        tmpF = A.alloc(512, F32)
        NSJ = self.NS * 2
        stT = A.alloc(NFC * NSJ, F32)
        stT3 = stT.re("p (c n) -> p c n", c=NFC)
        bk = self.bank[0]
        bk2 = self.bank[1]
        for c0 in range(0, NFC, 4):
            c1 = min(NFC, c0 + 4)
            w = (c1 - c0) * 128
            P.dma("sp", tmpF[0:3, 0:w], dr["ffn_conv_w"][layer, :, c0 * 128:c1 * 128])
            P.dma("sp", tmpF[3:4, 0:w], dr["ffn_conv_b"][layer:layer + 1, c0 * 128:c1 * 128])
            for c in range(c0, c1):
                P.tr(bk[:, c * 4:(c + 1) * 4], tmpF[0:4, (c - c0) * 128:(c - c0 + 1) * 128], self.c32("ident")[0:4, 0:4])
        P.copy("act", cw, bk[:, 0:NFC * 4])
        if STOP == "cw":
            return
        for c0 in range(0, NFC, 4):
            c1 = min(NFC, c0 + 4)
            w = (c1 - c0) * 128
            P.dma("sp", tmpF[0:NSJ, 0:w], dr["st_conv"][layer].re("s j f -> (s j) f")[:, c0 * 128:c1 * 128])
            for c in range(c0, c1):
                P.tr(bk2[:, (c - c0) * NSJ:(c - c0 + 1) * NSJ], tmpF[0:NSJ, (c - c0) * 128:(c - c0 + 1) * 128], self.c32("ident")[0:NSJ, 0:NSJ])
            P.copy("act", stT3[:, c0:c1, :], bk2[:, 0:(c1 - c0) * NSJ].re("p (c n) -> p c n", n=NSJ))
        if STOP == "st":
            return
        carry = A.alloc(NFC * 2, F32)
        carry3 = carry.re("p (c j) -> p c j", j=2)
        P.memset("pool", carry, 0.0)
        gsave = A.alloc(NFC * (NSJ + 2), F32)
        gsave3 = gsave.re("p (c n) -> p c n", c=NFC)
        xnT = A.alloc(8 * 512, BF16)
        xnT3 = xnT.re("p (k t) -> p k t", k=8)
        hT = A.alloc(NFC * 512, BF16)
        hT3 = hT.re("p (c t) -> p c t", c=NFC)
        xr = [A.alloc(D, F32) for _ in range(2)]
        xn = [A.alloc(D, BF16) for _ in range(2)]
        junk = hT[:, 0:2 * D].bitcast(F32)
        small = [A.alloc(4, F32) for _ in range(2)]
        gext = [A.alloc(514, F32) for _ in range(2)]
        acc = [A.alloc(512, F32) for _ in range(2)]
        yt = xr
        ysm = [A.alloc(4, F32) for _ in range(2)]
        macros = []
        i = 0
        while i < NT:
            n = min(4, NT - i)
            macros.append((i, n, False))
            i += n
        macros.append((NT, NST, True))
        cnt = 0
        for (t0, n, is_s) in macros:
            ntok = n * 128
            for j in range(n):
                b = cnt % 2
                cnt += 1
                P.dma("sp", xr[b], self.res_src(rin, t0 + j))
                if STOP == "L0":
                    continue
                self.rmsnorm(xr[b], gbc, xn[b], junk, small[b])
                if STOP == "L1":
                    continue
                self.transpose_bf(xn[b], xnT3, j * 128, self.bank[1])
            if STOP in ("L0", "L1", "L2"):
                continue
            for c in range(NFC):
                b = c % 2
                pg = self.bank[2 + b]
                pv = self.bank[4 + b]
                for k in range(8):
                    P.mm(pg[:, 0:ntok], wup3[:, k, c * 128:(c + 1) * 128], xnT3[:, k, 0:ntok], start=(k == 0), stop=(k == 7))
                for k in range(8):
                    P.mm(pv[:, 0:ntok], wup3[:, k, FF + c * 128:FF + (c + 1) * 128], xnT3[:, k, 0:ntok], start=(k == 0), stop=(k == 7))
                ge = gext[b]
                if STOP == "L3a":
                    continue
                P.copy("pool", ge[:, 0:2], carry3[:, c, :])
                P.copy("act", ge[:, 2:2 + ntok], pg[:, 0:ntok])
                if STOP == "L3":
                    continue
                if is_s:
                    P.copy("pool", ge[:, 2:2 + ntok].re("p (s q) -> p s q", q=16)[:, :, 0:2],
                           stT3[:, c, :].re("p (s j) -> p s j", j=2))
                a = acc[b]
                if STOP == "L3b":
                    continue
                P.ts("dve", a[:, 0:ntok], ge[:, 2:2 + ntok], cw3[:, c, 2:3], ALU.mult, cw3[:, c, 3:4], ALU.add)
                if STOP == "L3c":
                    continue
                P.stt("dve", a[:, 0:ntok], ge[:, 1:1 + ntok], cw3[:, c, 1:2], a[:, 0:ntok], ALU.mult, ALU.add)
                P.stt("dve", a[:, 0:ntok], ge[:, 0:ntok], cw3[:, c, 0:1], a[:, 0:ntok], ALU.mult, ALU.add)
                if STOP == "L4":
                    continue
                P.act(a[:, 0:ntok], a[:, 0:ntok], AF.Silu)
                P.tt("dve", hT3[:, c, 0:ntok], a[:, 0:ntok], pv[:, 0:ntok], ALU.mult)
                if is_s:
                    P.copy("pool", gsave3[:, c, 0:NSJ].re("p (s j) -> p s j", j=2),
                           ge[:, 2:2 + ntok].re("p (s q) -> p s q", q=16)[:, :, 4:6])
                else:
                    P.copy("pool", carry3[:, c, :], ge[:, ntok:ntok + 2])
                    if t0 + n == NT:
                        P.copy("pool", gsave3[:, c, NSJ:NSJ + 2], ge[:, ntok:ntok + 2])
            if STOP in ("L3a", "L3", "L3b", "L3c", "L4", "L6"):
                continue
            for j in range(n):
                b = cnt % 2
                cnt += 1
                ti = t0 + j
                P.dma("sp", xr[b], self.res_src(rin, ti))
                for hh in range(2):
                    pd = self.bank[6 + hh]
                    for c in range(NFC):
                        P.mm(pd, hT3[:, c, j * 128:(j + 1) * 128], wdn3[:, c, hh * 512:(hh + 1) * 512], start=(c == 0), stop=(c == NFC - 1))
                    P.tt("dve", yt[b][:, hh * 512:(hh + 1) * 512], pd, xr[b][:, hh * 512:(hh + 1) * 512], ALU.add)
                if not final:
                    P.dma("sp", self.res_src(rout, ti), yt[b])
                else:
                    self.rmsnorm(yt[b], gfin, yt[b], xnT[:, 0:2 * D].bitcast(F32), ysm[b])
                    if ti < NT:
                        P.dma("sp", dr["y_p"][ti * 128:(ti + 1) * 128, :], yt[b])
                    else:
                        P.dma("sp", dr["ys_pad"][(ti - NT) * 128:(ti - NT + 1) * 128, :], yt[b])
        if STOP == "main":
            return
        if final:
            P.dma("sp", dr["y_s"].re("(s j) d -> s j d", j=4), dr["ys_pad"].re("(s q) d -> s q d", q=16)[:, 2:6, :])
        NG = NSJ + 2
        sc_rows = dr["s_conv"][layer].re("s j f -> (s j) f")
        for c0 in range(0, NFC, 4):
            c1 = min(NFC, c0 + 4)
            w = (c1 - c0) * 128
            bk = self.bank[0]
            for c in range(c0, c1):
                P.tr(bk[0:NG, (c - c0) * 128:(c - c0 + 1) * 128], gsave3[:, c, :], self.c32("ident"))
            P.copy("act", tmpF[0:NG, 0:w], bk[0:NG, 0:w])
            P.dma("sp", sc_rows[:, c0 * 128:c1 * 128], tmpF[0:NSJ, 0:w])
            P.dma("sp", dr["p_conv"][layer][:, c0 * 128:c1 * 128], tmpF[NSJ:NSJ + 2, 0:w])


    def nb(self):
        self._nb = (getattr(self, "_nb", -1) + 1) % 6
        return self.bank[self._nb]

    def phase_rwkv(self):
        P, A, NT, NST, NS = self.P, self.A, self.NT, self.NST, self.NS
        dr = self.dr
        ident = self.c32("ident")
        identb = self.c16("ident")
        m_sl, m_su, m_ui, m_blk, maskc = (self.c32(k) for k in ("m_sl", "m_su", "m_ui", "m_blk", "maskc"))
        o_su = CF["m_su"][0]
        mask2 = self.cf[:, o_su:o_su + 256]
        wrkv = A.alloc(3 * 8 * D, BF16); wrkv4 = wrkv.re("p (q k n) -> p q k n", q=3, k=8)
        wo = A.alloc(8 * D, BF16); wo3 = wo.re("p (k n) -> p k n", k=8)
        w1 = A.alloc(8 * 64, BF16); w13 = w1.re("p (k n) -> p k n", k=8)
        a1 = A.alloc(8 * 64, BF16); a13 = a1.re("p (k n) -> p k n", k=8)
        g1 = A.alloc(8 * 128, BF16); g13 = g1.re("p (k n) -> p k n", k=8)
        w2 = A.alloc(D, BF16); a2 = A.alloc(D, BF16); g2 = A.alloc(D, BF16)
        for q in range(3):
            P.dma("pool", wrkv4[:, q, :, :], dr["rwkv_w_rkv"][q].re("(k p) n -> p k n", p=128))
        P.dma("pool", wo3, dr["rwkv_w_o"].re("(k p) n -> p k n", p=128))
        P.dma("pool", w13, dr["rwkv_w1"].re("(k p) n -> p k n", p=128))
        P.dma("pool", a13, dr["rwkv_a1"].re("(k p) n -> p k n", p=128))
        P.dma("pool", g13, dr["rwkv_g1"].re("(k p) n -> p k n", p=128))
        P.dma("pool", w2[0:64, :], dr["rwkv_w2"])
        P.dma("pool", a2[0:64, :], dr["rwkv_a2"])
        P.dma("pool", g2, dr["rwkv_g2"])
        bc = {}
        for nm, src in (("gmix", dr["norm_mix"][0:1, :]), ("w0", dr["rwkv_w0"]), ("a0", dr["rwkv_a0"]), ("kk", dr["rwkv_k_k"]),
                        ("ka", dr["rwkv_k_a"]), ("rk", dr["rwkv_r_k"]), ("lnw", dr["rwkv_ln_w"]), ("lnb", dr["rwkv_ln_b"])):
            bc[nm] = A.alloc(D, F32)
            self.load_bcast(bc[nm], src)
        FW = 8 * 129
        Ft = [A.alloc(FW, F32) for _ in range(8)]
        F = [f[:, 0:D] for f in Ft]
        mu_sb = A.alloc(48, F32); mu3 = mu_sb.re("p (q k) -> p q k", q=6)
        P.dma("sp", F[0][0:6, :], dr["rwkv_mu"])
        bk = self.nb()
        for k in range(8):
            P.tr(bk[:, k * 6:(k + 1) * 6], F[0][0:6, k * 128:(k + 1) * 128], ident[0:6, 0:6])
        P.copy("act", mu_sb.re("p (q k) -> p k q", q=6), bk[:, 0:48].re("p (k q) -> p k q", q=6))
        carry = A.alloc(8, F32); P.memset("pool", carry, 0.0)
        small = A.alloc(64, F32)
        PcT = A.alloc(64, F32); PcT3 = PcT.re("p (a c) -> p a c", a=8)
        prodb = A.alloc(6 * 8 * 128, BF16)
        xsT = prodb.re("p (q k t) -> p q k t", q=6, k=8)
        At, Bt, Kt, Rt, Bp = (prodb[:, i * D:(i + 1) * D] for i in range(5))
        yb = prodb[:, 0:D]
        yT = prodb[:, D:2 * D]; yT3 = yT.re("p (k t) -> p k t", k=8)
        QT = A.alloc(8 * 4 * 128, BF16); QT4 = QT.re("p (a q t) -> p a q t", a=8, q=4)
        Vblk = A.alloc(8 * 2 * 128, BF16); Vblk4 = Vblk.re("p (a h n) -> p a h n", a=8, h=2)
        P.memset("pool", Vblk, 0.0)
        S32p = [A.alloc(128, F32) for _ in range(8)]
        Sbfp = [A.alloc(128, BF16) for _ in range(8)]
        for t_ in S32p + Sbfp:
            P.memset("pool", t_, 0.0)
        fence_scr = A.alloc(2, F32)
        alias_toks = []

        def sub(fk, off, cols, dt):
            if dt == BF16:
                ap = Ft[fk].ap.bitcast(BF16)[:, off // 2:off // 2 + cols]
            else:
                ap = Ft[fk].ap[:, off // 4:off // 4 + cols]
            v_ = V(ap, toks=[Tok()])
            alias_toks.extend(v_.toks)
            return v_
        XBs, MBs, ARKs, P2s, P4s, P8s, NHs, Bblks, tmpDs = [], [], [], [], [], [], [], [], []
        for hs in range(4):
            if hs < 2:
                XBs.append((A.alloc(320, BF16), A.alloc(320, BF16)))
                MBs.append(A.alloc(256, BF16)); ARKs.append(A.alloc(128, BF16))
                P2s.append(A.alloc(256, BF16)); P4s.append(A.alloc(256, BF16)); P8s.append(A.alloc(128, BF16))
                NHs.append(A.alloc(128, BF16)); Bblks.append(A.alloc(512, BF16)); tmpDs.append(A.alloc(64, F32))
            else:
                fk = 1 if hs == 2 else 2
                XBs.append((sub(fk, 0, 320, BF16), sub(fk, 640, 320, BF16)))
                MBs.append(sub(fk, 1280, 256, BF16)); ARKs.append(sub(fk, 1792, 128, BF16))
                P2s.append(sub(fk, 2048, 256, BF16)); P4s.append(sub(fk, 2560, 256, BF16)); P8s.append(sub(fk, 3072, 128, BF16))
                NHs.append(sub(fk, 3328, 128, BF16))
                o6 = (hs - 2) * 1280
                Bblks.append(sub(6, o6, 512, BF16)); tmpDs.append(sub(6, o6 + 1024, 64, F32))
        Apair = [A.alloc(128, BF16) for _ in range(2)]
        TTp = [A.alloc(D, BF16) for _ in range(2)]; TTp3 = [t_.re("p (c n) -> p c n", c=8) for t_ in TTp]
        Db = [[A.alloc(D, BF16) for _ in range(2)] for _ in range(2)]
        Db3 = [[d_.re("p (c n) -> p c n", c=8) for d_ in dd_] for dd_ in Db]
        RTb = [A.alloc(1152, BF16) for _ in range(2)]
        WT = [[A.alloc(128, BF16) for _ in range(2)] for _ in range(2)]
        hw = A.alloc(128, BF16); ha = A.alloc(128, BF16); hg = A.alloc(128, BF16)
        NSC = 2
        Sc32 = [A.alloc(128, F32) for _ in range(NSC)]
        Scb = [A.alloc(128, BF16) for _ in range(NSC)]
        for t_ in TTp + Db[0] + Db[1] + RTb:
            P.memset("pool", t_, 0.0)

        def fence():
            f_ = fence_scr.ap
            P.add("pool", lambda e: e.memset(f_, 0.0), wr=F[1].toks + F[2].toks + F[6].toks + alias_toks + fence_scr.toks, cost=150.0)
        ST_in = self.dscr("wkv_in_T", [NS * 8 * 128, 128])
        ST_out = self.dscr("wkv_out_T", [NS * 8 * 128, 128])
        Z = F[1]; Z3 = Z.re("p (a n) -> p a n", a=8)
        Zo = F[2]; Zo3 = Zo.re("p (a n) -> p a n", a=8)
        P.memset("pool", Z, 0.0)
        for s in range(NS):
            for hh in range(2):
                P.dma("sp", Z3[64 * hh:64 * hh + 64, :, 64 * hh:64 * hh + 64],
                      dr["st_wkv"][s].re("(a h) v k -> h v a k", h=2)[hh])
            for a in range(8):
                bk = self.nb()
                P.tr(bk[:, 0:128], Z3[:, a, :], ident)
                P.copy("act", Zo3[:, a, :], bk[:, 0:128])
            P.dma("sp", ST_in[s * 1024:(s + 1) * 1024, :].re("(a p) n -> p a n", p=128), Zo3)

        for ti in range(self.NTT):
            is_s = ti >= NT
            stl = ti - NT
            negc = self.c32("negc_s") if is_s else self.c32("negc_p")
            rowm = self.c32("rowm_s")
            x = F[0]
            P.dma("sp", x, self.res_src("R0", ti))
            xn32 = F[5]
            self.rmsnorm(x, bc["gmix"], xn32, F[7], small[:, 0:4])
            if is_s:
                for s_ in range(8):
                    sq = stl * 8 + s_
                    P.dma("sp", xn32[16 * s_ + 1:16 * s_ + 2, :], dr["st_shift"][sq:sq + 1, :])
                for s_ in range(8):
                    sq = stl * 8 + s_
                    P.dma("sp", dr["s_shift"][sq:sq + 1, :], xn32[16 * s_ + 5:16 * s_ + 6, :])
            elif ti == NT - 1:
                P.dma("sp", dr["p_shift"], xn32[127:128, :])
            xnTe = Ft[4].re("p (k t) -> p k t", k=8)
            for hf in range(2):
                bk = self.nb()
                for k in range(4):
                    kk_ = hf * 4 + k
                    P.tr(bk[:, k * 128:(k + 1) * 128], xn32[:, kk_ * 128:(kk_ + 1) * 128], ident)
                P.copy("act", xnTe[:, hf * 4:hf * 4 + 4, 1:129], bk.re("p (k t) -> p k t", k=4))
            P.copy("pool", xnTe[:, :, 0:1], carry.re("p (k o) -> p k o", o=1))
            P.copy("pool", carry.re("p (k o) -> p k o", o=1), xnTe[:, :, 128:129])
            xx = F[6].re("p (k t) -> p k t", k=8)
            mt = F[7].re("p (k t) -> p k t", k=8)
            P.tt("pool", xx, xnTe[:, :, 0:128], xnTe[:, :, 1:129], ALU.subtract)
            for q in range(6):
                for k in range(8):
                    P.stt("dve", xsT[:, q, k, :], xx[:, k, :], mu3[:, q, k:k + 1], xnTe[:, k, 1:129], ALU.mult, ALU.add)
            bk = self.nb()
            for k in range(8):
                P.mm(bk[0:64, 0:128], w13[:, k, :], xsT[:, 3, k, :], start=(k == 0), stop=(k == 7))
            for k in range(8):
                P.mm(bk[0:64, 128:256], a13[:, k, :], xsT[:, 4, k, :], start=(k == 0), stop=(k == 7))
            for k in range(8):
                P.mm(bk[:, 256:384], g13[:, k, :], xsT[:, 5, k, :], start=(k == 0), stop=(k == 7))
            P.act(hw[0:64, :], bk[0:64, 0:128], AF.Tanh)
            P.copy("act", ha[0:64, :], bk[0:64, 128:256])
            P.act(hg, bk[:, 256:384], AF.Sigmoid)
            r32, k32, v32, g32, lw, a32, kkn = F[0], F[1], F[3], F[4], F[5], F[6], F[2]
            for q, dst in ((0, r32), (1, k32), (2, v32)):
                for hf in range(2):
                    bk = self.nb()
                    for k in range(8):
                        P.mm(bk, xsT[:, q, k, :], wrkv4[:, q, k, hf * 512:(hf + 1) * 512], start=(k == 0), stop=(k == 7))
                    P.copy("act", dst[:, hf * 512:(hf + 1) * 512], bk)
            for hf in range(2):
                cs = slice(hf * 512, (hf + 1) * 512)
                bk = self.nb()
                P.mm(bk, hw[0:64, :], w2[0:64, cs])
                P.tt("dve", lw[:, cs], bk, bc["w0"][:, cs], ALU.add)
                bk = self.nb()
                P.mm(bk, ha[0:64, :], a2[0:64, cs])
                P.tt("dve", a32[:, cs], bk, bc["a0"][:, cs], ALU.add)
                bk = self.nb()
                P.mm(bk, hg, g2[:, cs])
                P.copy("act", g32[:, cs], bk)
            P.act(lw, lw, AF.Sigmoid)
            P.ts("dve", lw, lw, negc, ALU.mult)
            P.act(a32, a32, AF.Sigmoid)
            if is_s:
                P.ts("pool", v32, v32, rowm, ALU.mult)
            v4 = v32.re("p (a h n) -> p a h n", a=8, h=2)
            P.copy("pool", Vblk4[:, :, 0, 0:64], v4[:, :, 0, :])
            P.copy("pool", Vblk4[:, :, 1, 64:128], v4[:, :, 1, :])
            t3 = F[7]
            P.tt("pool", kkn, k32, bc["kk"], ALU.mult)
            P.tt("pool", t3, kkn, kkn, ALU.mult)
            P.red("dve", small[:, 16:32], t3.re("p (h n) -> p h n", h=16))
            P.act(small[:, 16:32], small[:, 16:32], AF.Ln, bias=self.c32("eps24"))
            P.act(small[:, 16:32], small[:, 16:32], AF.Exp, scale=-0.5)
            if is_s:
                P.ts("dve", small[:, 16:32], small[:, 16:32], rowm, ALU.mult)
            P.tt("dve", kkn.re("p (h n) -> p h n", h=16), kkn.re("p (h n) -> p h n", h=16),
                 small[:, 16:32].bc(2, [128, 16, 64]), ALU.mult)
            P.stt("dve", t3, a32, -1.0, bc["ka"], ALU.add, ALU.mult)
            P.stt("dve", k32, t3, 1.0, k32, ALU.add, ALU.mult)
            if is_s:
                P.ts("pool", k32, k32, rowm, ALU.mult)
            P.tt("pool", t3, r32, bc["rk"], ALU.mult)
            P.tt("pool", t3, t3, k32, ALU.mult)
            P.red("dve", small[:, 32:48], t3.re("p (h n) -> p h n", h=16))
            P.tt("pool", a32, a32, kkn, ALU.mult)
            bk = self.nb()
            for a in range(8):
                P.mm(bk[:, a * 8:(a + 1) * 8], lw[:, a * 128:(a + 1) * 128], maskc)
            P.act(PcT, bk[:, 0:64], AF.Exp)
            def cum(mask, hf):
                b_ = self.nb()
                P.mm(b_, mask, lw[:, hf * 512:(hf + 1) * 512])
                return b_
            for hf in range(2):
                cs = slice(hf * 512, (hf + 1) * 512)
                bL = cum(m_ui, hf)
                P.act(t3[:, cs], bL, AF.Exp)
                P.tt("dve", Rt[:, cs], r32[:, cs], t3[:, cs], ALU.mult)
            for hf in range(2):
                cs = slice(hf * 512, (hf + 1) * 512)
                bL = cum(m_ui, hf)
                P.act(t3[:, cs], bL, AF.Exp, scale=-1.0)
                P.tt("dve", Bt[:, cs], a32[:, cs], t3[:, cs], ALU.mult)
                P.tt("pool", Kt[:, cs], k32[:, cs], t3[:, cs], ALU.mult)
            for hf in range(2):
                cs = slice(hf * 512, (hf + 1) * 512)
                bL = cum(m_su, hf)
                P.act(t3[:, cs], bL, AF.Exp)
                P.stt("dve", At[:, cs], kkn[:, cs], -1.0, t3[:, cs], ALU.mult, ALU.mult)
            Kp = lw
            for hf in range(2):
                cs = slice(hf * 512, (hf + 1) * 512)
                bL = cum(m_sl, hf)
                P.act(t3[:, cs], bL, AF.Exp)
                P.tt("dve", Bp[:, cs], a32[:, cs], t3[:, cs], ALU.mult)
            P.tt("pool", Kp, k32, t3, ALU.mult)
            for q, src in enumerate((Kt, Bt, At, Rt)):
                bk = self.nb()
                pb = bk.bitcast(BF16)
                for a in range(8):
                    P.tr(pb[:, a * 128:(a + 1) * 128], src[:, a * 128:(a + 1) * 128], identb)
                P.copy("act", QT4[:, :, q, :], pb.re("p (a t) -> p a t", a=8))
            Yall = F[0]
            fence()
            for a in range(8):
                pp_ = a % 2
                for hh in range(2):
                    hs = (2 * a + hh) % 4
                    MB_, ARK_, P2_, P4_, P8_, NH_ = MBs[hs], ARKs[hs], P2s[hs], P4s[hs], P8s[hs], NHs[hs]
                    rows = slice(64 * hh, 64 * hh + 64)
                    hc = slice((2 * a + hh) * 64, (2 * a + hh) * 64 + 64)
                    xb0, xb1 = XBs[hs]
                    P.copy("pool", xb0[:, 0:64], At[:, hc])
                    bk = self.nb()
                    P.mm(bk[:, 0:256], QT4[rows, a, 2, :], QT4[rows, a, 0:2, :])
                    P.tt("dve", xb0[:, 64:320].re("p (b t) -> p b t", b=2), bk[:, 0:256].re("p (b t) -> p b t", b=2),
                         m_sl.bc(1, [128, 2, 128]), ALU.mult)
                    Nak, Nm = xb0[:, 64:192], xb0[:, 192:320]
                    bk = self.nb()
                    P.mm(bk[:, 0:256], QT4[rows, a, 1, :], QT4[rows, a, 2:4, :])
                    P.tt("dve", MB_, bk[:, 0:256], mask2, ALU.mult)
                    Mm, Arb = MB_[:, 0:128], MB_[:, 128:256]
                    bk = self.nb()
                    P.mm(bk[:, 0:128], QT4[rows, a, 0, :], QT4[rows, a, 3, :])
                    P.tt("dve", ARK_, bk[:, 0:128], m_ui, ALU.mult)
                    bk = self.nb()
                    P.mm(bk[:, 0:128], Nm, Mm)
                    P.mm(bk[:, 128:256], Mm, Nm)
                    P.copy("act", P2_, bk[:, 0:256])
                    bk = self.nb()
                    P.mm(bk[:, 0:128], P2_[:, 128:256], P2_[:, 0:128])
                    P.mm(bk[:, 128:256], P2_[:, 0:128], P2_[:, 128:256])
                    P.copy("act", P4_, bk[:, 0:256])
                    bk = self.nb()
                    P.mm(bk[:, 0:128], P4_[:, 128:256], P4_[:, 0:128])
                    P.copy("act", P8_, bk[:, 0:128])
                    cur, nxt = xb0, xb1
                    for lv, Ml in enumerate((Mm, P2_[:, 0:128], P4_[:, 0:128], P8_)):
                        bk = self.nb()
                        P.mm(bk[:, 0:192], Ml, cur[:, 0:192])
                        if lv < 3:
                            P.tt("dve", nxt[:, 0:192], bk[:, 0:192], cur[:, 0:192], ALU.add)
                            cur, nxt = nxt, cur
                        else:
                            P.tt("dve", Apair[pp_][:, 64 * hh:64 * hh + 64], bk[:, 0:64], cur[:, 0:64], ALU.add)
                            P.tt("dve", NH_, bk[:, 64:192], cur[:, 64:192], ALU.add)
                for hh in range(2):
                    rows = slice(64 * hh, 64 * hh + 64)
                    hc = slice((2 * a + hh) * 64, (2 * a + hh) * 64 + 64)
                    hs = (2 * a + hh) % 4
                    Arb = MBs[hs][:, 128:256]
                    bb = Bblks[hs].re("p (c n) -> p c n", c=8)
                    P.tt("dve", bb, Bp[:, hc].bc(1, [128, 8, 64]), maskc.bc(2, [128, 8, 64]), ALU.mult)
                    bk = self.nb()
                    P.mm(bk, Apair[pp_], Bblks[hs])
                    P.copy("act", TTp3[pp_][rows, :, 64 * hh:64 * hh + 64], bk[rows, :].re("p (c n) -> p c n", c=8))
                    bk = self.nb()
                    P.mm(bk[:, 0:64], NHs[hs], Bp[:, hc])
                    P.tt("dve", tmpDs[hs], bk[:, 0:64], Kp[:, hc], ALU.add)
                    P.tt("dve", Db3[pp_][hh][:, :, 64 * hh:64 * hh + 64], tmpDs[hs].bc(1, [128, 8, 64]), maskc.bc(2, [128, 8, 64]), ALU.mult)
                    bk = self.nb()
                    P.mm(bk[:, 0:128], Apair[pp_], Arb)
                    P.tt("dve", RTb[pp_][rows, :].re("p (c s) -> p c s", s=144)[:, :, 0:16],
                         bk[rows, 0:128].re("p (c t) -> p c t", t=16), QT4[rows, a, 3, :].re("p (c t) -> p c t", t=16), ALU.add)
                    bk = self.nb()
                    P.mm(bk[:, 0:128], NHs[hs], Arb)
                    P.tt("dve", WT[pp_][hh], bk[:, 0:128], ARKs[hs], ALU.add)
                ybk = self.bank[6 + a % 2]
                P.mm(ybk[:, 0:128], WT[pp_][0], Vblk4[:, a, 0, :], start=True, stop=False)
                P.mm(ybk[:, 0:128], WT[pp_][1], Vblk4[:, a, 1, :], start=False, stop=False)
                for c in range(8):
                    if is_s:
                        sq = stl * 8 + c
                        sc32, scb = Sc32[c % NSC], Scb[c % NSC]
                        r0 = (sq * 8 + a) * 128
                        P.dma("sp", sc32, ST_in[r0:r0 + 128, :])
                        P.copy("act", scb, sc32)
                        s_b, s_32 = scb, sc32
                    else:
                        s_b, s_32 = Sbfp[a], S32p[a]
                    P.mm(ybk[:, 0:128], RTb[pp_][:, c * 128:(c + 1) * 128], s_b, start=False, stop=(c == 7))
                    sbk = self.nb()
                    P.mm(sbk[:, 0:128], TTp3[pp_][:, c, :], s_b, start=True, stop=False)
                    P.mm(sbk[:, 0:128], Db3[pp_][0][:, c, :], Vblk4[:, a, 0, :], start=False, stop=False)
                    P.mm(sbk[:, 0:128], Db3[pp_][1][:, c, :], Vblk4[:, a, 1, :], start=False, stop=True)
                    if is_s:
                        P.stt("dve", s_32, s_32, PcT3[:, a, c:c + 1], sbk[:, 0:128], ALU.mult, ALU.add)
                        P.dma("sp", ST_out[r0:r0 + 128, :], s_32)
                    else:
                        P.stt("dve", s_b, s_32, PcT3[:, a, c:c + 1], sbk[:, 0:128], ALU.mult, ALU.add)
                        P.stt("dve", s_32, s_32, PcT3[:, a, c:c + 1], sbk[:, 0:128], ALU.mult, ALU.add)
                P.copy("act", Yall[:, a * 128:(a + 1) * 128], ybk[:, 0:128])
            fence()
            if self.debug:
                P.dma("sp", self.dr["dbg_y"][ti * 128:(ti + 1) * 128, :], Yall)
                P.dma("sp", self.dr["dbg_g"][ti * 128:(ti + 1) * 128, :], g32)
            Y3 = Yall.re("p (h n) -> p h n", h=16)
            sm = small
            P.red("dve", sm[:, 0:16], Y3)
            P.tt("pool", t3, Yall, Yall, ALU.mult)
            P.red("dve", sm[:, 48:64], t3.re("p (h n) -> p h n", h=16))
            P.ts("dve", sm[:, 0:16], sm[:, 0:16], 1.0 / 64, ALU.mult)
            P.tt("dve", sm[:, 16:32], sm[:, 0:16], sm[:, 0:16], ALU.mult)
            P.stt("dve", sm[:, 48:64], sm[:, 48:64], 1.0 / 64, sm[:, 16:32], ALU.mult, ALU.subtract)
            P.act(sm[:, 48:64], sm[:, 48:64], AF.Ln, bias=self.c32("gneps"))
            P.act(sm[:, 48:64], sm[:, 48:64], AF.Exp, scale=-0.5)
            P.tt("dve", Y3, Y3, sm[:, 0:16].bc(2, [128, 16, 64]), ALU.subtract)
            P.tt("dve", Y3, Y3, sm[:, 48:64].bc(2, [128, 16, 64]), ALU.mult)
            P.tt("pool", Yall, Yall, bc["lnw"], ALU.mult)
            P.tt("pool", Yall, Yall, bc["lnb"], ALU.add)
            P.tt("pool", t3.re("p (h n) -> p h n", h=16), v32.re("p (h n) -> p h n", h=16),
                 sm[:, 32:48].bc(2, [128, 16, 64]), ALU.mult)
            P.tt("dve", Yall, Yall, t3, ALU.add)
            if self.debug:
                P.dma("sp", self.dr["dbg_yn"][ti * 128:(ti + 1) * 128, :], Yall)
            P.tt("dve", yb, Yall, g32, ALU.mult)
            self.transpose_bf(yb, yT3, 0, self.nb())
            xo = F[0]
            xr_ = F[7]
            P.dma("sp", xr_, self.res_src("R0", ti))
            for hf in range(2):
                cs = slice(hf * 512, (hf + 1) * 512)
                bk = self.nb()
                for k in range(8):
                    P.mm(bk, yT3[:, k, :], wo3[:, k, cs], start=(k == 0), stop=(k == 7))
                P.tt("dve", xo[:, cs], bk, xr_[:, cs], ALU.add)
            P.dma("sp", self.res_src("R1", ti), xo)
        Zt = F[1]; Zt3 = Zt.re("p (a n) -> p a n", a=8)
        for a in range(8):
            bk = self.nb()
            P.tr(bk[:, 0:128], S32p[a], ident)
            P.copy("act", Zt3[:, a, :], bk[:, 0:128])
        for hh in range(2):
            P.dma("sp", dr["p_wkv"].re("(a h) v k -> h v a k", h=2)[hh], Zt3[64 * hh:64 * hh + 64, :, 64 * hh:64 * hh + 64])
        for s in range(NS):
            Zi = F[3].re("p (a n) -> p a n", a=8)
            Zo2 = F[4 + s % 2].re("p (a n) -> p a n", a=8)
            P.dma("sp", Zi, ST_out[s * 1024:(s + 1) * 1024, :].re("(a p) n -> p a n", p=128))
            for a in range(8):
                bk = self.nb()
                P.tr(bk[:, 0:128], Zi[:, a, :], ident)
                P.copy("act", Zo2[:, a, :], bk[:, 0:128])
            for hh in range(2):
                P.dma("sp", dr["s_wkv"][s].re("(a h) v k -> h v a k", h=2)[hh], Zo2[64 * hh:64 * hh + 64, :, 64 * hh:64 * hh + 64])


    def phase_attn(self):
        P, A, NT, NST, NS, NTT, T = self.P, self.A, self.NT, self.NST, self.NS, self.NTT, self.T
        dr = self.dr
        identb = self.c16("ident")
        QTs = [[self.dscr("qT%d_%d" % (g, j), [8 * 128, NTT * 128], BF16) for j in range(2)] for g in range(3)]
        Vs = [self.dscr("vs%d" % g, [NTT * 128, H * HS], BF16) for g in range(3)]
        Os = [self.dscr("os%d" % g, [NTT * 128, H * 65], F32) for g in range(3)]
        qs = [self.dscr("qs%d" % g, [NST * 128, D], F32) for g in range(3)]
        kpad = [self.dscr("kpad%d" % g, [NST * 128, D], F32) for g in range(3)]
        vpad = [self.dscr("vpad%d" % g, [NST * 128, D], F32) for g in range(3)]
        A.mark()
        win = A.alloc(8 * 9 * D, BF16); win3 = win.re("p (k n) -> p k n", k=8)
        for k in range(8):
            P.dma("pool", win3[:, k, :], dr["attn_w_in"][k * 128:(k + 1) * 128, :])
        gbc = A.alloc(D, F32)
        self.load_bcast(gbc, dr["norm_mix"][1:2, :])
        rope = A.alloc(NT * 128, F32); rope3 = rope.re("p (t n) -> p t n", n=128)
        P.dma("sp", rope, dr["rope_p"])
        x = A.alloc(D, F32)
        xn = A.alloc(D, BF16)
        xnT = A.alloc(D, BF16); xnT3 = xnT.re("p (k t) -> p k t", k=8)
        small = A.alloc(4, F32)
        xq = A.alloc(D, F32); A32 = A.alloc(D, F32); B32 = A.alloc(D, F32)
        ob = [A.alloc(D, BF16) for _ in range(2)]
        vb = [A.alloc(H * HS, BF16) for _ in range(2)]
        for v_ in vb:
            P.memset("pool", v_, 1.0)
        QTt = [A.alloc(D, BF16) for _ in range(2)]
        cnt = 0
        for ti in range(NTT):
            is_s = ti >= NT
            stl = ti - NT
            P.dma("sp", x, self.res_src("R2", ti))
            self.rmsnorm(x, gbc, xn, A32, small)
            self.transpose_bf(xn, xnT3, 0, self.bank[6])
            if is_s:
                o_r = CF["rope_s"][0]
                cs2 = self.cf[:, o_r:o_r + 64]
                sn2 = self.cf[:, o_r + 64:o_r + 128]
            else:
                cs2 = rope3[:, ti, 0:64]
                sn2 = rope3[:, ti, 64:128]
            for g in range(3):
                for j in range(3):
                    col0 = g * 3 * D + j * D
                    bks = []
                    for hf in range(2):
                        bk = self.bank[(2 * cnt + hf) % 6]
                        bks.append(bk)
                        for k in range(8):
                            P.mm(bk, xnT3[:, k, :], win3[:, k, col0 + hf * 512:col0 + (hf + 1) * 512], start=(k == 0), stop=(k == 7))
                    cnt += 1
                    if j == 2:
                        for hf in range(2):
                            P.copy("act", xq[:, hf * 512:(hf + 1) * 512], bks[hf])
                        vbb = vb[cnt % 2]
                        P.copy("pool", vbb.re("p (h n) -> p h n", n=HS)[:, :, 0:64], xq.re("p (h n) -> p h n", n=64))
                        P.dma("sp", Vs[g][ti * 128:(ti + 1) * 128, :], vbb)
                        if is_s:
                            P.dma("sp", vpad[g][stl * 128:(stl + 1) * 128, :], xq)
                        else:
                            keep = min(WINDOWS[g], T) // 128
                            if ti >= NT - keep:
                                r0 = (ti - (NT - keep)) * 128
                                P.dma("sp", dr["p_v%d" % g][r0:r0 + 128, :], xq)
                        continue
                    for hf in range(2):
                        cs = slice(hf * 512, (hf + 1) * 512)
                        P.copy("act", xq[:, cs], bks[hf])
                        P.tt("dve", A32[:, cs].re("p (h n) -> p h n", n=64), xq[:, cs].re("p (h n) -> p h n", n=64),
                             cs2.bc(1, [128, 8, 64]), ALU.mult)
                    P.tt("pool", B32.re("p (h n) -> p h n", n=64), xq.re("p (h n) -> p h n", n=64), sn2.bc(1, [128, 16, 64]), ALU.mult)
                    A4 = A32.re("p (h b n) -> p h b n", b=2, n=32)
                    B4 = B32.re("p (h b n) -> p h b n", b=2, n=32)
                    P.tt("dve", A4[:, :, 0, :], A4[:, :, 0, :], B4[:, :, 1, :], ALU.subtract)
                    P.tt("pool", A4[:, :, 1, :], A4[:, :, 1, :], B4[:, :, 0, :], ALU.add)
                    if j == 1:
                        if is_s:
                            P.dma("sp", kpad[g][stl * 128:(stl + 1) * 128, :], A32)
                        else:
                            keep = min(WINDOWS[g], T) // 128
                            if ti >= NT - keep:
                                r0 = (ti - (NT - keep)) * 128
                                P.dma("sp", dr["p_k%d" % g][r0:r0 + 128, :], A32)
                    elif is_s:
                        P.dma("sp", qs[g][stl * 128:(stl + 1) * 128, :], A32)
                    obb = ob[cnt % 2]
                    P.copy("act", obb, A32)
                    qt = QTt[cnt % 2]
                    bk = self.bank[7]
                    pb = bk.bitcast(BF16)
                    for a in range(8):
                        P.tr(pb[:, a * 128:(a + 1) * 128], obb[:, a * 128:(a + 1) * 128], identb)
                    P.copy("act", qt, pb)
                    P.dma("sp", QTs[g][j][:, ti * 128:(ti + 1) * 128].re("(a p) t -> p a t", p=128), qt.re("p (a t) -> p a t", a=8))
        A.release()
        P.barrier()
        import os
        ASTOP = os.environ.get("ATT_STOP", "")
        if ASTOP == "s1":
            return
        for g in range(3):
            P.dma("sp", dr["s_k%d" % g].re("(s j) d -> s j d", j=4), kpad[g].re("(s q) d -> s q d", q=16)[:, 2:6, :])
            P.dma("sp", dr["s_v%d" % g].re("(s j) d -> s j d", j=4), vpad[g].re("(s q) d -> s q d", q=16)[:, 2:6, :])
        A.mark()
        NBLK = NT
        Vg = A.alloc(NBLK * H * HS, BF16); Vg4 = Vg.re("p (b h n) -> p b h n", b=NBLK, n=HS)
        Qn = [A.alloc(T, BF16) for _ in range(2)]
        Kn = [A.alloc(T, BF16) for _ in range(2)]
        Qc = [A.alloc(T, BF16) for _ in range(2)]
        Kc = [A.alloc(T, BF16) for _ in range(2)]
        Obuf = [A.alloc(NBLK * 130, F32) for _ in range(2)]
        pT = [A.alloc(512, BF16) for _ in range(3)]
        o_m = CB["am_ge"][0]
        am2 = self.cb[:, o_m:o_m + 256]
        it = 0
        for g in range(3):
            d = DILS[g]
            L = T // d
            nbc = L // 128
            assert nbc >= 1
            for c in range(d):
                for n in range(nbc):
                    blk = c * nbc + n
                    r0 = c + d * 128 * n
                    src = Vs[g][r0:r0 + d * 127 + 1, :]
                    if d > 1:
                        src = V(Vs[g].ap[r0:r0 + d * 128 - (d - 1):d, :], Vs[g].toks) if False else None
                    if d == 1:
                        P.dma("sp", Vg4[:, blk, :, :], Vs[g][r0:r0 + 128, :].re("p (h n) -> p h n", n=HS))
                    else:
                        rows = Vs[g][0:T, :].re("(l c) n -> c l n", c=d)[c, n * 128:(n + 1) * 128, :]
                        P.dma("sp", Vg4[:, blk, :, :], rows.re("p (h n) -> p h n", n=HS))
            if ASTOP == "s2a":
                continue
            for a in range(8):
                b = a % 2
                P.dma("sp", Qn[b], QTs[g][0][a * 128:(a + 1) * 128, 0:T])
                P.dma("sp", Kn[b], QTs[g][1][a * 128:(a + 1) * 128, 0:T])
                if d == 1:
                    qc3 = Qn[b].re("p (c l) -> p c l", c=1)
                    kc3 = Kn[b].re("p (c l) -> p c l", c=1)
                else:
                    qc3 = Qc[b].re("p (c l) -> p c l", c=d)
                    kc3 = Kc[b].re("p (c l) -> p c l", c=d)
                    P.copy("pool", qc3, Qn[b].re("p (l c) -> p c l", c=d))
                    P.copy("pool", kc3, Kn[b].re("p (l c) -> p c l", c=d))
                Ob3 = Obuf[b].re("p (b n) -> p b n", n=130)
                if ASTOP == "s2q":
                    continue
                for c in range(d):
                    for n in range(nbc):
                        blk = c * nbc + n
                        it += 1
                        pt = pT[it % 3]
                        for hh in range(2):
                            bk = self.bank[(2 * it + hh) % 4]
                            rows = slice(64 * hh, 64 * hh + 64)
                            if n > 0:
                                P.mm(bk[:, 0:128], kc3[rows, c, (n - 1) * 128:n * 128], qc3[rows, c, n * 128:(n + 1) * 128])
                            P.mm(bk[:, 128:256], kc3[rows, c, n * 128:(n + 1) * 128], qc3[rows, c, n * 128:(n + 1) * 128])
                            if n > 0:
                                P.act(pt[:, hh * 256:(hh + 1) * 256], bk[:, 0:256], AF.Exp, scale=0.125)
                            else:
                                P.act(pt[:, hh * 256 + 128:(hh + 1) * 256], bk[:, 128:256], AF.Exp, scale=0.125)
                        if n > 0:
                            P.tt("dve", pt.re("p (h m) -> p h m", h=2), pt.re("p (h m) -> p h m", h=2), am2.bc(1, [128, 2, 256]), ALU.mult)
                        else:
                            ptv = pt.re("p (h m) -> p h m", h=2)[:, :, 128:256]
                            P.tt("dve", ptv, ptv, am2[:, 128:256].bc(1, [128, 2, 128]), ALU.mult)
                        if ASTOP == "s2m":
                            continue
                        okb = self.bank[4 + it % 4]
                        for hh in range(2):
                            hd = 2 * a + hh
                            if n > 0:
                                P.mm(okb[:, hh * 65:hh * 65 + 65], pt[:, hh * 256:hh * 256 + 128], Vg4[:, blk - 1, hd, 0:65], start=True, stop=False)
                            P.mm(okb[:, hh * 65:hh * 65 + 65], pt[:, hh * 256 + 128:hh * 256 + 256], Vg4[:, blk, hd, 0:65], start=(n == 0), stop=True)
                        P.copy("act", Ob3[:, blk, :], okb[:, 0:130])
                if ASTOP in ("s2m", "s2p"):
                    continue
                dst = Os[g][0:T, a * 130:(a + 1) * 130].re("(n i c) m -> i c n m", i=128, c=d)
                srcv = Obuf[b].re("p (c n m) -> p c n m", c=d, m=130)
                for c in range(d):
                    for n0 in range(0, nbc, 8):
                        n1 = min(nbc, n0 + 8)
                        P.dma("sp", dst[:, c, n0:n1, :], srcv[:, c, n0:n1, :])
        A.release()
        P.barrier()
        if ASTOP in ("s2", "s2a", "s2q", "s2m", "s2p"):
            return
        A.mark()
        self.attn_sample(Os, qs, kpad, vpad)
        A.release()
        P.barrier()
        if ASTOP == "s2b":
            return
        A.mark()
        wo = A.alloc(8 * D, BF16); wo3 = wo.re("p (k n) -> p k n", k=8)
        P.dma("pool", wo3, dr["attn_w_o"].re("(k p) n -> p k n", p=128))
        O3 = [[A.alloc(H * 65, F32) for _ in range(3)] for _ in range(2)]
        xr = [A.alloc(D, F32) for _ in range(2)]
        obf = [A.alloc(D, BF16) for _ in range(2)]
        oT = [A.alloc(D, BF16) for _ in range(2)]
        rden = [A.alloc(16, F32) for _ in range(2)]
        for ti in range(NTT):
            b = ti % 2
            for g in range(3):
                P.dma("sp", O3[b][g], Os[g][ti * 128:(ti + 1) * 128, :])
            P.dma("sp", xr[b], self.res_src("R2", ti))
            P.tt("dve", O3[b][0], O3[b][0], O3[b][1], ALU.add)
            P.tt("pool", O3[b][0], O3[b][0], O3[b][2], ALU.add)
            o3 = O3[b][0].re("p (h n) -> p h n", n=65)
            P.add("dve", (lambda o_, i_: (lambda e: e.reciprocal(out=o_, in_=i_)))(rden[b].ap.rearrange("p (h o) -> p h o", o=1), o3.ap[:, :, 64:65]),
                  rd=o3.toks, wr=rden[b].toks)
            P.tt("dve", obf[b].re("p (h n) -> p h n", n=64), o3[:, :, 0:64], rden[b].bc(2, [128, 16, 64]), ALU.mult)
            oT3 = oT[b].re("p (k t) -> p k t", k=8)
            self.transpose_bf(obf[b], oT3, 0, self.bank[6])
            for hf in range(2):
                cs = slice(hf * 512, (hf + 1) * 512)
                bk = self.bank[hf + 2 * (ti % 2)]
                for k in range(8):
                    P.mm(bk, oT3[:, k, :], wo3[:, k, cs], start=(k == 0), stop=(k == 7))
                P.tt("dve", xr[b][:, cs], bk, xr[b][:, cs], ALU.add)
            P.dma("sp", self.res_src("R3", ti), xr[b])
        A.release()

    def attn_sample(self, Os, qs, kpad, vpad):
        P, A, NS, NT = self.P, self.A, self.NS, self.NT
        dr = self.dr
        ones = A.alloc(H * 65, F32)
        P.memset("pool", ones, 1.0)
        for g in range(3):
            for stl in range(self.NST):
                r0 = (NT + stl) * 128
                P.dma("sp", Os[g][r0:r0 + 128, :], ones)
        Kt = [A.alloc(D, F32) for _ in range(2)]
        Vt = [A.alloc(D, F32) for _ in range(2)]
        Vb = [A.alloc(H * HS, BF16) for _ in range(2)]
        qbc = [A.alloc(D, F32) for _ in range(2)]
        prod = [A.alloc(D, F32) for _ in range(2)]
        sc = [A.alloc(16, F32) for _ in range(2)]
        pb = [A.alloc(16, BF16) for _ in range(2)]
        Kn = [A.alloc(D, F32) for _ in range(2)]
        Vn = [A.alloc(D, F32) for _ in range(2)]
        Vnb = [A.alloc(H * HS, BF16) for _ in range(2)]
        prodn = A.alloc(D, F32)
        scn = A.alloc(16, F32)
        pnb = [A.alloc(16, BF16) for _ in range(2)]
        orow = [A.alloc(H * 65, F32) for _ in range(2)]
        for v_ in Vb + Vnb:
            P.memset("pool", v_, 1.0)
        o_ge, o_le, o_eq = CF["ge_j"][0], CF["le_j"][0], CF["eq_j"][0]
        HCH = ((0, 7), (7, 14), (14, 16))
        u = 0
        kv = 0
        for s in range(NS):
            for g in range(3):
                d = DILS[g]
                W = WINDOWS[g]
                nb_ = (s * 3 + g) % 2
                rown = NT * 128 + 16 * s + 2
                P.dma("sp", Kn[nb_][0:4, :], kpad[g][16 * s + 2:16 * s + 6, :])
                P.dma("sp", Vn[nb_][0:4, :], vpad[g][16 * s + 2:16 * s + 6, :])
                P.copy("pool", Vnb[nb_][0:4, :].re("p (h n) -> p h n", n=HS)[:, :, 0:64], Vn[nb_][0:4, :].re("p (h n) -> p h n", n=64))
                for j in range(4):
                    if g > 0 or j == 0:
                        kb = kv % 2
                        kv += 1
                        if g == 0:
                            ksrc = dr["ck0"][s]
                            vsrc = dr["cv0"][s]
                        else:
                            ksrc = dr["ck%d" % g][s].re("(l c) n -> c l n", c=d)[j]
                            vsrc = dr["cv%d" % g][s].re("(l c) n -> c l n", c=d)[j]
                        P.dma("sp", Kt[kb], ksrc)
                        P.dma("sp", Vt[kb], vsrc)
                        P.copy("pool", Vb[kb].re("p (h n) -> p h n", n=HS)[:, :, 0:64], Vt[kb].re("p (h n) -> p h n", n=64))
                    ub = u % 2
                    u += 1
                    qrow = qs[g][16 * s + 2 + j:16 * s + 3 + j, :]
                    P.dma("sp", qbc[ub], V(qrow.ap.to_broadcast([128, D]), qrow.toks))
                    P.tt("dve", prod[ub], Kt[kb], qbc[ub], ALU.mult)
                    P.red("dve", sc[ub], prod[ub].re("p (h n) -> p h n", h=16))
                    P.act(pb[ub], sc[ub], AF.Exp, scale=0.125)
                    if g == 0 and j > 0:
                        P.ts("dve", pb[ub], pb[ub], self.cf[:, o_ge + j:o_ge + j + 1], ALU.mult)
                    P.tt("pool", prodn[0:4, :], Kn[nb_][0:4, :], qbc[ub][0:4, :], ALU.mult)
                    P.red("dve", scn[0:4, :], prodn[0:4, :].re("p (h n) -> p h n", h=16))
                    P.act(pnb[ub][0:4, :], scn[0:4, :], AF.Exp, scale=0.125)
                    om = (o_le if g == 0 else o_eq) + j
                    P.ts("dve", pnb[ub][0:4, :], pnb[ub][0:4, :], self.cf[0:4, om:om + 1], ALU.mult)
                    Vb3 = Vb[kb].re("p (h n) -> p h n", n=HS)
                    Vn3 = Vnb[nb_].re("p (h n) -> p h n", n=HS)
                    for ci, (h0, h1) in enumerate(HCH):
                        bk = self.bank[(3 * u + ci) % 8]
                        for h in range(h0, h1):
                            oc = slice((h - h0) * 65, (h - h0 + 1) * 65)
                            P.mm(bk[0:1, oc], pb[ub][:, h:h + 1], Vb3[:, h, 0:65], start=True, stop=False)
                            P.mm(bk[0:1, oc], pnb[ub][0:4, h:h + 1], Vn3[0:4, h, 0:65], start=False, stop=True)
                        P.copy("act", orow[ub][0:1, h0 * 65:h1 * 65], bk[0:1, 0:(h1 - h0) * 65])
                    P.dma("sp", Os[g][rown + j:rown + j + 1, :], orow[ub][0:1, :])


def core_inputs(inp, b, s0, NS, T, consts):
    cf, cb, rope_p = consts
    f = lambda a: np.ascontiguousarray(a, dtype=np.float32)
    m = {}
    m["xp"] = f(inp["x_prompt"][b])
    m["xs"] = f(inp["x_sample"][s0:s0 + NS]).reshape(NS * 4, D)
    m["st_shift"] = f(inp["state_rwkv_shift"][0, s0:s0 + NS])
    m["st_wkv"] = f(inp["state_rwkv_wkv"][0, s0:s0 + NS])
    caches = ((inp["cache_k_w128"], inp["cache_v_w128"]), (inp["cache_k_w512"], inp["cache_v_w512"]),
              (inp["cache_k_w2048"], inp["cache_v_w2048"]))
    for g, (ck, cv) in enumerate(caches):
        m["ck%d" % g] = f(ck[0, s0:s0 + NS]).reshape(NS, -1, D)
        m["cv%d" % g] = f(cv[0, s0:s0 + NS]).reshape(NS, -1, D)
    m["st_conv"] = f(inp["state_ffn_conv"][:, s0:s0 + NS])
    m["norm_mix"] = f(inp["norm_mix"]); m["norm_ffn"] = f(inp["norm_ffn"]); m["norm_final"] = f(inp["norm_final"]).reshape(1, D)
    m["rwkv_mu"] = f(inp["rwkv_mu"][0]); m["rwkv_w_rkv"] = f(inp["rwkv_w_rkv"][0])
    for k in ("w0", "a0", "k_k", "k_a", "ln_w", "ln_b"):
        m["rwkv_" + k] = f(inp["rwkv_" + k][0]).reshape(1, D)
    m["rwkv_r_k"] = f(inp["rwkv_r_k"][0]).reshape(1, D)
    for k in ("w1", "w2", "a1", "a2", "g1", "g2", "w_o"):
        m["rwkv_" + k] = f(inp["rwkv_" + k][0])
    m["attn_w_in"] = f(inp["attn_w_in"][0]); m["attn_w_o"] = f(inp["attn_w_o"][0])
    m["ffn_w_up"] = f(inp["ffn_w_up"]); m["ffn_conv_w"] = f(inp["ffn_conv_w"]); m["ffn_conv_b"] = f(inp["ffn_conv_b"])
    m["ffn_w_down"] = f(inp["ffn_w_down"])
    m["cf"] = cf; m["cb"] = cb; m["rope_p"] = rope_p
    return m


_CACHE = {}


def kernel(**inputs):
    inp = {k: np.asarray(v) for k, v in inputs.items()}
    T = inp["x_prompt"].shape[1]
    NB = inp["x_prompt"].shape[0]
    NSEQ = inp["x_sample"].shape[0]
    NS = NSEQ // NCORES
    key = (T, NS)
    if key not in _CACHE:
        _CACHE[key] = (Builder(T, NS).build(), host_consts(T))
    nc, consts = _CACHE[key]
    in_maps = [core_inputs(inp, c % NB, c * NS, NS, T, consts) for c in range(NCORES)]
    res = run_bass_kernel_spmd(nc, in_maps, core_ids=list(range(NCORES)))
    R = res.results
    f = np.float32
    pr = lambda name, shape: np.stack([np.asarray(R[b][name], f).reshape(shape) for b in range(NB)])
    sm = lambda name, shape: np.concatenate([np.asarray(R[c][name], f).reshape(shape) for c in range(NCORES)], 0)
    outs = [pr("y_p", (T, D)), sm("y_s", (NS, 4, D)),
            pr("p_shift", (D,))[None], pr("p_wkv", (H, DH, DH))[None]]
    for g, w in enumerate(WINDOWS):
        kp = min(w, T)
        outs.append(pr("p_k%d" % g, (kp, H, DH))[None])
        outs.append(pr("p_v%d" % g, (kp, H, DH))[None])
    outs.append(np.stack([np.asarray(R[b]["p_conv"], f).reshape(2, 2, FF) for b in range(NB)], 1))
    outs.append(sm("s_shift", (NS, D))[None])
    outs.append(sm("s_wkv", (NS, H, DH, DH))[None])
    for g in range(3):
        outs.append(sm("s_k%d" % g, (NS, 4, H, DH))[None])
        outs.append(sm("s_v%d" % g, (NS, 4, H, DH))[None])
    outs.append(np.concatenate([np.asarray(R[c]["s_conv"], f).reshape(2, NS, 2, FF) for c in range(NCORES)], 1))
    return tuple(outs)
```

```python
import math
import os
import numpy as np
import ml_dtypes
from contextlib import ExitStack
import concourse.bass as bass
import concourse.mybir as mybir
from concourse.bass_utils import run_bass_kernel_spmd

F32 = mybir.dt.float32
BF16 = mybir.dt.bfloat16
ALU = mybir.AluOpType
AF = mybir.ActivationFunctionType
AX = mybir.AxisListType

D = 1024
H = 16
DH = 64
FF = 2816
NFC = FF // 128
NGRP = 3
WINDOWS = (128, 512, 2048)
DILS = (1, 4, 16)
PAST = 2048
NCORES = 8
HS = 68
EXPM05 = math.exp(-0.5)
DEBUG_SITES = False
SITES = {}


class Tok:
    __slots__ = ("w", "r")

    def __init__(self):
        self.w = None
        self.r = []


class Op:
    __slots__ = ("eng", "fn", "deps", "need_inc", "cnt", "is_dma", "dsem", "dval", "prev_dval", "site",
                 "sdeps", "cost", "idx", "succ", "indeg", "ready", "fin")


class V:
    def __init__(self, ap, toks=None):
        self.ap = ap
        self.toks = toks if toks is not None else [Tok()]

    def __getitem__(self, k):
        return V(self.ap[k], self.toks)

    def re(self, pat, **kw):
        return V(self.ap.rearrange(pat, **kw), self.toks)

    def bc(self, axis, shape):
        return V(self.ap.unsqueeze(axis).to_broadcast(list(shape)), self.toks)

    def bitcast(self, dt):
        return V(self.ap.bitcast(dt), self.toks)


def _ap(x):
    return x.ap if isinstance(x, V) else x


def _fsz(x):
    sh = _ap(x).shape
    n = 1
    for d_ in sh[1:]:
        n *= int(d_)
    return n


def _toks(*xs):
    r = []
    for x in xs:
        if isinstance(x, V):
            r.extend(x.toks)
    return r


class Prog:
    ENGS = ("pe", "act", "dve", "pool", "sp")
    NDSEM = 14

    def __init__(self, nc, stack):
        self.nc = nc
        self.ops = {e: [] for e in self.ENGS}
        self.esem = {e: stack.enter_context(nc.semaphore("es_" + e)) for e in ("pe", "act", "dve", "pool")}
        self.dsems = {e: [stack.enter_context(nc.semaphore("ds_%s%d" % (e, i))) for i in range(self.NDSEM)]
                      for e in ("sp", "act", "pool")}
        self.ndma = {e: 0 for e in ("sp", "act", "pool")}
        self.regions = []
        self.cur = []
        self.nins = 0

    def add(self, eng, fn, rd=(), wr=(), dma=False, extra=(), cost=100.0):
        o = Op()
        o.eng = eng
        o.fn = fn
        o.site = None
        if DEBUG_SITES:
            import traceback
            o.site = [f.lineno for f in traceback.extract_stack(limit=5)[:-1]]
        o.need_inc = False
        o.cnt = 0
        o.is_dma = dma
        o.cost = cost
        o.idx = self.nins
        deps = list(extra)
        for t in rd:
            if t.w is not None:
                deps.append(t.w)
        for t in wr:
            if t.w is not None:
                deps.append(t.w)
            deps.extend(t.r)
        dd = []
        sd = []
        seen = set()
        for d in deps:
            if id(d) in seen or d is o:
                continue
            seen.add(id(d))
            sd.append(d)
            if (not d.is_dma) and d.eng == "pe" and eng == "pe" and not dma:
                continue
            dd.append(d)
        o.deps = dd
        o.sdeps = sd
        for t in rd:
            t.r.append(o)
        for t in wr:
            t.w = o
            t.r = []
        self.cur.append(o)
        self.nins += 1
        return o

    def barrier(self):
        if self.cur:
            self.regions.append(self.cur)
            self.cur = []

    def mm(self, out, lhsT, rhs, start=True, stop=True, extra_rd=()):
        o, l, r = _ap(out), _ap(lhsT), _ap(rhs)
        c_ = max(_fsz(rhs), 64) / 1.6 + 20.0
        if l.dtype == F32:
            c_ *= 4.0
        return self.add("pe", lambda e: e.matmul(o, lhsT=l, rhs=r, start=start, stop=stop),
                        rd=_toks(lhsT, rhs) + list(extra_rd), wr=_toks(out), cost=c_)

    def tr(self, out, in_, ident):
        o, i, d = _ap(out), _ap(in_), _ap(ident)
        c_ = 110.0 * (4.0 if i.dtype == F32 else 1.0)
        return self.add("pe", lambda e: e.transpose(out=o, in_=i, identity=d), rd=_toks(in_, ident), wr=_toks(out), cost=c_)

    def tt(self, eng, out, in0, in1, op):
        o, a, b = _ap(out), _ap(in0), _ap(in1)
        return self.add(eng, lambda e: e.tensor_tensor(out=o, in0=a, in1=b, op=op), rd=_toks(in0, in1), wr=_toks(out),
                        cost=self.ecost(eng, out))

    def ts(self, eng, out, in0, s1, op0, s2=None, op1=None):
        o, a = _ap(out), _ap(in0)
        s1a, s2a = _ap(s1), _ap(s2)
        if op1 is None:
            fn = lambda e: e.tensor_scalar(out=o, in0=a, scalar1=s1a, scalar2=None, op0=op0)
        else:
            fn = lambda e: e.tensor_scalar(out=o, in0=a, scalar1=s1a, scalar2=s2a, op0=op0, op1=op1)
        return self.add(eng, fn, rd=_toks(in0, s1, s2), wr=_toks(out), cost=self.ecost(eng, out))

    def stt(self, eng, out, in0, scalar, in1, op0, op1):
        o, a, s, b = _ap(out), _ap(in0), _ap(scalar), _ap(in1)
        return self.add(eng, lambda e: e.scalar_tensor_tensor(out=o, in0=a, scalar=s, in1=b, op0=op0, op1=op1),
                        rd=_toks(in0, scalar, in1), wr=_toks(out), cost=self.ecost(eng, out))

    def copy(self, eng, out, in_):
        o, i = _ap(out), _ap(in_)
        if eng == "act":
            fn = lambda e: e.copy(out=o, in_=i)
        else:
            fn = lambda e: e.tensor_copy(out=o, in_=i)
        return self.add(eng, fn, rd=_toks(in_), wr=_toks(out), cost=self.ecost(eng, out))

    def ecost(self, eng, out):
        n = _fsz(out)
        if eng == "act":
            return 230.0 + 0.85 * n
        if eng == "pool":
            return 120.0 + 2.0 * n
        return 70.0 + 1.05 * n

    def act(self, out, in_, func, scale=1.0, bias=None, accum=None):
        o, i, b, ac = _ap(out), _ap(in_), _ap(bias), _ap(accum)
        kw = {}
        if bias is not None:
            kw["bias"] = b
        if accum is not None:
            kw["accum_out"] = ac
        sc = _ap(scale)
        return self.add("act", lambda e: e.activation(out=o, in_=i, func=func, scale=sc, **kw),
                        rd=_toks(in_, bias, scale), wr=_toks(out, accum), cost=self.ecost("act", out))

    def red(self, eng, out, in_, op=None, axis=None):
        o, i = _ap(out), _ap(in_)
        op = op or ALU.add
        axis = axis or AX.X
        return self.add(eng, lambda e: e.tensor_reduce(out=o, in_=i, axis=axis, op=op), rd=_toks(in_), wr=_toks(out),
                        cost=self.ecost(eng, in_))

    def memset(self, eng, out, val):
        o = _ap(out)
        return self.add(eng, lambda e: e.memset(o, val), wr=_toks(out), cost=self.ecost(eng, out))

    def dma(self, q, out, in_, **kw):
        o, i = _ap(out), _ap(in_)
        nb_ = _fsz(out) * int(o.shape[0]) * (4 if o.dtype == F32 else 2)
        return self.add(q, lambda e: e.dma_start(out=o, in_=i, **kw), rd=_toks(in_), wr=_toks(out), dma=True,
                        cost=2200.0 + nb_ / 150.0)

    LAT = 300.0

    def schedule_region(self, ops):
        engs = self.ENGS
        pend = {e: [] for e in engs}
        inreg = set(id(o) for o in ops)
        for o in ops:
            o.succ = []
            o.indeg = 0
            o.ready = 0.0
        for o in ops:
            for d in o.sdeps:
                if id(d) in inreg:
                    d.succ.append(o)
                    o.indeg += 1
            pend[o.eng].append(o)
        free = {e: 0.0 for e in engs}
        out = {e: [] for e in engs}
        n = len(ops)
        W = int(os.environ.get("BASS_W", "96"))
        LAT = self.LAT
        PRIO = os.environ.get("BASS_PRIO", "1") == "1"
        for o in reversed(ops):
            b_ = 0.0
            for s_ in o.succ:
                v_ = s_.cnt + (0.0 if (s_.eng == o.eng and not o.is_dma) else LAT)
                if v_ > b_:
                    b_ = v_
            o.cnt = b_ + o.cost
        SLACK = 40.0
        while n > 0:
            bst = None
            bo = None
            be = None
            bi = 0
            for e in engs:
                lst = pend[e]
                if not lst:
                    continue
                fe = free[e]
                lim = W if len(lst) > W else len(lst)
                cst = None
                co = None
                ci = 0
                if PRIO:
                    mst = None
                    for i in range(lim):
                        o = lst[i]
                        if o.indeg == 0:
                            st = o.ready if o.ready > fe else fe
                            if mst is None or st < mst:
                                mst = st
                    if mst is None:
                        continue
                    for i in range(lim):
                        o = lst[i]
                        if o.indeg == 0:
                            st = o.ready if o.ready > fe else fe
                            if st <= mst + SLACK and (co is None or o.cnt > co.cnt):
                                cst, co, ci = st, o, i
                else:
                    for i in range(lim):
                        o = lst[i]
                        if o.indeg == 0:
                            st = o.ready if o.ready > fe else fe
                            if co is None or st < cst:
                                cst, co, ci = st, o, i
                            if st <= fe:
                                break
                    if co is None:
                        continue
                if bo is None or cst < bst or (cst == bst and co.idx < bo.idx):
                    bst, bo, be, bi = cst, co, e, ci
            o = bo
            del pend[be][bi]
            out[be].append(o)
            if o.is_dma:
                free[be] = bst + 120.0
                fin = bst + o.cost
            elif be == "pe":
                free[be] = bst + o.cost
                fin = bst + o.cost + 120.0
            else:
                free[be] = bst + o.cost
                fin = free[be]
            o.fin = (bst, fin, o.ready >= bst - 1e-6)
            for s_ in o.succ:
                s_.indeg -= 1
                r = fin if (s_.eng == be and not o.is_dma) else fin + LAT
                if r > s_.ready:
                    s_.ready = r
                    s_.dsem = o
            n -= 1
        if os.environ.get("BASS_SCHED_STATS"):
            busy = {e: sum(o.cost if not o.is_dma else 120.0 for o in out[e]) for e in engs}
            print("SCHED region: n=%d makespan=%.0f us busy(us)=%s" % (len(ops), max(free.values()) / 1e3, {e: round(v / 1e3) for e, v in busy.items()}))
        return out

    def finalize(self):
        if self.cur:
            self.regions.append(self.cur)
            self.cur = []
        sched = os.environ.get("BASS_NOSCHED", "") == ""
        for ops in self.regions:
            if sched:
                out = self.schedule_region(ops)
            else:
                out = {e: [o for o in ops if o.eng == e] for e in self.ENGS}
            last = {}
            dmas = []
            for e in self.ENGS:
                for o in out[e]:
                    if o.is_dma:
                        dmas.append(o)
                    else:
                        last[e] = o
                self.ops[e].extend(out[e])
            for e in self.ENGS:
                bo = Op()
                bo.eng = e
                bo.fn = None
                bo.site = None
                bo.need_inc = False
                bo.cnt = 0
                bo.is_dma = False
                bo.deps = [o for k, o in last.items() if k != e] + dmas
                self.ops[e].append(bo)
        for e in self.ndma:
            i = 0
            for o in self.ops[e]:
                if o.is_dma:
                    o.dsem = self.dsems[e][i % self.NDSEM]
                    o.dval = 16 * (i // self.NDSEM + 1)
                    o.prev_dval = o.dval - 16
                    i += 1
            self.ndma[e] = i
        for e in self.ENGS:
            for o in self.ops[e]:
                for d in o.deps:
                    if not d.is_dma:
                        d.need_inc = True

    def emit(self):
        nc = self.nc
        self.finalize()
        for e in ("pe", "act", "dve", "pool"):
            c = 0
            for o in self.ops[e]:
                if o.is_dma or o.fn is None:
                    o.cnt = c
                    continue
                if o.need_inc:
                    c += 1
                o.cnt = c
        prog = self

        def run(e, eng):
            known = {}
            for o in prog.ops[e]:
                waits = []
                for d in o.deps:
                    if d.is_dma:
                        waits.append((d.dsem, d.dval))
                    else:
                        waits.append((prog.esem[d.eng], d.cnt))
                if o.is_dma and o.prev_dval > 0:
                    waits.append((o.dsem, o.prev_dval))
                for (s, v) in waits:
                    k = id(s)
                    if known.get(k, 0) < v:
                        eng.wait_ge(s, v)
                        known[k] = v
                if o.fn is None:
                    continue
                ins = o.fn(eng)
                if DEBUG_SITES:
                    SITES[str(ins.ins.name)] = o.site
                if o.is_dma:
                    ins.then_inc(o.dsem, 16)
                elif o.need_inc:
                    ins.then_inc(prog.esem[e], 1)
            if e in prog.ndma:
                n = prog.ndma[e]
                for i in range(min(n, prog.NDSEM)):
                    uses = (n - 1 - i) // prog.NDSEM + 1
                    s = prog.dsems[e][i]
                    if known.get(id(s), 0) < 16 * uses:
                        eng.wait_ge(s, 16 * uses)

        with nc.Block() as block:
            @block.sync
            def _(eng):
                run("sp", eng)

            @block.tensor
            def _(eng):
                run("pe", eng)

            @block.scalar
            def _(eng):
                run("act", eng)

            @block.vector
            def _(eng):
                run("dve", eng)

            @block.gpsimd
            def _(eng):
                run("pool", eng)


class Arena:
    def __init__(self, big, nbytes):
        self.big = big
        self.nbytes = nbytes
        self.off = 0
        self.marks = []

    def mark(self):
        self.marks.append(self.off)

    def release(self):
        self.off = self.marks.pop()

    def alloc(self, cols, dt, parts=128):
        esz = 4 if dt == F32 else 2
        self.off = (self.off + 63) // 64 * 64
        nb = cols * esz
        assert self.off + nb <= self.nbytes, ("arena overflow", self.off, nb, self.nbytes)
        ap = self.big[0:parts, self.off // 2:(self.off + nb) // 2]
        if dt == F32:
            ap = ap.bitcast(F32)
        self.off += nb
        return V(ap)


CF = {}
CB = {}


def _layout(spec, table):
    off = 0
    for name, n in spec:
        table[name] = (off, n)
        off += n
    return off


_CF_SPEC = [("ident", 128), ("m_sl", 128), ("m_su", 128), ("m_ui", 128), ("m_blk", 128), ("maskc", 8),
            ("negc_p", 1), ("negc_s", 1), ("rowm_s", 1), ("one", 1), ("eps6", 1), ("eps24", 1), ("gneps", 1),
            ("rope_s", 128), ("ge_j", 4), ("le_j", 4), ("eq_j", 4)]
_CB_SPEC = [("ident", 128), ("am_ge", 128), ("am_le", 128)]
NCF = _layout(_CF_SPEC, CF)
NCB = _layout(_CB_SPEC, CB)


def host_consts(T):
    NT = T // 128
    p = np.arange(128)
    row = p[:, None]
    col = p[None, :]
    same = (row // 16) == (col // 16)
    cf = np.zeros((128, NCF), np.float32)

    def put(name, arr):
        o, n = CF[name]
        cf[:, o:o + n] = arr

    put("ident", np.eye(128))
    put("m_sl", same & (col < row))
    put("m_su", same & (row < col))
    put("m_ui", same & (row <= col))
    put("m_blk", same)
    put("maskc", (row // 16) == np.arange(8)[None, :])
    slot = p % 16
    rowm = ((slot >= 2) & (slot < 6)).astype(np.float32)[:, None]
    put("negc_p", -EXPM05)
    put("negc_s", -EXPM05 * rowm)
    put("rowm_s", rowm)
    put("one", 1.0)
    put("eps6", 1e-6)
    put("eps24", 1e-24)
    put("gneps", 64e-5)
    jj = np.arange(4)[None, :]
    put("ge_j", row >= jj)
    put("le_j", row <= jj)
    put("eq_j", row == jj)
    half = DH // 2
    inv = np.exp(-math.log(10000.0) * np.arange(half, dtype=np.float32) * 2.0 / DH).astype(np.float32)
    pos_s = (PAST + np.clip(slot - 2, 0, 3)).astype(np.float32)
    ang = pos_s[:, None] * inv[None, :]
    put("rope_s", np.concatenate([np.cos(ang), np.cos(ang), np.sin(ang), np.sin(ang)], 1))
    pos = (np.arange(NT)[None, :] * 128 + p[:, None]).astype(np.float32)
    angp = pos[:, :, None] * inv[None, None, :]
    rope_p = np.concatenate([np.cos(angp), np.cos(angp), np.sin(angp), np.sin(angp)], 2).astype(np.float32).reshape(128, NT * 128)
    cb = np.zeros((128, NCB), np.float32)

    def putb(name, arr):
        o, n = CB[name]
        cb[:, o:o + n] = arr

    putb("ident", np.eye(128))
    putb("am_ge", row >= col)
    putb("am_le", row <= col)
    return cf, cb.astype(ml_dtypes.bfloat16), rope_p


class Builder:
    def __init__(self, T, NS=16, phases="ABCD", debug=False):
        self.T = T
        self.NS = NS
        self.NT = T // 128
        self.NST = NS * 16 // 128
        self.NTT = self.NT + self.NST
        self.phases = phases
        self.debug = debug
        self.nc = bass.Bass("TRN2", target_bir_lowering=False)
        self.dr = {}

    def din(self, name, shape, dt=F32):
        self.dr[name] = V(self.nc.dram_tensor(name, list(shape), dt, kind="ExternalInput").ap())
        return self.dr[name]

    def dout(self, name, shape):
        self.dr[name] = V(self.nc.dram_tensor(name, list(shape), F32, kind="ExternalOutput").ap())
        return self.dr[name]

    def dscr(self, name, shape, dt=F32, dbg=False):
        if dbg and self.debug:
            return self.dout(name, shape)
        self.dr[name] = V(self.nc.dram_tensor(name, list(shape), dt).ap())
        return self.dr[name]

    def declare(self):
        T, NS, NTT = self.T, self.NS, self.NTT
        d = self.din
        d("xp", [T, D]); d("xs", [NS * 4, D]); d("st_shift", [NS, D]); d("st_wkv", [NS, H, DH, DH])
        for g, w in enumerate(WINDOWS):
            d("ck%d" % g, [NS, w, D]); d("cv%d" % g, [NS, w, D])
        d("st_conv", [2, NS, 2, FF])
        d("norm_mix", [2, D]); d("norm_ffn", [2, D]); d("norm_final", [1, D])
        d("rwkv_mu", [6, D]); d("rwkv_w_rkv", [3, D, D]); d("rwkv_w0", [1, D]); d("rwkv_w1", [D, 64]); d("rwkv_w2", [64, D])
        d("rwkv_a0", [1, D]); d("rwkv_a1", [D, 64]); d("rwkv_a2", [64, D]); d("rwkv_g1", [D, 128]); d("rwkv_g2", [128, D])
        d("rwkv_k_k", [1, D]); d("rwkv_k_a", [1, D]); d("rwkv_r_k", [1, D]); d("rwkv_ln_w", [1, D]); d("rwkv_ln_b", [1, D])
        d("rwkv_w_o", [D, D]); d("attn_w_in", [D, 9 * D]); d("attn_w_o", [D, D])
        d("ffn_w_up", [2, D, 2 * FF]); d("ffn_conv_w", [2, 3, FF]); d("ffn_conv_b", [2, FF]); d("ffn_w_down", [2, FF, D])
        d("cf", [128, NCF]); d("cb", [128, NCB], BF16); d("rope_p", [128, self.NT * 128])
        o = self.dout
        o("y_p", [T, D]); o("y_s", [NS * 4, D]); o("p_shift", [1, D]); o("p_wkv", [H, DH, DH])
        for g, w in enumerate(WINDOWS):
            o("p_k%d" % g, [min(w, T), D]); o("p_v%d" % g, [min(w, T), D])
        o("p_conv", [2, 2, FF]); o("s_shift", [NS, D]); o("s_wkv", [NS, H, DH, DH])
        for g in range(3):
            o("s_k%d" % g, [NS * 4, D]); o("s_v%d" % g, [NS * 4, D])
        o("s_conv", [2, NS, 2, FF])
        s = self.dscr
        s("xs_pad", [self.NST * 128, D])
        for nm, prod, cons in (("R1", "A", "B"), ("R2", "B", "C"), ("R3", "C", "D")):
            if prod in self.phases:
                s(nm, [NTT * 128, D], dbg=True)
            elif cons in self.phases:
                d(nm, [NTT * 128, D])
            else:
                s(nm, [NTT * 128, D])
        s("ys_pad", [self.NST * 128, D])
        if self.debug:
            o("dbg_y", [NTT * 128, D]); o("dbg_g", [NTT * 128, D]); o("dbg_yn", [NTT * 128, D])

    def res_src(self, name, i):
        if name == "R0":
            if i < self.NT:
                return self.dr["xp"][i * 128:(i + 1) * 128, :]
            return self.dr["xs_pad"][(i - self.NT) * 128:(i - self.NT + 1) * 128, :]
        return self.dr[name][i * 128:(i + 1) * 128, :]

    def build(self):
        nc = self.nc
        self.declare()
        with ExitStack() as st:
            self.P = P = Prog(nc, st)
            big = st.enter_context(nc.sbuf_tensor("arena", [128, 106000], BF16))
            self.A = A = Arena(big, 212000)
            pp = [st.enter_context(nc.psum_tensor("pp%d" % i, [128, 1024], F32)) for i in range(4)]
            self.bank = []
            for i in range(4):
                for h in range(2):
                    self.bank.append(V(pp[i][:, h * 512:(h + 1) * 512]))
            self.dbank = [V(pp[i][:, :], toks=self.bank[2 * i].toks + self.bank[2 * i + 1].toks) for i in range(4)]
            self.cf = A.alloc(NCF, F32)
            self.cb = A.alloc(NCB, BF16)
            P.dma("sp", self.cf, self.dr["cf"])
            P.dma("sp", self.cb, self.dr["cb"])
            self.zero = A.alloc(D, F32)
            P.memset("pool", self.zero, 0.0)
            for stl in range(self.NST):
                P.dma("sp", self.dr["xs_pad"][stl * 128:(stl + 1) * 128, :], self.zero)
            P.dma("sp", self.dr["xs_pad"].re("(s q) d -> s q d", q=16)[:, 2:6, :],
                  self.dr["xs"].re("(s j) d -> s j d", j=4))
            P.barrier()
            if "A" in self.phases:
                A.mark(); self.phase_rwkv(); A.release(); P.barrier()
            if "B" in self.phases:
                A.mark(); self.phase_ffn(0, "R1", "R2", final=False); A.release(); P.barrier()
            if "C" in self.phases:
                A.mark(); self.phase_attn(); A.release(); P.barrier()
            if "D" in self.phases:
                A.mark(); self.phase_ffn(1, "R3", None, final=True); A.release(); P.barrier()
            P.emit()
        return nc

    def c32(self, name):
        o, n = CF[name]
        return self.cf[:, o:o + n]

    def c16(self, name):
        o, n = CB[name]
        return self.cb[:, o:o + n]

    def load_bcast(self, dst, src_row):
        n = src_row.ap.shape[-1]
        self.P.dma("sp", dst, V(src_row.ap.to_broadcast([128, n]), src_row.toks))

    def rmsnorm(self, x, gbc, out, junk, small):
        P = self.P
        P.act(junk, x, AF.Square)
        P.red("dve", small[:, 0:1], junk)
        P.act(small[:, 1:2], small[:, 0:1], AF.Ln, scale=1.0 / D, bias=self.c32("eps6"))
        P.act(small[:, 2:3], small[:, 1:2], AF.Exp, scale=-0.5)
        P.stt("dve", out, x, small[:, 2:3], gbc, ALU.mult, ALU.mult)

    def transpose_bf(self, src, dstT, col0, bank):
        P = self.P
        pb = bank.bitcast(BF16)
        for kc in range(8):
            P.tr(pb[:, kc * 128:(kc + 1) * 128], src[:, kc * 128:(kc + 1) * 128], self.c16("ident"))
        P.copy("act", dstT[:, :, col0:col0 + 128], pb.re("p (k t) -> p k t", k=8))

    def phase_ffn(self, layer, rin, rout, final):
        P, A, NT, NST = self.P, self.A, self.NT, self.NST
        dr = self.dr
        wup = A.alloc(8 * 2 * FF, BF16)
        wdn = A.alloc(NFC * D, BF16)
        wup3 = wup.re("p (k n) -> p k n", k=8)
        wdn3 = wdn.re("p (k n) -> p k n", k=NFC)
        for k in range(8):
            P.dma("pool", wup3[:, k, :], dr["ffn_w_up"][layer, k * 128:(k + 1) * 128, :])
        for k0 in range(0, NFC, 6):
            k1 = min(NFC, k0 + 6)
            P.dma("pool", wdn3[:, k0:k1, :], dr["ffn_w_down"][layer, k0 * 128:k1 * 128, :].re("(k p) n -> p k n", p=128))
        import os
        STOP = os.environ.get("FFN_STOP", "")
        if STOP == "w":
            return
        gbc = A.alloc(D, F32)
        self.load_bcast(gbc, dr["norm_ffn"][layer:layer + 1, :])
        gfin = None
        if final:
            gfin = A.alloc(D, F32)
            self.load_bcast(gfin, dr["norm_final"][0:1, :])
        if STOP == "bc":
            return
        cw = A.alloc(NFC * 4, F32)
        cw3 = cw.re("p (c j) -> p c j", j=4)
        tmpF = A.alloc(512, F32)
        NSJ = self.NS * 2
        stT = A.alloc(NFC * NSJ, F32)
        stT3 = stT.re("p (c n) -> p c n", c=NFC)
        bk = self.bank[0]
        bk2 = self.bank[1]
        for c0 in range(0, NFC, 4):
            c1 = min(NFC, c0 + 4)
            w = (c1 - c0) * 128
            P.dma("sp", tmpF[0:3, 0:w], dr["ffn_conv_w"][layer, :, c0 * 128:c1 * 128])
            P.dma("sp", tmpF[3:4, 0:w], dr["ffn_conv_b"][layer:layer + 1, c0 * 128:c1 * 128])
            for c in range(c0, c1):
                P.tr(bk[:, c * 4:(c + 1) * 4], tmpF[0:4, (c - c0) * 128:(c - c0 + 1) * 128], self.c32("ident")[0:4, 0:4])
        P.copy("act", cw, bk[:, 0:NFC * 4])
        if STOP == "cw":
            return
        for c0 in range(0, NFC, 4):
            c1 = min(NFC, c0 + 4)
            w = (c1 - c0) * 128
            P.dma("sp", tmpF[0:NSJ, 0:w], dr["st_conv"][layer].re("s j f -> (s j) f")[:, c0 * 128:c1 * 128])
            for c in range(c0, c1):
                P.tr(bk2[:, (c - c0) * NSJ:(c - c0 + 1) * NSJ], tmpF[0:NSJ, (c - c0) * 128:(c - c0 + 1) * 128], self.c32("ident")[0:NSJ, 0:NSJ])
            P.copy("act", stT3[:, c0:c1, :], bk2[:, 0:(c1 - c0) * NSJ].re("p (c n) -> p c n", n=NSJ))
        if STOP == "st":
            return
        carry = A.alloc(NFC * 2, F32)
        carry3 = carry.re("p (c j) -> p c j", j=2)
        P.memset("pool", carry, 0.0)
        gsave = A.alloc(NFC * (NSJ + 2), F32)
        gsave3 = gsave.re("p (c n) -> p c n", c=NFC)
        xnT = A.alloc(8 * 512, BF16)
        xnT3 = xnT.re("p (k t) -> p k t", k=8)
        hT = A.alloc(NFC * 512, BF16)
        hT3 = hT.re("p (c t) -> p c t", c=NFC)
        xr = [A.alloc(D, F32) for _ in range(2)]
        xn = [A.alloc(D, BF16) for _ in range(2)]
        junk = hT[:, 0:2 * D].bitcast(F32)
        small = [A.alloc(4, F32) for _ in range(2)]
        gext = [A.alloc(514, F32) for _ in range(2)]
        acc = [A.alloc(512, F32) for _ in range(2)]
        yt = xr
        ysm = [A.alloc(4, F32) for _ in range(2)]
        macros = []
        i = 0
        while i < NT:
            n = min(4, NT - i)
            macros.append((i, n, False))
            i += n
        macros.append((NT, NST, True))
        cnt = 0
        for (t0, n, is_s) in macros:
            ntok = n * 128
            for j in range(n):
                b = cnt % 2
                cnt += 1
                P.dma("sp", xr[b], self.res_src(rin, t0 + j))
                if STOP == "L0":
                    continue
                self.rmsnorm(xr[b], gbc, xn[b], junk, small[b])
                if STOP == "L1":
                    continue
                self.transpose_bf(xn[b], xnT3, j * 128, self.bank[1])
            if STOP in ("L0", "L1", "L2"):
                continue
            for c in range(NFC):
                b = c % 2
                pg = self.bank[2 + b]
                pv = self.bank[4 + b]
                for k in range(8):
                    P.mm(pg[:, 0:ntok], wup3[:, k, c * 128:(c + 1) * 128], xnT3[:, k, 0:ntok], start=(k == 0), stop=(k == 7))
                for k in range(8):
                    P.mm(pv[:, 0:ntok], wup3[:, k, FF + c * 128:FF + (c + 1) * 128], xnT3[:, k, 0:ntok], start=(k == 0), stop=(k == 7))
                ge = gext[b]
                if STOP == "L3a":
                    continue
                P.copy("pool", ge[:, 0:2], carry3[:, c, :])
                P.copy("act", ge[:, 2:2 + ntok], pg[:, 0:ntok])
                if STOP == "L3":
                    continue
                if is_s:
                    P.copy("pool", ge[:, 2:2 + ntok].re("p (s q) -> p s q", q=16)[:, :, 0:2],
                           stT3[:, c, :].re("p (s j) -> p s j", j=2))
                a = acc[b]
                if STOP == "L3b":
                    continue
                P.ts("dve", a[:, 0:ntok], ge[:, 2:2 + ntok], cw3[:, c, 2:3], ALU.mult, cw3[:, c, 3:4], ALU.add)
                if STOP == "L3c":
                    continue
                P.stt("dve", a[:, 0:ntok], ge[:, 1:1 + ntok], cw3[:, c, 1:2], a[:, 0:ntok], ALU.mult, ALU.add)
                P.stt("dve", a[:, 0:ntok], ge[:, 0:ntok], cw3[:, c, 0:1], a[:, 0:ntok], ALU.mult, ALU.add)
                if STOP == "L4":
                    continue
                P.act(a[:, 0:ntok], a[:, 0:ntok], AF.Silu)
                P.tt("dve", hT3[:, c, 0:ntok], a[:, 0:ntok], pv[:, 0:ntok], ALU.mult)
                if is_s:
                    P.copy("pool", gsave3[:, c, 0:NSJ].re("p (s j) -> p s j", j=2),
                           ge[:, 2:2 + ntok].re("p (s q) -> p s q", q=16)[:, :, 4:6])
                else:
                    P.copy("pool", carry3[:, c, :], ge[:, ntok:ntok + 2])
                    if t0 + n == NT:
                        P.copy("pool", gsave3[:, c, NSJ:NSJ + 2], ge[:, ntok:ntok + 2])
            if STOP in ("L3a", "L3", "L3b", "L3c", "L4", "L6"):
                continue
            for j in range(n):
                b = cnt % 2
                cnt += 1
                ti = t0 + j
                P.dma("sp", xr[b], self.res_src(rin, ti))
                for hh in range(2):
                    pd = self.bank[6 + hh]
                    for c in range(NFC):
                        P.mm(pd, hT3[:, c, j * 128:(j + 1) * 128], wdn3[:, c, hh * 512:(hh + 1) * 512], start=(c == 0), stop=(c == NFC - 1))
                    P.tt("dve", yt[b][:, hh * 512:(hh + 1) * 512], pd, xr[b][:, hh * 512:(hh + 1) * 512], ALU.add)
                if not final:
                    P.dma("sp", self.res_src(rout, ti), yt[b])
                else:
                    self.rmsnorm(yt[b], gfin, yt[b], xnT[:, 0:2 * D].bitcast(F32), ysm[b])
                    if ti < NT:
                        P.dma("sp", dr["y_p"][ti * 128:(ti + 1) * 128, :], yt[b])
                    else:
                        P.dma("sp", dr["ys_pad"][(ti - NT) * 128:(ti - NT + 1) * 128, :], yt[b])
        if STOP == "main":
            return
        if final:
            P.dma("sp", dr["y_s"].re("(s j) d -> s j d", j=4), dr["ys_pad"].re("(s q) d -> s q d", q=16)[:, 2:6, :])
        NG = NSJ + 2
        sc_rows = dr["s_conv"][layer].re("s j f -> (s j) f")
        for c0 in range(0, NFC, 4):
            c1 = min(NFC, c0 + 4)
            w = (c1 - c0) * 128
            bk = self.bank[0]
            for c in range(c0, c1):
                P.tr(bk[0:NG, (c - c0) * 128:(c - c0 + 1) * 128], gsave3[:, c, :], self.c32("ident"))
            P.copy("act", tmpF[0:NG, 0:w], bk[0:NG, 0:w])
            P.dma("sp", sc_rows[:, c0 * 128:c1 * 128], tmpF[0:NSJ, 0:w])
            P.dma("sp", dr["p_conv"][layer][:, c0 * 128:c1 * 128], tmpF[NSJ:NSJ + 2, 0:w])


    def nb(self):
        self._nb = (getattr(self, "_nb", -1) + 1) % 6
        return self.bank[self._nb]

    def phase_rwkv(self):
        P, A, NT, NST, NS = self.P, self.A, self.NT, self.NST, self.NS
        dr = self.dr
        ident = self.c32("ident")
        identb = self.c16("ident")
        m_sl, m_su, m_ui, m_blk, maskc = (self.c32(k) for k in ("m_sl", "m_su", "m_ui", "m_blk", "maskc"))
        o_su = CF["m_su"][0]
        mask2 = self.cf[:, o_su:o_su + 256]
        wrkv = A.alloc(3 * 8 * D, BF16); wrkv4 = wrkv.re("p (q k n) -> p q k n", q=3, k=8)
        wo = A.alloc(8 * D, BF16); wo3 = wo.re("p (k n) -> p k n", k=8)
        w1 = A.alloc(8 * 64, BF16); w13 = w1.re("p (k n) -> p k n", k=8)
        a1 = A.alloc(8 * 64, BF16); a13 = a1.re("p (k n) -> p k n", k=8)
        g1 = A.alloc(8 * 128, BF16); g13 = g1.re("p (k n) -> p k n", k=8)
        w2 = A.alloc(D, BF16); a2 = A.alloc(D, BF16); g2 = A.alloc(D, BF16)
        for q in range(3):
            P.dma("pool", wrkv4[:, q, :, :], dr["rwkv_w_rkv"][q].re("(k p) n -> p k n", p=128))
        P.dma("pool", wo3, dr["rwkv_w_o"].re("(k p) n -> p k n", p=128))
        P.dma("pool", w13, dr["rwkv_w1"].re("(k p) n -> p k n", p=128))
        P.dma("pool", a13, dr["rwkv_a1"].re("(k p) n -> p k n", p=128))
        P.dma("pool", g13, dr["rwkv_g1"].re("(k p) n -> p k n", p=128))
        P.dma("pool", w2[0:64, :], dr["rwkv_w2"])
        P.dma("pool", a2[0:64, :], dr["rwkv_a2"])
        P.dma("pool", g2, dr["rwkv_g2"])
        bc = {}
        for nm, src in (("gmix", dr["norm_mix"][0:1, :]), ("w0", dr["rwkv_w0"]), ("a0", dr["rwkv_a0"]), ("kk", dr["rwkv_k_k"]),
                        ("ka", dr["rwkv_k_a"]), ("rk", dr["rwkv_r_k"]), ("lnw", dr["rwkv_ln_w"]), ("lnb", dr["rwkv_ln_b"])):
            bc[nm] = A.alloc(D, F32)
            self.load_bcast(bc[nm], src)
        FW = 8 * 129
        Ft = [A.alloc(FW, F32) for _ in range(8)]
        F = [f[:, 0:D] for f in Ft]
        mu_sb = A.alloc(48, F32); mu3 = mu_sb.re("p (q k) -> p q k", q=6)
        P.dma("sp", F[0][0:6, :], dr["rwkv_mu"])
        bk = self.nb()
        for k in range(8):
            P.tr(bk[:, k * 6:(k + 1) * 6], F[0][0:6, k * 128:(k + 1) * 128], ident[0:6, 0:6])
        P.copy("act", mu_sb.re("p (q k) -> p k q", q=6), bk[:, 0:48].re("p (k q) -> p k q", q=6))
        carry = A.alloc(8, F32); P.memset("pool", carry, 0.0)
        small = A.alloc(64, F32)
        PcT = A.alloc(64, F32); PcT3 = PcT.re("p (a c) -> p a c", a=8)
        prodb = A.alloc(6 * 8 * 128, BF16)
        xsT = prodb.re("p (q k t) -> p q k t", q=6, k=8)
        At, Bt, Kt, Rt, Bp = (prodb[:, i * D:(i + 1) * D] for i in range(5))
        yb = prodb[:, 0:D]
        yT = prodb[:, D:2 * D]; yT3 = yT.re("p (k t) -> p k t", k=8)
        QT = A.alloc(8 * 4 * 128, BF16); QT4 = QT.re("p (a q t) -> p a q t", a=8, q=4)
        Vblk = A.alloc(8 * 2 * 128, BF16); Vblk4 = Vblk.re("p (a h n) -> p a h n", a=8, h=2)
        P.memset("pool", Vblk, 0.0)
        S32p = [A.alloc(128, F32) for _ in range(8)]
        Sbfp = [A.alloc(128, BF16) for _ in range(8)]
        for t_ in S32p + Sbfp:
            P.memset("pool", t_, 0.0)
        fence_scr = A.alloc(2, F32)
        alias_toks = []

        def sub(fk, off, cols, dt):
            if dt == BF16:
                ap = Ft[fk].ap.bitcast(BF16)[:, off // 2:off // 2 + cols]
            else:
                ap = Ft[fk].ap[:, off // 4:off // 4 + cols]
            v_ = V(ap, toks=[Tok()])
            alias_toks.extend(v_.toks)
            return v_
        XBs, MBs, ARKs, P2s, P4s, P8s, NHs, Bblks, tmpDs = [], [], [], [], [], [], [], [], []
        for hs in range(4):
            if hs < 2:
                XBs.append((A.alloc(320, BF16), A.alloc(320, BF16)))
                MBs.append(A.alloc(256, BF16)); ARKs.append(A.alloc(128, BF16))
                P2s.append(A.alloc(256, BF16)); P4s.append(A.alloc(256, BF16)); P8s.append(A.alloc(128, BF16))
                NHs.append(A.alloc(128, BF16)); Bblks.append(A.alloc(512, BF16)); tmpDs.append(A.alloc(64, F32))
            else:
                fk = 1 if hs == 2 else 2
                XBs.append((sub(fk, 0, 320, BF16), sub(fk, 640, 320, BF16)))
                MBs.append(sub(fk, 1280, 256, BF16)); ARKs.append(sub(fk, 1792, 128, BF16))
                P2s.append(sub(fk, 2048, 256, BF16)); P4s.append(sub(fk, 2560, 256, BF16)); P8s.append(sub(fk, 3072, 128, BF16))
                NHs.append(sub(fk, 3328, 128, BF16))
                o6 = (hs - 2) * 1280
                Bblks.append(sub(6, o6, 512, BF16)); tmpDs.append(sub(6, o6 + 1024, 64, F32))
        Apair = [A.alloc(128, BF16) for _ in range(2)]
        TTp = [A.alloc(D, BF16) for _ in range(2)]; TTp3 = [t_.re("p (c n) -> p c n", c=8) for t_ in TTp]
        Db = [[A.alloc(D, BF16) for _ in range(2)] for _ in range(2)]
        Db3 = [[d_.re("p (c n) -> p c n", c=8) for d_ in dd_] for dd_ in Db]
        RTb = [A.alloc(1152, BF16) for _ in range(2)]
        WT = [[A.alloc(128, BF16) for _ in range(2)] for _ in range(2)]
        hw = A.alloc(128, BF16); ha = A.alloc(128, BF16); hg = A.alloc(128, BF16)
        NSC = 2
        Sc32 = [A.alloc(128, F32) for _ in range(NSC)]
        Scb = [A.alloc(128, BF16) for _ in range(NSC)]
        for t_ in TTp + Db[0] + Db[1] + RTb:
            P.memset("pool", t_, 0.0)

        def fence():
            f_ = fence_scr.ap
            P.add("pool", lambda e: e.memset(f_, 0.0), wr=F[1].toks + F[2].toks + F[6].toks + alias_toks + fence_scr.toks, cost=150.0)
        ST_in = self.dscr("wkv_in_T", [NS * 8 * 128, 128])
        ST_out = self.dscr("wkv_out_T", [NS * 8 * 128, 128])
        Z = F[1]; Z3 = Z.re("p (a n) -> p a n", a=8)
        Zo = F[2]; Zo3 = Zo.re("p (a n) -> p a n", a=8)
        P.memset("pool", Z, 0.0)
        for s in range(NS):
            for hh in range(2):
                P.dma("sp", Z3[64 * hh:64 * hh + 64, :, 64 * hh:64 * hh + 64],
                      dr["st_wkv"][s].re("(a h) v k -> h v a k", h=2)[hh])
            for a in range(8):
                bk = self.nb()
                P.tr(bk[:, 0:128], Z3[:, a, :], ident)
                P.copy("act", Zo3[:, a, :], bk[:, 0:128])
            P.dma("sp", ST_in[s * 1024:(s + 1) * 1024, :].re("(a p) n -> p a n", p=128), Zo3)

        for ti in range(self.NTT):
            is_s = ti >= NT
            stl = ti - NT
            negc = self.c32("negc_s") if is_s else self.c32("negc_p")
            rowm = self.c32("rowm_s")
            x = F[0]
            P.dma("sp", x, self.res_src("R0", ti))
            xn32 = F[5]
            self.rmsnorm(x, bc["gmix"], xn32, F[7], small[:, 0:4])
            if is_s:
                for s_ in range(8):
                    sq = stl * 8 + s_
                    P.dma("sp", xn32[16 * s_ + 1:16 * s_ + 2, :], dr["st_shift"][sq:sq + 1, :])
                for s_ in range(8):
                    sq = stl * 8 + s_
                    P.dma("sp", dr["s_shift"][sq:sq + 1, :], xn32[16 * s_ + 5:16 * s_ + 6, :])
            elif ti == NT - 1:
                P.dma("sp", dr["p_shift"], xn32[127:128, :])
            xnTe = Ft[4].re("p (k t) -> p k t", k=8)
            for hf in range(2):
                bk = self.nb()
                for k in range(4):
                    kk_ = hf * 4 + k
                    P.tr(bk[:, k * 128:(k + 1) * 128], xn32[:, kk_ * 128:(kk_ + 1) * 128], ident)
                P.copy("act", xnTe[:, hf * 4:hf * 4 + 4, 1:129], bk.re("p (k t) -> p k t", k=4))
            P.copy("pool", xnTe[:, :, 0:1], carry.re("p (k o) -> p k o", o=1))
            P.copy("pool", carry.re("p (k o) -> p k o", o=1), xnTe[:, :, 128:129])
            xx = F[6].re("p (k t) -> p k t", k=8)
            mt = F[7].re("p (k t) -> p k t", k=8)
            P.tt("pool", xx, xnTe[:, :, 0:128], xnTe[:, :, 1:129], ALU.subtract)
            for q in range(6):
                for k in range(8):
                    P.stt("dve", xsT[:, q, k, :], xx[:, k, :], mu3[:, q, k:k + 1], xnTe[:, k, 1:129], ALU.mult, ALU.add)
            bk = self.nb()
            for k in range(8):
                P.mm(bk[0:64, 0:128], w13[:, k, :], xsT[:, 3, k, :], start=(k == 0), stop=(k == 7))
            for k in range(8):
                P.mm(bk[0:64, 128:256], a13[:, k, :], xsT[:, 4, k, :], start=(k == 0), stop=(k == 7))
            for k in range(8):
                P.mm(bk[:, 256:384], g13[:, k, :], xsT[:, 5, k, :], start=(k == 0), stop=(k == 7))
            P.act(hw[0:64, :], bk[0:64, 0:128], AF.Tanh)
            P.copy("act", ha[0:64, :], bk[0:64, 128:256])
            P.act(hg, bk[:, 256:384], AF.Sigmoid)
            r32, k32, v32, g32, lw, a32, kkn = F[0], F[1], F[3], F[4], F[5], F[6], F[2]
            for q, dst in ((0, r32), (1, k32), (2, v32)):
                for hf in range(2):
                    bk = self.nb()
                    for k in range(8):
                        P.mm(bk, xsT[:, q, k, :], wrkv4[:, q, k, hf * 512:(hf + 1) * 512], start=(k == 0), stop=(k == 7))
                    P.copy("act", dst[:, hf * 512:(hf + 1) * 512], bk)
            for hf in range(2):
                cs = slice(hf * 512, (hf + 1) * 512)
                bk = self.nb()
                P.mm(bk, hw[0:64, :], w2[0:64, cs])
                P.tt("dve", lw[:, cs], bk, bc["w0"][:, cs], ALU.add)
                bk = self.nb()
                P.mm(bk, ha[0:64, :], a2[0:64, cs])
                P.tt("dve", a32[:, cs], bk, bc["a0"][:, cs], ALU.add)
                bk = self.nb()
                P.mm(bk, hg, g2[:, cs])
                P.copy("act", g32[:, cs], bk)
            P.act(lw, lw, AF.Sigmoid)
            P.ts("dve", lw, lw, negc, ALU.mult)
            P.act(a32, a32, AF.Sigmoid)
            if is_s:
                P.ts("pool", v32, v32, rowm, ALU.mult)
            v4 = v32.re("p (a h n) -> p a h n", a=8, h=2)
            P.copy("pool", Vblk4[:, :, 0, 0:64], v4[:, :, 0, :])
            P.copy("pool", Vblk4[:, :, 1, 64:128], v4[:, :, 1, :])
            t3 = F[7]
            P.tt("pool", kkn, k32, bc["kk"], ALU.mult)
            P.tt("pool", t3, kkn, kkn, ALU.mult)
            P.red("dve", small[:, 16:32], t3.re("p (h n) -> p h n", h=16))
            P.act(small[:, 16:32], small[:, 16:32], AF.Ln, bias=self.c32("eps24"))
            P.act(small[:, 16:32], small[:, 16:32], AF.Exp, scale=-0.5)
            if is_s:
                P.ts("dve", small[:, 16:32], small[:, 16:32], rowm, ALU.mult)
            P.tt("dve", kkn.re("p (h n) -> p h n", h=16), kkn.re("p (h n) -> p h n", h=16),
                 small[:, 16:32].bc(2, [128, 16, 64]), ALU.mult)
            P.stt("dve", t3, a32, -1.0, bc["ka"], ALU.add, ALU.mult)
            P.stt("dve", k32, t3, 1.0, k32, ALU.add, ALU.mult)
            if is_s:
                P.ts("pool", k32, k32, rowm, ALU.mult)
            P.tt("pool", t3, r32, bc["rk"], ALU.mult)
            P.tt("pool", t3, t3, k32, ALU.mult)
            P.red("dve", small[:, 32:48], t3.re("p (h n) -> p h n", h=16))
            P.tt("pool", a32, a32, kkn, ALU.mult)
            bk = self.nb()
            for a in range(8):
                P.mm(bk[:, a * 8:(a + 1) * 8], lw[:, a * 128:(a + 1) * 128], maskc)
            P.act(PcT, bk[:, 0:64], AF.Exp)
            def cum(mask, hf):
                b_ = self.nb()
                P.mm(b_, mask, lw[:, hf * 512:(hf + 1) * 512])
                return b_
            for hf in range(2):
                cs = slice(hf * 512, (hf + 1) * 512)
                bL = cum(m_ui, hf)
                P.act(t3[:, cs], bL, AF.Exp)
                P.tt("dve", Rt[:, cs], r32[:, cs], t3[:, cs], ALU.mult)
            for hf in range(2):
                cs = slice(hf * 512, (hf + 1) * 512)
                bL = cum(m_ui, hf)
                P.act(t3[:, cs], bL, AF.Exp, scale=-1.0)
                P.tt("dve", Bt[:, cs], a32[:, cs], t3[:, cs], ALU.mult)
                P.tt("pool", Kt[:, cs], k32[:, cs], t3[:, cs], ALU.mult)
            for hf in range(2):
                cs = slice(hf * 512, (hf + 1) * 512)
                bL = cum(m_su, hf)
                P.act(t3[:, cs], bL, AF.Exp)
                P.stt("dve", At[:, cs], kkn[:, cs], -1.0, t3[:, cs], ALU.mult, ALU.mult)
            Kp = lw
            for hf in range(2):
                cs = slice(hf * 512, (hf + 1) * 512)
                bL = cum(m_sl, hf)
                P.act(t3[:, cs], bL, AF.Exp)
                P.tt("dve", Bp[:, cs], a32[:, cs], t3[:, cs], ALU.mult)
            P.tt("pool", Kp, k32, t3, ALU.mult)
            for q, src in enumerate((Kt, Bt, At, Rt)):
                bk = self.nb()
                pb = bk.bitcast(BF16)
                for a in range(8):
                    P.tr(pb[:, a * 128:(a + 1) * 128], src[:, a * 128:(a + 1) * 128], identb)
                P.copy("act", QT4[:, :, q, :], pb.re("p (a t) -> p a t", a=8))
            Yall = F[0]
            fence()
            for a in range(8):
                pp_ = a % 2
                for hh in range(2):
                    hs = (2 * a + hh) % 4
                    MB_, ARK_, P2_, P4_, P8_, NH_ = MBs[hs], ARKs[hs], P2s[hs], P4s[hs], P8s[hs], NHs[hs]
                    rows = slice(64 * hh, 64 * hh + 64)
                    hc = slice((2 * a + hh) * 64, (2 * a + hh) * 64 + 64)
                    xb0, xb1 = XBs[hs]
                    P.copy("pool", xb0[:, 0:64], At[:, hc])
                    bk = self.nb()
                    P.mm(bk[:, 0:256], QT4[rows, a, 2, :], QT4[rows, a, 0:2, :])
                    P.tt("dve", xb0[:, 64:320].re("p (b t) -> p b t", b=2), bk[:, 0:256].re("p (b t) -> p b t", b=2),
                         m_sl.bc(1, [128, 2, 128]), ALU.mult)
                    Nak, Nm = xb0[:, 64:192], xb0[:, 192:320]
                    bk = self.nb()
                    P.mm(bk[:, 0:256], QT4[rows, a, 1, :], QT4[rows, a, 2:4, :])
                    P.tt("dve", MB_, bk[:, 0:256], mask2, ALU.mult)
                    Mm, Arb = MB_[:, 0:128], MB_[:, 128:256]
                    bk = self.nb()
                    P.mm(bk[:, 0:128], QT4[rows, a, 0, :], QT4[rows, a, 3, :])
                    P.tt("dve", ARK_, bk[:, 0:128], m_ui, ALU.mult)
                    bk = self.nb()
                    P.mm(bk[:, 0:128], Nm, Mm)
                    P.mm(bk[:, 128:256], Mm, Nm)
                    P.copy("act", P2_, bk[:, 0:256])
                    bk = self.nb()
                    P.mm(bk[:, 0:128], P2_[:, 128:256], P2_[:, 0:128])
                    P.mm(bk[:, 128:256], P2_[:, 0:128], P2_[:, 128:256])
                    P.copy("act", P4_, bk[:, 0:256])
                    bk = self.nb()
                    P.mm(bk[:, 0:128], P4_[:, 128:256], P4_[:, 0:128])
                    P.copy("act", P8_, bk[:, 0:128])
                    cur, nxt = xb0, xb1
                    for lv, Ml in enumerate((Mm, P2_[:, 0:128], P4_[:, 0:128], P8_)):
                        bk = self.nb()
                        P.mm(bk[:, 0:192], Ml, cur[:, 0:192])
                        if lv < 3:
                            P.tt("dve", nxt[:, 0:192], bk[:, 0:192], cur[:, 0:192], ALU.add)
                            cur, nxt = nxt, cur
                        else:
                            P.tt("dve", Apair[pp_][:, 64 * hh:64 * hh + 64], bk[:, 0:64], cur[:, 0:64], ALU.add)
                            P.tt("dve", NH_, bk[:, 64:192], cur[:, 64:192], ALU.add)
                for hh in range(2):
                    rows = slice(64 * hh, 64 * hh + 64)
                    hc = slice((2 * a + hh) * 64, (2 * a + hh) * 64 + 64)
                    hs = (2 * a + hh) % 4
                    Arb = MBs[hs][:, 128:256]
                    bb = Bblks[hs].re("p (c n) -> p c n", c=8)
                    P.tt("dve", bb, Bp[:, hc].bc(1, [128, 8, 64]), maskc.bc(2, [128, 8, 64]), ALU.mult)
                    bk = self.nb()
                    P.mm(bk, Apair[pp_], Bblks[hs])
                    P.copy("act", TTp3[pp_][rows, :, 64 * hh:64 * hh + 64], bk[rows, :].re("p (c n) -> p c n", c=8))
                    bk = self.nb()
                    P.mm(bk[:, 0:64], NHs[hs], Bp[:, hc])
                    P.tt("dve", tmpDs[hs], bk[:, 0:64], Kp[:, hc], ALU.add)
                    P.tt("dve", Db3[pp_][hh][:, :, 64 * hh:64 * hh + 64], tmpDs[hs].bc(1, [128, 8, 64]), maskc.bc(2, [128, 8, 64]), ALU.mult)
                    bk = self.nb()
                    P.mm(bk[:, 0:128], Apair[pp_], Arb)
                    P.tt("dve", RTb[pp_][rows, :].re("p (c s) -> p c s", s=144)[:, :, 0:16],
                         bk[rows, 0:128].re("p (c t) -> p c t", t=16), QT4[rows, a, 3, :].re("p (c t) -> p c t", t=16), ALU.add)
                    bk = self.nb()
                    P.mm(bk[:, 0:128], NHs[hs], Arb)
                    P.tt("dve", WT[pp_][hh], bk[:, 0:128], ARKs[hs], ALU.add)
                ybk = self.bank[6 + a % 2]
                P.mm(ybk[:, 0:128], WT[pp_][0], Vblk4[:, a, 0, :], start=True, stop=False)
                P.mm(ybk[:, 0:128], WT[pp_][1], Vblk4[:, a, 1, :], start=False, stop=False)
                for c in range(8):
                    if is_s:
                        sq = stl * 8 + c
                        sc32, scb = Sc32[c % NSC], Scb[c % NSC]
                        r0 = (sq * 8 + a) * 128
                        P.dma("sp", sc32, ST_in[r0:r0 + 128, :])
                        P.copy("act", scb, sc32)
                        s_b, s_32 = scb, sc32
                    else:
                        s_b, s_32 = Sbfp[a], S32p[a]
                    P.mm(ybk[:, 0:128], RTb[pp_][:, c * 128:(c + 1) * 128], s_b, start=False, stop=(c == 7))
                    sbk = self.nb()
                    P.mm(sbk[:, 0:128], TTp3[pp_][:, c, :], s_b, start=True, stop=False)
                    P.mm(sbk[:, 0:128], Db3[pp_][0][:, c, :], Vblk4[:, a, 0, :], start=False, stop=False)
                    P.mm(sbk[:, 0:128], Db3[pp_][1][:, c, :], Vblk4[:, a, 1, :], start=False, stop=True)
                    if is_s:
                        P.stt("dve", s_32, s_32, PcT3[:, a, c:c + 1], sbk[:, 0:128], ALU.mult, ALU.add)
                        P.dma("sp", ST_out[r0:r0 + 128, :], s_32)
                    else:
                        P.stt("dve", s_b, s_32, PcT3[:, a, c:c + 1], sbk[:, 0:128], ALU.mult, ALU.add)
                        P.stt("dve", s_32, s_32, PcT3[:, a, c:c + 1], sbk[:, 0:128], ALU.mult, ALU.add)
                P.copy("act", Yall[:, a * 128:(a + 1) * 128], ybk[:, 0:128])
            fence()
            if self.debug:
                P.dma("sp", self.dr["dbg_y"][ti * 128:(ti + 1) * 128, :], Yall)
                P.dma("sp", self.dr["dbg_g"][ti * 128:(ti + 1) * 128, :], g32)
            Y3 = Yall.re("p (h n) -> p h n", h=16)
            sm = small
            P.red("dve", sm[:, 0:16], Y3)
            P.tt("pool", t3, Yall, Yall, ALU.mult)
            P.red("dve", sm[:, 48:64], t3.re("p (h n) -> p h n", h=16))
            P.ts("dve", sm[:, 0:16], sm[:, 0:16], 1.0 / 64, ALU.mult)
            P.tt("dve", sm[:, 16:32], sm[:, 0:16], sm[:, 0:16], ALU.mult)
            P.stt("dve", sm[:, 48:64], sm[:, 48:64], 1.0 / 64, sm[:, 16:32], ALU.mult, ALU.subtract)
            P.act(sm[:, 48:64], sm[:, 48:64], AF.Ln, bias=self.c32("gneps"))
            P.act(sm[:, 48:64], sm[:, 48:64], AF.Exp, scale=-0.5)
            P.tt("dve", Y3, Y3, sm[:, 0:16].bc(2, [128, 16, 64]), ALU.subtract)
            P.tt("dve", Y3, Y3, sm[:, 48:64].bc(2, [128, 16, 64]), ALU.mult)
            P.tt("pool", Yall, Yall, bc["lnw"], ALU.mult)
            P.tt("pool", Yall, Yall, bc["lnb"], ALU.add)
            P.tt("pool", t3.re("p (h n) -> p h n", h=16), v32.re("p (h n) -> p h n", h=16),
                 sm[:, 32:48].bc(2, [128, 16, 64]), ALU.mult)
            P.tt("dve", Yall, Yall, t3, ALU.add)
            if self.debug:
                P.dma("sp", self.dr["dbg_yn"][ti * 128:(ti + 1) * 128, :], Yall)
            P.tt("dve", yb, Yall, g32, ALU.mult)
            self.transpose_bf(yb, yT3, 0, self.nb())
            xo = F[0]
            xr_ = F[7]
            P.dma("sp", xr_, self.res_src("R0", ti))
            for hf in range(2):
                cs = slice(hf * 512, (hf + 1) * 512)
                bk = self.nb()
                for k in range(8):
                    P.mm(bk, yT3[:, k, :], wo3[:, k, cs], start=(k == 0), stop=(k == 7))
                P.tt("dve", xo[:, cs], bk, xr_[:, cs], ALU.add)
            P.dma("sp", self.res_src("R1", ti), xo)
        Zt = F[1]; Zt3 = Zt.re("p (a n) -> p a n", a=8)
        for a in range(8):
            bk = self.nb()
            P.tr(bk[:, 0:128], S32p[a], ident)
            P.copy("act", Zt3[:, a, :], bk[:, 0:128])
        for hh in range(2):
            P.dma("sp", dr["p_wkv"].re("(a h) v k -> h v a k", h=2)[hh], Zt3[64 * hh:64 * hh + 64, :, 64 * hh:64 * hh + 64])
        for s in range(NS):
            Zi = F[3].re("p (a n) -> p a n", a=8)
            Zo2 = F[4 + s % 2].re("p (a n) -> p a n", a=8)
            P.dma("sp", Zi, ST_out[s * 1024:(s + 1) * 1024, :].re("(a p) n -> p a n", p=128))
            for a in range(8):
                bk = self.nb()
                P.tr(bk[:, 0:128], Zi[:, a, :], ident)
                P.copy("act", Zo2[:, a, :], bk[:, 0:128])
            for hh in range(2):
                P.dma("sp", dr["s_wkv"][s].re("(a h) v k -> h v a k", h=2)[hh], Zo2[64 * hh:64 * hh + 64, :, 64 * hh:64 * hh + 64])


    def phase_attn(self):
        P, A, NT, NST, NS, NTT, T = self.P, self.A, self.NT, self.NST, self.NS, self.NTT, self.T
        dr = self.dr
        identb = self.c16("ident")
        QTs = [[self.dscr("qT%d_%d" % (g, j), [8 * 128, NTT * 128], BF16) for j in range(2)] for g in range(3)]
        Vs = [self.dscr("vs%d" % g, [NTT * 128, H * HS], BF16) for g in range(3)]
        Os = [self.dscr("os%d" % g, [NTT * 128, H * 65], F32) for g in range(3)]
        qs = [self.dscr("qs%d" % g, [NST * 128, D], F32) for g in range(3)]
        kpad = [self.dscr("kpad%d" % g, [NST * 128, D], F32) for g in range(3)]
        vpad = [self.dscr("vpad%d" % g, [NST * 128, D], F32) for g in range(3)]
        A.mark()
        win = A.alloc(8 * 9 * D, BF16); win3 = win.re("p (k n) -> p k n", k=8)
        for k in range(8):
            P.dma("pool", win3[:, k, :], dr["attn_w_in"][k * 128:(k + 1) * 128, :])
        gbc = A.alloc(D, F32)
        self.load_bcast(gbc, dr["norm_mix"][1:2, :])
        rope = A.alloc(NT * 128, F32); rope3 = rope.re("p (t n) -> p t n", n=128)
        P.dma("sp", rope, dr["rope_p"])
        x = A.alloc(D, F32)
        xn = A.alloc(D, BF16)
        xnT = A.alloc(D, BF16); xnT3 = xnT.re("p (k t) -> p k t", k=8)
        small = A.alloc(4, F32)
        xq = A.alloc(D, F32); A32 = A.alloc(D, F32); B32 = A.alloc(D, F32)
        ob = [A.alloc(D, BF16) for _ in range(2)]
        vb = [A.alloc(H * HS, BF16) for _ in range(2)]
        for v_ in vb:
            P.memset("pool", v_, 1.0)
        QTt = [A.alloc(D, BF16) for _ in range(2)]
        cnt = 0
        for ti in range(NTT):
            is_s = ti >= NT
            stl = ti - NT
            P.dma("sp", x, self.res_src("R2", ti))
            self.rmsnorm(x, gbc, xn, A32, small)
            self.transpose_bf(xn, xnT3, 0, self.bank[6])
            if is_s:
                o_r = CF["rope_s"][0]
                cs2 = self.cf[:, o_r:o_r + 64]
                sn2 = self.cf[:, o_r + 64:o_r + 128]
            else:
                cs2 = rope3[:, ti, 0:64]
                sn2 = rope3[:, ti, 64:128]
            for g in range(3):
                for j in range(3):
                    col0 = g * 3 * D + j * D
                    bks = []
                    for hf in range(2):
                        bk = self.bank[(2 * cnt + hf) % 6]
                        bks.append(bk)
                        for k in range(8):
                            P.mm(bk, xnT3[:, k, :], win3[:, k, col0 + hf * 512:col0 + (hf + 1) * 512], start=(k == 0), stop=(k == 7))
                    cnt += 1
                    if j == 2:
                        for hf in range(2):
                            P.copy("act", xq[:, hf * 512:(hf + 1) * 512], bks[hf])
                        vbb = vb[cnt % 2]
                        P.copy("pool", vbb.re("p (h n) -> p h n", n=HS)[:, :, 0:64], xq.re("p (h n) -> p h n", n=64))
                        P.dma("sp", Vs[g][ti * 128:(ti + 1) * 128, :], vbb)
                        if is_s:
                            P.dma("sp", vpad[g][stl * 128:(stl + 1) * 128, :], xq)
                        else:
                            keep = min(WINDOWS[g], T) // 128
                            if ti >= NT - keep:
                                r0 = (ti - (NT - keep)) * 128
                                P.dma("sp", dr["p_v%d" % g][r0:r0 + 128, :], xq)
                        continue
                    for hf in range(2):
                        cs = slice(hf * 512, (hf + 1) * 512)
                        P.copy("act", xq[:, cs], bks[hf])
                        P.tt("dve", A32[:, cs].re("p (h n) -> p h n", n=64), xq[:, cs].re("p (h n) -> p h n", n=64),
                             cs2.bc(1, [128, 8, 64]), ALU.mult)
                    P.tt("dve", B32.re("p (h n) -> p h n", n=64), xq.re("p (h n) -> p h n", n=64), sn2.bc(1, [128, 16, 64]), ALU.mult)
                    A4 = A32.re("p (h b n) -> p h b n", b=2, n=32)
                    B4 = B32.re("p (h b n) -> p h b n", b=2, n=32)
                    P.tt("dve", A4[:, :, 0, :], A4[:, :, 0, :], B4[:, :, 1, :], ALU.subtract)
                    P.tt("pool", A4[:, :, 1, :], A4[:, :, 1, :], B4[:, :, 0, :], ALU.add)
                    if j == 1:
                        if is_s:
                            P.dma("sp", kpad[g][stl * 128:(stl + 1) * 128, :], A32)
                        else:
                            keep = min(WINDOWS[g], T) // 128
                            if ti >= NT - keep:
                                r0 = (ti - (NT - keep)) * 128
                                P.dma("sp", dr["p_k%d" % g][r0:r0 + 128, :], A32)
                    elif is_s:
                        P.dma("sp", qs[g][stl * 128:(stl + 1) * 128, :], A32)
                    obb = ob[cnt % 2]
                    P.copy("act", obb, A32)
                    qt = QTt[cnt % 2]
                    bk = self.bank[7]
                    pb = bk.bitcast(BF16)
                    for a in range(8):
                        P.tr(pb[:, a * 128:(a + 1) * 128], obb[:, a * 128:(a + 1) * 128], identb)
                    P.copy("act", qt, pb)
                    P.dma("sp", QTs[g][j][:, ti * 128:(ti + 1) * 128].re("(a p) t -> p a t", p=128), qt.re("p (a t) -> p a t", a=8))
        A.release()
        P.barrier()
        import os
        ASTOP = os.environ.get("ATT_STOP", "")
        if ASTOP == "s1":
            return
        for g in range(3):
            P.dma("sp", dr["s_k%d" % g].re("(s j) d -> s j d", j=4), kpad[g].re("(s q) d -> s q d", q=16)[:, 2:6, :])
            P.dma("sp", dr["s_v%d" % g].re("(s j) d -> s j d", j=4), vpad[g].re("(s q) d -> s q d", q=16)[:, 2:6, :])
        A.mark()
        NBLK = NT
        Vg = A.alloc(NBLK * H * HS, BF16); Vg4 = Vg.re("p (b h n) -> p b h n", b=NBLK, n=HS)
        Qn = [A.alloc(T, BF16) for _ in range(2)]
        Kn = [A.alloc(T, BF16) for _ in range(2)]
        Qc = [A.alloc(T, BF16) for _ in range(2)]
        Kc = [A.alloc(T, BF16) for _ in range(2)]
        Obuf = [A.alloc(NBLK * 130, F32) for _ in range(2)]
        pT = [A.alloc(512, BF16) for _ in range(3)]
        o_m = CB["am_ge"][0]
        am2 = self.cb[:, o_m:o_m + 256]
        it = 0
        for g in range(3):
            d = DILS[g]
            L = T // d
            nbc = L // 128
            assert nbc >= 1
            for c in range(d):
                for n in range(nbc):
                    blk = c * nbc + n
                    r0 = c + d * 128 * n
                    src = Vs[g][r0:r0 + d * 127 + 1, :]
                    if d > 1:
                        src = V(Vs[g].ap[r0:r0 + d * 128 - (d - 1):d, :], Vs[g].toks) if False else None
                    if d == 1:
                        P.dma("sp", Vg4[:, blk, :, :], Vs[g][r0:r0 + 128, :].re("p (h n) -> p h n", n=HS))
                    else:
                        rows = Vs[g][0:T, :].re("(l c) n -> c l n", c=d)[c, n * 128:(n + 1) * 128, :]
                        P.dma("sp", Vg4[:, blk, :, :], rows.re("p (h n) -> p h n", n=HS))
            if ASTOP == "s2a":
                continue
            for a in range(8):
                b = a % 2
                P.dma("sp", Qn[b], QTs[g][0][a * 128:(a + 1) * 128, 0:T])
                P.dma("sp", Kn[b], QTs[g][1][a * 128:(a + 1) * 128, 0:T])
                if d == 1:
                    qc3 = Qn[b].re("p (c l) -> p c l", c=1)
                    kc3 = Kn[b].re("p (c l) -> p c l", c=1)
                else:
                    qc3 = Qc[b].re("p (c l) -> p c l", c=d)
                    kc3 = Kc[b].re("p (c l) -> p c l", c=d)
                    P.copy("pool", qc3, Qn[b].re("p (l c) -> p c l", c=d))
                    P.copy("pool", kc3, Kn[b].re("p (l c) -> p c l", c=d))
                Ob3 = Obuf[b].re("p (b n) -> p b n", n=130)
                if ASTOP == "s2q":
                    continue
                for c in range(d):
                    for n in range(nbc):
                        blk = c * nbc + n
                        it += 1
                        pt = pT[it % 3]
                        for hh in range(2):
                            bk = self.bank[(2 * it + hh) % 4]
                            rows = slice(64 * hh, 64 * hh + 64)
                            if n > 0:
                                P.mm(bk[:, 0:128], kc3[rows, c, (n - 1) * 128:n * 128], qc3[rows, c, n * 128:(n + 1) * 128])
                            P.mm(bk[:, 128:256], kc3[rows, c, n * 128:(n + 1) * 128], qc3[rows, c, n * 128:(n + 1) * 128])
                            if n > 0:
                                P.act(pt[:, hh * 256:(hh + 1) * 256], bk[:, 0:256], AF.Exp, scale=0.125)
                            else:
                                P.act(pt[:, hh * 256 + 128:(hh + 1) * 256], bk[:, 128:256], AF.Exp, scale=0.125)
                        if n > 0:
                            P.tt("dve", pt.re("p (h m) -> p h m", h=2), pt.re("p (h m) -> p h m", h=2), am2.bc(1, [128, 2, 256]), ALU.mult)
                        else:
                            ptv = pt.re("p (h m) -> p h m", h=2)[:, :, 128:256]
                            P.tt("dve", ptv, ptv, am2[:, 128:256].bc(1, [128, 2, 128]), ALU.mult)
                        if ASTOP == "s2m":
                            continue
                        okb = self.bank[4 + it % 4]
                        for hh in range(2):
                            hd = 2 * a + hh
                            if n > 0:
                                P.mm(okb[:, hh * 65:hh * 65 + 65], pt[:, hh * 256:hh * 256 + 128], Vg4[:, blk - 1, hd, 0:65], start=True, stop=False)
                            P.mm(okb[:, hh * 65:hh * 65 + 65], pt[:, hh * 256 + 128:hh * 256 + 256], Vg4[:, blk, hd, 0:65], start=(n == 0), stop=True)
                        P.copy("act", Ob3[:, blk, :], okb[:, 0:130])
                if ASTOP in ("s2m", "s2p"):
                    continue
                dst = Os[g][0:T, a * 130:(a + 1) * 130].re("(n i c) m -> i c n m", i=128, c=d)
                srcv = Obuf[b].re("p (c n m) -> p c n m", c=d, m=130)
                for c in range(d):
                    for n0 in range(0, nbc, 8):
                        n1 = min(nbc, n0 + 8)
                        P.dma("sp", dst[:, c, n0:n1, :], srcv[:, c, n0:n1, :])
        A.release()
        P.barrier()
        if ASTOP in ("s2", "s2a", "s2q", "s2m", "s2p"):
            return
        A.mark()
        self.attn_sample(Os, qs, kpad, vpad)
        A.release()
        P.barrier()
        if ASTOP == "s2b":
            return
        A.mark()
        wo = A.alloc(8 * D, BF16); wo3 = wo.re("p (k n) -> p k n", k=8)
        P.dma("pool", wo3, dr["attn_w_o"].re("(k p) n -> p k n", p=128))
        O3 = [[A.alloc(H * 65, F32) for _ in range(3)] for _ in range(2)]
        xr = [A.alloc(D, F32) for _ in range(2)]
        obf = [A.alloc(D, BF16) for _ in range(2)]
        oT = [A.alloc(D, BF16) for _ in range(2)]
        rden = [A.alloc(16, F32) for _ in range(2)]
        for ti in range(NTT):
            b = ti % 2
            for g in range(3):
                P.dma("sp", O3[b][g], Os[g][ti * 128:(ti + 1) * 128, :])
            P.dma("sp", xr[b], self.res_src("R2", ti))
            P.tt("dve", O3[b][0], O3[b][0], O3[b][1], ALU.add)
            P.tt("pool", O3[b][0], O3[b][0], O3[b][2], ALU.add)
            o3 = O3[b][0].re("p (h n) -> p h n", n=65)
            P.add("dve", (lambda o_, i_: (lambda e: e.reciprocal(out=o_, in_=i_)))(rden[b].ap.rearrange("p (h o) -> p h o", o=1), o3.ap[:, :, 64:65]),
                  rd=o3.toks, wr=rden[b].toks)
            P.tt("dve", obf[b].re("p (h n) -> p h n", n=64), o3[:, :, 0:64], rden[b].bc(2, [128, 16, 64]), ALU.mult)
            oT3 = oT[b].re("p (k t) -> p k t", k=8)
            self.transpose_bf(obf[b], oT3, 0, self.bank[6])
            for hf in range(2):
                cs = slice(hf * 512, (hf + 1) * 512)
                bk = self.bank[hf + 2 * (ti % 2)]
                for k in range(8):
                    P.mm(bk, oT3[:, k, :], wo3[:, k, cs], start=(k == 0), stop=(k == 7))
                P.tt("dve", xr[b][:, cs], bk, xr[b][:, cs], ALU.add)
            P.dma("sp", self.res_src("R3", ti), xr[b])
        A.release()

    def attn_sample(self, Os, qs, kpad, vpad):
        P, A, NS, NT = self.P, self.A, self.NS, self.NT
        dr = self.dr
        ones = A.alloc(H * 65, F32)
        P.memset("pool", ones, 1.0)
        for g in range(3):
            for stl in range(self.NST):
                r0 = (NT + stl) * 128
                P.dma("sp", Os[g][r0:r0 + 128, :], ones)
        Kt = [A.alloc(D, F32) for _ in range(2)]
        Vt = [A.alloc(D, F32) for _ in range(2)]
        Vb = [A.alloc(H * HS, BF16) for _ in range(2)]
        qbc = [A.alloc(D, F32) for _ in range(2)]
        prod = [A.alloc(D, F32) for _ in range(2)]
        sc = [A.alloc(16, F32) for _ in range(2)]
        pb = [A.alloc(16, BF16) for _ in range(2)]
        Kn = [A.alloc(D, F32) for _ in range(2)]
        Vn = [A.alloc(D, F32) for _ in range(2)]
        Vnb = [A.alloc(H * HS, BF16) for _ in range(2)]
        prodn = A.alloc(D, F32)
        scn = A.alloc(16, F32)
        pnb = [A.alloc(16, BF16) for _ in range(2)]
        orow = [A.alloc(H * 65, F32) for _ in range(2)]
        for v_ in Vb + Vnb:
            P.memset("pool", v_, 1.0)
        o_ge, o_le, o_eq = CF["ge_j"][0], CF["le_j"][0], CF["eq_j"][0]
        HCH = ((0, 7), (7, 14), (14, 16))
        u = 0
        kv = 0
        for s in range(NS):
            for g in range(3):
                d = DILS[g]
                W = WINDOWS[g]
                nb_ = (s * 3 + g) % 2
                rown = NT * 128 + 16 * s + 2
                P.dma("sp", Kn[nb_][0:4, :], kpad[g][16 * s + 2:16 * s + 6, :])
                P.dma("pool", Vn[nb_][0:4, :], vpad[g][16 * s + 2:16 * s + 6, :])
                P.copy("pool", Vnb[nb_][0:4, :].re("p (h n) -> p h n", n=HS)[:, :, 0:64], Vn[nb_][0:4, :].re("p (h n) -> p h n", n=64))
                for j in range(4):
                    if g > 0 or j == 0:
                        kb = kv % 2
                        kv += 1
                        if g == 0:
                            ksrc = dr["ck0"][s]
                            vsrc = dr["cv0"][s]
                        else:
                            ksrc = dr["ck%d" % g][s].re("(l c) n -> c l n", c=d)[j]
                            vsrc = dr["cv%d" % g][s].re("(l c) n -> c l n", c=d)[j]
                        P.dma("sp", Kt[kb], ksrc)
                        P.dma("pool", Vt[kb], vsrc)
                        P.copy("act", Vb[kb].re("p (h n) -> p h n", n=HS)[:, :, 0:64], Vt[kb].re("p (h n) -> p h n", n=64))
                    ub = u % 2
                    u += 1
                    qrow = qs[g][16 * s + 2 + j:16 * s + 3 + j, :]
                    P.dma("pool", qbc[ub], V(qrow.ap.to_broadcast([128, D]), qrow.toks))
                    P.tt("dve", prod[ub], Kt[kb], qbc[ub], ALU.mult)
                    P.red("dve", sc[ub], prod[ub].re("p (h n) -> p h n", h=16))
                    P.act(pb[ub], sc[ub], AF.Exp, scale=0.125)
                    if g == 0 and j > 0:
                        P.ts("dve", pb[ub], pb[ub], self.cf[:, o_ge + j:o_ge + j + 1], ALU.mult)
                    P.tt("pool", prodn[0:4, :], Kn[nb_][0:4, :], qbc[ub][0:4, :], ALU.mult)
                    P.red("dve", scn[0:4, :], prodn[0:4, :].re("p (h n) -> p h n", h=16))
                    P.act(pnb[ub][0:4, :], scn[0:4, :], AF.Exp, scale=0.125)
                    om = (o_le if g == 0 else o_eq) + j
                    P.ts("dve", pnb[ub][0:4, :], pnb[ub][0:4, :], self.cf[0:4, om:om + 1], ALU.mult)
                    Vb3 = Vb[kb].re("p (h n) -> p h n", n=HS)
                    Vn3 = Vnb[nb_].re("p (h n) -> p h n", n=HS)
                    for ci, (h0, h1) in enumerate(HCH):
                        bk = self.bank[(3 * u + ci) % 8]
                        for h in range(h0, h1):
                            oc = slice((h - h0) * 65, (h - h0 + 1) * 65)
                            P.mm(bk[0:1, oc], pb[ub][:, h:h + 1], Vb3[:, h, 0:65], start=True, stop=False)
                            P.mm(bk[0:1, oc], pnb[ub][0:4, h:h + 1], Vn3[0:4, h, 0:65], start=False, stop=True)
                        P.copy("act", orow[ub][0:1, h0 * 65:h1 * 65], bk[0:1, 0:(h1 - h0) * 65])
                    P.dma("sp", Os[g][rown + j:rown + j + 1, :], orow[ub][0:1, :])


def core_inputs(inp, b, s0, NS, T, consts):
    cf, cb, rope_p = consts
    f = lambda a: np.ascontiguousarray(a, dtype=np.float32)
    m = {}
    m["xp"] = f(inp["x_prompt"][b])
    m["xs"] = f(inp["x_sample"][s0:s0 + NS]).reshape(NS * 4, D)
    m["st_shift"] = f(inp["state_rwkv_shift"][0, s0:s0 + NS])
    m["st_wkv"] = f(inp["state_rwkv_wkv"][0, s0:s0 + NS])
    caches = ((inp["cache_k_w128"], inp["cache_v_w128"]), (inp["cache_k_w512"], inp["cache_v_w512"]),
              (inp["cache_k_w2048"], inp["cache_v_w2048"]))
    for g, (ck, cv) in enumerate(caches):
        m["ck%d" % g] = f(ck[0, s0:s0 + NS]).reshape(NS, -1, D)
        m["cv%d" % g] = f(cv[0, s0:s0 + NS]).reshape(NS, -1, D)
    m["st_conv"] = f(inp["state_ffn_conv"][:, s0:s0 + NS])
    m["norm_mix"] = f(inp["norm_mix"]); m["norm_ffn"] = f(inp["norm_ffn"]); m["norm_final"] = f(inp["norm_final"]).reshape(1, D)
    m["rwkv_mu"] = f(inp["rwkv_mu"][0]); m["rwkv_w_rkv"] = f(inp["rwkv_w_rkv"][0])
    for k in ("w0", "a0", "k_k", "k_a", "ln_w", "ln_b"):
        m["rwkv_" + k] = f(inp["rwkv_" + k][0]).reshape(1, D)
    m["rwkv_r_k"] = f(inp["rwkv_r_k"][0]).reshape(1, D)
    for k in ("w1", "w2", "a1", "a2", "g1", "g2", "w_o"):
        m["rwkv_" + k] = f(inp["rwkv_" + k][0])
    m["attn_w_in"] = f(inp["attn_w_in"][0]); m["attn_w_o"] = f(inp["attn_w_o"][0])
    m["ffn_w_up"] = f(inp["ffn_w_up"]); m["ffn_conv_w"] = f(inp["ffn_conv_w"]); m["ffn_conv_b"] = f(inp["ffn_conv_b"])
    m["ffn_w_down"] = f(inp["ffn_w_down"])
    m["cf"] = cf; m["cb"] = cb; m["rope_p"] = rope_p
    return m


_CACHE = {}


def kernel(**inputs):
    inp = {k: np.asarray(v) for k, v in inputs.items()}
    T = inp["x_prompt"].shape[1]
    NB = inp["x_prompt"].shape[0]
    NSEQ = inp["x_sample"].shape[0]
    NS = NSEQ // NCORES
    key = (T, NS)
    if key not in _CACHE:
        _CACHE[key] = (Builder(T, NS).build(), host_consts(T))
    nc, consts = _CACHE[key]
    in_maps = [core_inputs(inp, c % NB, c * NS, NS, T, consts) for c in range(NCORES)]
    res = run_bass_kernel_spmd(nc, in_maps, core_ids=list(range(NCORES)))
    R = res.results
    f = np.float32
    pr = lambda name, shape: np.stack([np.asarray(R[b][name], f).reshape(shape) for b in range(NB)])
    sm = lambda name, shape: np.concatenate([np.asarray(R[c][name], f).reshape(shape) for c in range(NCORES)], 0)
    outs = [pr("y_p", (T, D)), sm("y_s", (NS, 4, D)),
            pr("p_shift", (D,))[None], pr("p_wkv", (H, DH, DH))[None]]
    for g, w in enumerate(WINDOWS):
        kp = min(w, T)
        outs.append(pr("p_k%d" % g, (kp, H, DH))[None])
        outs.append(pr("p_v%d" % g, (kp, H, DH))[None])
    outs.append(np.stack([np.asarray(R[b]["p_conv"], f).reshape(2, 2, FF) for b in range(NB)], 1))
    outs.append(sm("s_shift", (NS, D))[None])
    outs.append(sm("s_wkv", (NS, H, DH, DH))[None])
    for g in range(3):
        outs.append(sm("s_k%d" % g, (NS, 4, H, DH))[None])
        outs.append(sm("s_v%d" % g, (NS, 4, H, DH))[None])
    outs.append(np.concatenate([np.asarray(R[c]["s_conv"], f).reshape(2, NS, 2, FF) for c in range(NCORES)], 1))
    return tuple(outs)
```
